# Optimizing a Trainium2 kernel written in Bass

```python
import math
import jax, jax.numpy as jnp
from jax import lax
import numpy as np

D_MODEL = 2048
BATCH = 4
SEQ = 8192
DEPTH = 4

GRID_W = 64
CTX_LEN = 256
N_MIXERS = 4
EPS = 1e-6
ROPE_BASE = 10000.0
Q_BLOCK = 128
NEG_INF = -1e30

MLA_HEADS = 16
MLA_NOPE = 128
MLA_ROPE = 64
MLA_V = 128
MLA_Q_RANK = 512
MLA_KV_RANK = 512
MLA_WIDTH = MLA_HEADS * MLA_V
MLA_IN = MLA_Q_RANK + MLA_KV_RANK + MLA_ROPE + MLA_WIDTH

SWA_HEADS = 32
SWA_KV_HEADS = 4
SWA_HD = 64
SWA_WINDOW = 128
SWA_Q_DIM = SWA_HEADS * SWA_HD
SWA_KV_DIM = SWA_KV_HEADS * SWA_HD
SWA_WIDTH = SWA_Q_DIM
SWA_IN = SWA_Q_DIM + 2 * SWA_KV_DIM + SWA_WIDTH

NA_HEADS = 32
NA_HD = 64
NA_WIN_R = 8
NA_WIN_C = 16
NA_WIDTH = NA_HEADS * NA_HD
NA_IN = 4 * NA_WIDTH

DIFF_HEADS = 8
DIFF_HD = 128
DIFF_VD = 2 * DIFF_HD
DIFF_QK_DIM = DIFF_HEADS * 2 * DIFF_HD
DIFF_WIDTH = DIFF_HEADS * DIFF_VD
DIFF_IN = 2 * DIFF_QK_DIM + 2 * DIFF_WIDTH

kernel_name = "hybrid_interleaved_dit_backbone"

f32 = jnp.float32


def rmsnorm(x, g):
    xf = x.astype(f32)
    y = xf * lax.rsqrt(jnp.mean(xf * xf, axis=-1, keepdims=True) + EPS)
    return (y * g.astype(f32)).astype(x.dtype)


def axial_rope_tables(n, d_rot):
    t = jnp.arange(n)
    row = (t // GRID_W).astype(f32)
    col = (t % GRID_W).astype(f32)
    d_axis = d_rot // 2
    inv = ROPE_BASE ** (-jnp.arange(0, d_axis, 2, dtype=f32) / d_axis)
    ang = jnp.concatenate([row[:, None] * inv, col[:, None] * inv], axis=-1)
    return jnp.cos(ang), jnp.sin(ang)


def apply_rope(x, cos, sin):
    x1, x2 = jnp.split(x, 2, axis=-1)
    c = cos[None, :, None, :]
    s = sin[None, :, None, :]
    return jnp.concatenate([x1 * c - x2 * s, x1 * s + x2 * c], axis=-1).astype(x.dtype)


def add_all(xs):
    out = xs[0]
    for t in xs[1:]:
        out = out + t
    return out


def gated_out(o, gate, w_out):
    b, n = o.shape[:2]
    return (o.reshape(b, n, -1) * jax.nn.silu(gate)) @ w_out


def map_query_blocks(fn, *qs):
    b, n = qs[0].shape[:2]
    nb = n // Q_BLOCK
    blk = tuple(jnp.moveaxis(q.reshape(b, nb, Q_BLOCK, *q.shape[2:]), 1, 0) for q in qs)
    out = lax.map(lambda a: fn(*a), blk)
    out = jnp.moveaxis(out, 0, 1)
    return out.reshape(b, n, *out.shape[3:])


def mla_mixer(h_lat, h_ctx, w_in, q_norm, w_qb, kv_norm, w_kvb, w_out, cos, sin, need_ctx):
    splits = [MLA_Q_RANK, MLA_Q_RANK + MLA_KV_RANK, MLA_Q_RANK + MLA_KV_RANK + MLA_ROPE]
    scale = (MLA_NOPE + MLA_ROPE) ** -0.5

    def project(h, rotary):
        b, n, _ = h.shape
        q_c, kv_c, k_r, gate = jnp.split(h @ w_in, splits, axis=-1)
        q = (rmsnorm(q_c, q_norm) @ w_qb).reshape(b, n, MLA_HEADS, MLA_NOPE + MLA_ROPE)
        kv = (rmsnorm(kv_c, kv_norm) @ w_kvb).reshape(b, n, MLA_HEADS, MLA_NOPE + MLA_V)
        q_n, q_r = q[..., :MLA_NOPE], q[..., MLA_NOPE:]
        k_n, v = kv[..., :MLA_NOPE], kv[..., MLA_NOPE:]
        k_r = k_r[:, :, None, :]
        if rotary:
            q_r = apply_rope(q_r, cos, sin)
            k_r = apply_rope(k_r, cos, sin)
        return q_n, q_r, k_n, k_r[:, :, 0, :], v, gate

    qn_l, qr_l, kn_l, kr_l, v_l, g_l = project(h_lat, True)
    qn_c, qr_c, kn_c, kr_c, v_c, g_c = project(h_ctx, False)

    def logits(qn, qr, kn, kr):
        return (jnp.einsum('bqhd,bkhd->bhqk', qn, kn, preferred_element_type=f32)
                + jnp.einsum('bqhr,bkr->bhqk', qr, kr, preferred_element_type=f32)) * scale

    def attend(qn, qr, kvs):
        p = jax.nn.softmax(jnp.concatenate([logits(qn, qr, kn, kr) for kn, kr, _ in kvs], axis=-1), axis=-1)
        outs, off = [], 0
        for kn, _, vv in kvs:
            n_k = kn.shape[1]
            outs.append(jnp.einsum('bhqk,bkhd->bqhd', p[..., off:off + n_k].astype(vv.dtype), vv))
            off += n_k
        return add_all(outs)

    ctx_kv = [(kn_c, kr_c, v_c)]
    lat_kv = ctx_kv + [(kn_l, kr_l, v_l)]
    o_l = map_query_blocks(lambda qn, qr: attend(qn, qr, lat_kv), qn_l, qr_l)
    y_l = gated_out(o_l, g_l, w_out)
    y_c = gated_out(attend(qn_c, qr_c, ctx_kv), g_c, w_out) if need_ctx else None
    return y_l, y_c


def swa_mixer(h_lat, h_ctx, w_in, sink, w_out, cos, sin, need_ctx):
    groups = SWA_HEADS // SWA_KV_HEADS
    scale = SWA_HD ** -0.5
    sink_col = sink.astype(f32).reshape(1, SWA_KV_HEADS, groups, 1, 1)

    def project(h, rotary):
        b, n, _ = h.shape
        q, k, v, gate = jnp.split(h @ w_in, [SWA_Q_DIM, SWA_Q_DIM + SWA_KV_DIM, SWA_Q_DIM + 2 * SWA_KV_DIM], axis=-1)
        q = q.reshape(b, n, SWA_HEADS, SWA_HD)
        k = k.reshape(b, n, SWA_KV_HEADS, SWA_HD)
        v = v.reshape(b, n, SWA_KV_HEADS, SWA_HD)
        if rotary:
            q = apply_rope(q, cos, sin)
            k = apply_rope(k, cos, sin)
        return q.reshape(b, n, SWA_KV_HEADS, groups, SWA_HD), k, v, gate

    def attend(qb, keys, vals, mask):
        s_list = [jnp.einsum('bqhgd,bshd->bhgqs', qb, kk, preferred_element_type=f32) * scale for kk in keys]
        if mask is not None:
            s_list[-1] = jnp.where(mask, s_list[-1], NEG_INF)
        sk = jnp.broadcast_to(sink_col, s_list[0].shape[:-1] + (1,))
        p = jax.nn.softmax(jnp.concatenate(s_list + [sk], axis=-1), axis=-1)
        outs, off = [], 0
        for kk, vv in zip(keys, vals):
            n_k = kk.shape[1]
            outs.append(jnp.einsum('bhgqs,bshd->bqhgd', p[..., off:off + n_k].astype(vv.dtype), vv))
            off += n_k
        o = add_all(outs)
        return o.reshape(o.shape[0], o.shape[1], SWA_HEADS, SWA_HD)

    q_l, k_l, v_l, g_l = project(h_lat, True)
    q_c, k_c, v_c, g_c = project(h_ctx, False)
    b, n = h_lat.shape[:2]
    nb = n // Q_BLOCK

    def band(t):
        tp = jnp.pad(t, ((0, 0), (Q_BLOCK, Q_BLOCK), (0, 0), (0, 0)))
        tb = tp.reshape(b, nb + 2, Q_BLOCK, *t.shape[2:])
        bnd = jnp.concatenate([tb[:, :-2], tb[:, 1:-1], tb[:, 2:]], axis=2)
        return jnp.moveaxis(bnd, 1, 0)

    q_blk = jnp.moveaxis(q_l.reshape(b, nb, Q_BLOCK, SWA_KV_HEADS, groups, SWA_HD), 1, 0)
    qpos_in = jnp.arange(Q_BLOCK)[:, None]
    kpos_in = jnp.arange(3 * Q_BLOCK)[None, :]

    def block(args):
        qb, kb, vb, i = args
        qpos = i * Q_BLOCK + qpos_in
        kpos = i * Q_BLOCK - Q_BLOCK + kpos_in
        mask = (jnp.abs(qpos - kpos) <= SWA_WINDOW) & (kpos >= 0) & (kpos < n)
        return attend(qb, [k_c, kb], [v_c, vb], mask)

    o_l = lax.map(block, (q_blk, band(k_l), band(v_l), jnp.arange(nb)))
    o_l = jnp.moveaxis(o_l, 0, 1).reshape(b, n, SWA_HEADS, SWA_HD)
    y_l = gated_out(o_l, g_l, w_out)
    y_c = gated_out(attend(q_c, [k_c], [v_c], None), g_c, w_out) if need_ctx else None
    return y_l, y_c


def na_mixer(h_lat, h_ctx, w_in, rpb, w_out, need_ctx):
    scale = NA_HD ** -0.5

    def project(h):
        b, n, _ = h.shape
        q, k, v, gate = jnp.split(h @ w_in, [NA_WIDTH, 2 * NA_WIDTH, 3 * NA_WIDTH], axis=-1)
        shp = (b, n, NA_HEADS, NA_HD)
        return q.reshape(shp), k.reshape(shp), v.reshape(shp), gate

    q_l, k_l, v_l, g_l = project(h_lat)
    q_c, k_c, v_c, g_c = project(h_ctx)
    b, n = h_lat.shape[:2]
    rows = n // GRID_W
    win_r = min(NA_WIN_R, rows)
    n_ctx = k_c.shape[1]

    k_grid = k_l.reshape(b, rows, GRID_W, NA_HEADS, NA_HD)
    v_grid = v_l.reshape(b, rows, GRID_W, NA_HEADS, NA_HD)
    q_rows = jnp.moveaxis(q_l.reshape(b, rows, GRID_W, NA_HEADS, NA_HD), 1, 0)

    cq = jnp.arange(GRID_W)
    cs = jnp.clip(cq - NA_WIN_C // 2, 0, GRID_W - NA_WIN_C)
    ck = jnp.arange(GRID_W)
    col_ok = (ck[None, :] >= cs[:, None]) & (ck[None, :] < cs[:, None] + NA_WIN_C)
    mask = jnp.broadcast_to(col_ok[:, None, :], (GRID_W, win_r, GRID_W)).reshape(GRID_W, win_r * GRID_W)
    dc_idx = jnp.clip(ck[None, :] - cq[:, None], -(NA_WIN_C - 1), NA_WIN_C - 1) + (NA_WIN_C - 1)

    def row_block(args):
        qr, r = args
        rs = jnp.clip(r - win_r // 2, 0, rows - win_r)
        kr = lax.dynamic_slice_in_dim(k_grid, rs, win_r, axis=1)
        vr = lax.dynamic_slice_in_dim(v_grid, rs, win_r, axis=1)
        dr_idx = rs + jnp.arange(win_r) - r + (NA_WIN_R - 1)
        bias = rpb[:, dr_idx[:, None, None], dc_idx[None, :, :]]
        bias = jnp.transpose(bias, (0, 2, 1, 3)).reshape(NA_HEADS, GRID_W, win_r * GRID_W)
        s_lat = jnp.einsum('bqhd,brkhd->bhqrk', qr, kr, preferred_element_type=f32)
        s_lat = s_lat.reshape(b, NA_HEADS, GRID_W, win_r * GRID_W) * scale + bias.astype(f32)
        s_lat = jnp.where(mask, s_lat, NEG_INF)
        s_ctx = jnp.einsum('bqhd,bshd->bhqs', qr, k_c, preferred_element_type=f32) * scale
        p = jax.nn.softmax(jnp.concatenate([s_ctx, s_lat], axis=-1), axis=-1)
        p_lat = p[..., n_ctx:].reshape(b, NA_HEADS, GRID_W, win_r, GRID_W)
        return (jnp.einsum('bhqs,bshd->bqhd', p[..., :n_ctx].astype(v_c.dtype), v_c)
                + jnp.einsum('bhqrk,brkhd->bqhd', p_lat.astype(vr.dtype), vr))

    o_l = lax.map(row_block, (q_rows, jnp.arange(rows)))
    o_l = jnp.moveaxis(o_l, 0, 1).reshape(b, n, NA_HEADS, NA_HD)
    y_l = gated_out(o_l, g_l, w_out)
    y_c = None
    if need_ctx:
        s = jnp.einsum('bqhd,bshd->bhqs', q_c, k_c, preferred_element_type=f32) * scale
        p = jax.nn.softmax(s, axis=-1)
        o_c = jnp.einsum('bhqs,bshd->bqhd', p.astype(v_c.dtype), v_c)
        y_c = gated_out(o_c, g_c, w_out)
    return y_l, y_c


def diff_mixer(h_lat, h_ctx, w_in, lam_q1, lam_k1, lam_q2, lam_k2, subln, w_out, cos, sin, lam_init, need_ctx):
    scale = DIFF_HD ** -0.5
    lam = (jnp.exp(jnp.sum(lam_q1.astype(f32) * lam_k1.astype(f32)))
           - jnp.exp(jnp.sum(lam_q2.astype(f32) * lam_k2.astype(f32))) + lam_init)

    def project(h, rotary):
        b, n, _ = h.shape
        q, k, v, gate = jnp.split(h @ w_in, [DIFF_QK_DIM, 2 * DIFF_QK_DIM, 2 * DIFF_QK_DIM + DIFF_WIDTH], axis=-1)
        q = q.reshape(b, n, 2 * DIFF_HEADS, DIFF_HD)
        k = k.reshape(b, n, 2 * DIFF_HEADS, DIFF_HD)
        if rotary:
            q = apply_rope(q, cos, sin)
            k = apply_rope(k, cos, sin)
        q = q.reshape(b, n, DIFF_HEADS, 2, DIFF_HD)
        k = k.reshape(b, n, DIFF_HEADS, 2, DIFF_HD)
        return q, k, v.reshape(b, n, DIFF_HEADS, DIFF_VD), gate

    def attend(qb, keys, vals):
        s = jnp.concatenate([jnp.einsum('bqhid,bshid->bhiqs', qb, kk, preferred_element_type=f32) for kk in keys],
                            axis=-1) * scale
        p = jax.nn.softmax(s, axis=-1)
        a = p[:, :, 0] - lam * p[:, :, 1]
        outs, off = [], 0
        for kk, vv in zip(keys, vals):
            n_k = kk.shape[1]
            outs.append(jnp.einsum('bhqs,bshd->bqhd', a[..., off:off + n_k].astype(vv.dtype), vv))
            off += n_k
        o = add_all(outs)
        return rmsnorm(o, subln) * (1.0 - lam_init)

    q_l, k_l, v_l, g_l = project(h_lat, True)
    q_c, k_c, v_c, g_c = project(h_ctx, False)
    o_l = map_query_blocks(lambda qb: attend(qb, [k_c, k_l], [v_c, v_l]), q_l)
    y_l = gated_out(o_l, g_l, w_out)
    y_c = gated_out(attend(q_c, [k_c], [v_c]), g_c, w_out) if need_ctx else None
    return y_l, y_c


def setup_inputs(seed: int = 0) -> dict:
    key = jax.random.key(seed)
    ks = list(jax.random.split(key, 48))
    D = D_MODEL

    def nrm(shape, s):
        return jax.random.normal(ks.pop(), shape, f32) * s

    def gain(n):
        return 1.0 + nrm((n,), 0.02)

    inp = {}
    inp["x"] = nrm((BATCH, SEQ, D), 1.0)
    inp["c"] = nrm((BATCH, D), 1.0)
    inp["ctx"] = nrm((BATCH, CTX_LEN, D), 1.0)
    inp["c_ctx"] = nrm((D,), 1.0)
    inp["l0_mod_w"] = nrm((D, 3 * D), D ** -0.5)
    inp["l0_mod_b"] = nrm((3 * D,), 0.02)
    inp["l0_norm"] = gain(D)
    inp["l0_w_in"] = nrm((D, MLA_IN), D ** -0.5)
    inp["l0_q_norm"] = gain(MLA_Q_RANK)
    inp["l0_w_qb"] = nrm((MLA_Q_RANK, MLA_HEADS * (MLA_NOPE + MLA_ROPE)), MLA_Q_RANK ** -0.5)
    inp["l0_kv_norm"] = gain(MLA_KV_RANK)
    inp["l0_w_kvb"] = nrm((MLA_KV_RANK, MLA_HEADS * (MLA_NOPE + MLA_V)), MLA_KV_RANK ** -0.5)
    inp["l0_w_out"] = nrm((MLA_WIDTH, D), MLA_WIDTH ** -0.5)
    inp["l1_mod_w"] = nrm((D, 3 * D), D ** -0.5)
    inp["l1_mod_b"] = nrm((3 * D,), 0.02)
    inp["l1_norm"] = gain(D)
    inp["l1_w_in"] = nrm((D, SWA_IN), D ** -0.5)
    inp["l1_sink"] = nrm((SWA_HEADS,), 0.5)
    inp["l1_w_out"] = nrm((SWA_WIDTH, D), SWA_WIDTH ** -0.5)
    inp["l2_mod_w"] = nrm((D, 3 * D), D ** -0.5)
    inp["l2_mod_b"] = nrm((3 * D,), 0.02)
    inp["l2_norm"] = gain(D)
    inp["l2_w_in"] = nrm((D, NA_IN), D ** -0.5)
    inp["l2_rpb"] = nrm((NA_HEADS, 2 * NA_WIN_R - 1, 2 * NA_WIN_C - 1), 0.1)
    inp["l2_w_out"] = nrm((NA_WIDTH, D), NA_WIDTH ** -0.5)
    inp["l3_mod_w"] = nrm((D, 3 * D), D ** -0.5)
    inp["l3_mod_b"] = nrm((3 * D,), 0.02)
    inp["l3_norm"] = gain(D)
    inp["l3_w_in"] = nrm((D, DIFF_IN), D ** -0.5)
    inp["l3_lam_q1"] = nrm((DIFF_HD,), 0.1)
    inp["l3_lam_k1"] = nrm((DIFF_HD,), 0.1)
    inp["l3_lam_q2"] = nrm((DIFF_HD,), 0.1)
    inp["l3_lam_k2"] = nrm((DIFF_HD,), 0.1)
    inp["l3_subln"] = gain(DIFF_VD)
    inp["l3_w_out"] = nrm((DIFF_WIDTH, D), DIFF_WIDTH ** -0.5)
    inp["final_norm"] = gain(D)
    return inp


def reference(x, c, ctx, c_ctx,
              l0_mod_w, l0_mod_b, l0_norm, l0_w_in, l0_q_norm, l0_w_qb, l0_kv_norm, l0_w_kvb, l0_w_out,
              l1_mod_w, l1_mod_b, l1_norm, l1_w_in, l1_sink, l1_w_out,
              l2_mod_w, l2_mod_b, l2_norm, l2_w_in, l2_rpb, l2_w_out,
              l3_mod_w, l3_mod_b, l3_norm, l3_w_in, l3_lam_q1, l3_lam_k1, l3_lam_q2, l3_lam_k2, l3_subln, l3_w_out,
              final_norm):
    n = x.shape[1]
    cos64, sin64 = axial_rope_tables(n, MLA_ROPE)
    cos128, sin128 = axial_rope_tables(n, DIFF_HD)

    layer_mods = [(l0_mod_w, l0_mod_b, l0_norm), (l1_mod_w, l1_mod_b, l1_norm),
                  (l2_mod_w, l2_mod_b, l2_norm), (l3_mod_w, l3_mod_b, l3_norm)]
    mixer_params = [(l0_w_in, l0_q_norm, l0_w_qb, l0_kv_norm, l0_w_kvb, l0_w_out),
                    (l1_w_in, l1_sink, l1_w_out),
                    (l2_w_in, l2_rpb, l2_w_out),
                    (l3_w_in, l3_lam_q1, l3_lam_k1, l3_lam_q2, l3_lam_k2, l3_subln, l3_w_out)]

    for i in range(DEPTH):
        kind = i % N_MIXERS
        need_ctx = i < DEPTH - 1
        mod_w, mod_b, norm_g = layer_mods[i]
        mod_l = jax.nn.silu(c) @ mod_w + mod_b
        mod_c = jax.nn.silu(c_ctx) @ mod_w + mod_b
        shift_l, scale_l, gate_l = jnp.split(mod_l, 3, axis=-1)
        shift_c, scale_c, gate_c = jnp.split(mod_c, 3, axis=-1)
        h_l = rmsnorm(x, norm_g) * (1.0 + scale_l[:, None, :]) + shift_l[:, None, :]
        h_c = rmsnorm(ctx, norm_g) * (1.0 + scale_c) + shift_c
        p = mixer_params[i]
        if kind == 0:
            y_l, y_c = mla_mixer(h_l, h_c, *p, cos64, sin64, need_ctx)
        elif kind == 1:
            y_l, y_c = swa_mixer(h_l, h_c, *p, cos64, sin64, need_ctx)
        elif kind == 2:
            y_l, y_c = na_mixer(h_l, h_c, *p, need_ctx)
        else:
            lam_init = 0.8 - 0.6 * math.exp(-0.3 * i)
            y_l, y_c = diff_mixer(h_l, h_c, *p, cos128, sin128, lam_init, need_ctx)
        x = x + gate_l[:, None, :] * y_l
        if need_ctx:
            ctx = ctx + gate_c * y_c

    return rmsnorm(x, final_norm)
```

```python
import contextlib
import math
import numpy as np
import ml_dtypes
import concourse.bass as bass
import concourse.mybir as mybir
from concourse.bass_utils import run_bass_kernel_spmd

F32 = mybir.dt.float32
BF16 = mybir.dt.bfloat16
AF = mybir.ActivationFunctionType
ALU = mybir.AluOpType

D = 2048
KC = 16
CT = 256
GRID_W = 64
EPS = 1e-6
NEG = -30000.0


class Buf:
    __slots__ = ("w", "r", "name")

    def __init__(self, name=""):
        self.w = None
        self.r = {}
        self.name = name


class FW:
    def __init__(self, nc, es, n_dma=24):
        self.nc = nc
        self.engs = dict(pe=nc.tensor, act=nc.scalar, dve=nc.vector, pool=nc.gpsimd, sp=nc.sync)
        self.sem = {k: es.enter_context(nc.semaphore("s_" + k)) for k in self.engs}
        self.cnt = {k: 0 for k in self.engs}
        self.seen = {k: {} for k in self.engs}
        self.dsem = [es.enter_context(nc.semaphore(f"d{i}")) for i in range(n_dma)]
        self.dtot = [0] * n_dma
        self.drr = 0
        self.csem = es.enter_context(nc.semaphore("ccs"))
        self.ccnt = 0
        self.pend = {k: [] for k in self.engs}
        self.uid = 0
        self.groups = [[0, 1], [2, 3], [4, 5], [6, 7]]

    def _semof(self, ev):
        kind, key, _ = ev
        if kind == "e":
            return self.sem[key]
        if kind == "d":
            return self.dsem[key]
        return self.csem

    def _wait(self, ek, ev):
        if ev is None:
            return
        kind, key, val = ev
        if kind == "p":
            assert key == ek, f"wait on pending event of {key} from {ek}"
            return
        if kind == "e" and key == ek:
            if ek == "pe" or self.cnt[ek] - val >= 3:
                return
        if self.seen[ek].get((kind, key), 0) >= val:
            return
        self.engs[ek].wait_ge(self._semof(ev), val)
        self.seen[ek][(kind, key)] = val

    def _deps(self, ek, reads, writes):
        for b in reads:
            self._wait(ek, b.w)
        for b in writes:
            self._wait(ek, b.w)
            for ev in list(b.r.values()):
                self._wait(ek, ev)

    def op(self, ek, fn, reads=(), writes=(), inc=True):
        self._deps(ek, reads, writes)
        ins = fn(self.engs[ek])
        if not inc:
            self.pend[ek].append((tuple(reads), tuple(writes)))
            pe = ("p", ek, 0)
            for b in reads:
                b.r[ek] = pe
            for b in writes:
                b.w = pe
                b.r = {}
            return ins
        self.cnt[ek] += 1
        ins.then_inc(self.sem[ek], 1)
        ev = ("e", ek, self.cnt[ek])
        for (r, w) in self.pend[ek]:
            for b in r:
                if b.r.get(ek) == ("p", ek, 0):
                    b.r[ek] = ev
            for b in w:
                if b.w == ("p", ek, 0):
                    b.w = ev
        self.pend[ek] = []
        for b in reads:
            b.r[ek] = ev
        for b in writes:
            b.w = ev
            b.r = {}
        return ins

    def dma(self, out, in_, reads=(), writes=(), q="sp"):
        i = self.drr
        self.drr = (i + 1) % len(self.dsem)
        if self.dtot[i]:
            self._wait(q, ("d", i, self.dtot[i]))
        self._deps(q, reads, writes)
        ins = self.engs[q].dma_start(out=out, in_=in_)
        self.dtot[i] += 16
        ins.then_inc(self.dsem[i], 16)
        ev = ("d", i, self.dtot[i])
        for b in reads:
            b.r[("d", i)] = ev
        for b in writes:
            b.w = ev
            b.r = {}

    def collective(self, in_ap, out_ap):
        import os
        if os.environ.get("NOCC"):
            n0 = in_ap.shape[0]
            self.dma(out_ap[0:n0], in_ap)
            self.dma(out_ap[n0:2 * n0], in_ap)
            return
        q = "pool"
        if self.ccnt:
            self._wait(q, ("c", 0, self.ccnt))
        ins = self.nc.gpsimd.collective_compute(
            "AllGather", ALU.bypass, replica_groups=self.groups,
            ins=[in_ap], outs=[out_ap])
        self.ccnt += 1
        ins.then_inc(self.csem)

    def barrier(self):
        for ek in self.engs:
            assert not self.pend[ek], ek
        evs = [("e", k, self.cnt[k]) for k in self.engs if self.cnt[k]]
        evs += [("d", i, t) for i, t in enumerate(self.dtot) if t]
        if self.ccnt:
            evs.append(("c", 0, self.ccnt))
        for ek in self.engs:
            for ev in evs:
                self._wait(ek, ev)

    def sb(self, ph, shape, dtype, name):
        self.uid += 1
        return ph.enter_context(self.nc.sbuf_tensor(f"{name}_{self.uid}", list(shape), dtype))


def tok_blocks(n, bs=512):
    out = []
    s = 0
    while s < n:
        out.append((s, min(bs, n - s)))
        s += bs
    return out


class Prog:
    def __init__(self, LT, layers, final, need_ctx_last, B=4):
        self.B = B
        self.LT = LT
        self.T = LT + CT
        self.layers = layers
        self.final = final
        self.need_ctx_last = need_ctx_last
        self.nc = bass.Bass("TRN2", target_bir_lowering=False)
        self.inputs = {}
        self.uid = 0
        self.scr = {}
        self.inp_aps = {}
        self.vec_cache = {}
        self.dbg_names = []
        self.dbg_out = []

    def inp(self, name, shape, dtype=F32):
        if name in self.inp_aps:
            assert self.inputs[name][0] == tuple(shape), name
            return self.inp_aps[name]
        t = self.nc.dram_tensor(name, list(shape), dtype, kind="ExternalInput")
        self.inputs[name] = (tuple(shape), dtype)
        self.inp_aps[name] = t.ap()
        return self.inp_aps[name]

    def scratch(self, name, shape, dtype):
        t = self.nc.dram_tensor(name, list(shape), dtype)
        self.scr[name] = t
        return t

    def dump_dbg(self):
        fw = self.fw
        for name in self.dbg_names:
            t = self.scr[name]
            o = self.nc.dram_tensor("dbg_" + name, list(t.shape), t.dtype, kind="ExternalOutput")
            self.dbg_out.append("dbg_" + name)
            rows = t.shape[0]
            for r0 in range(0, rows, 128):
                r1 = min(rows, r0 + 128)
                fw.dma(o.ap()[r0:r1, :], t.ap()[r0:r1, :])
        fw.barrier()

    def build(self):
        nc = self.nc
        LT, T = self.LT, self.T
        es = contextlib.ExitStack()
        with es:
            fw = self.fw = FW(nc, es)
            fw.groups = [[2 * i, 2 * i + 1] for i in range(self.B)]
            self._persist = es
            self.ps = [es.enter_context(nc.psum_tensor(f"ps{i}", [128, 512], F32)) for i in range(8)]
            self.psb = [Buf(f"ps{i}") for i in range(8)]
            self.ones_bf = fw.sb(es, [128, 128], BF16, "ones")
            self.ident_bf = fw.sb(es, [128, 128], BF16, "ident")
            self.cb = Buf("const")
            ident_in = self.inp("ident", [128, 128], F32)
            idf = fw.sb(es, [128, 128], F32, "identf")
            fw.dma(idf[:], ident_in[:, :], writes=[self.cb])
            fw.op("dve", lambda e: e.tensor_copy(self.ident_bf[:], idf[:]), reads=[self.cb], writes=[self.cb])
            fw.op("dve", lambda e: e.memset(self.ones_bf[:], 1.0), writes=[self.cb])
            self.eps_t = fw.sb(es, [128, 1], F32, "epst")
            fw.op("dve", lambda e: e.memset(self.eps_t[:], EPS), writes=[self.cb])
            xT_own = self.inp("xT_own", [D, LT])
            xT_oth = self.inp("xT_oth", [D, LT]) if 0 in self.layers else None
            ctxT = self.inp("ctxT", [D, CT])
            cvec = self.inp("cvec", [128, KC, 2])
            self.silu_c = fw.sb(es, [128, KC, 2], F32, "siluc")
            cv = fw.sb(es, [128, KC, 2], F32, "cv")
            fw.dma(cv[:], cvec[:, :, :], writes=[self.cb])
            fw.op("act", lambda e: e.activation(out=self.silu_c[:], in_=cv[:], func=AF.Silu), reads=[self.cb], writes=[self.cb])
            fw.barrier()
            XA = self.scratch("XA", [D, T], F32)
            XB = self.scratch("XB", [D, T], F32)
            cur = (xT_own, ctxT)
            nxt = [XA, XB]
            outT = self.nc.dram_tensor("outT", [D, LT], F32, kind="ExternalOutput").ap()
            for li, L in enumerate(self.layers):
                dst = nxt[li % 2]
                dst_pair = (dst.ap()[:, 0:LT], dst.ap()[:, LT:T])
                last = (L == 3) and not self.need_ctx_last
                getattr(self, f"layer{L}")(cur, dst_pair, xT_oth, need_ctx=not last)
                cur = dst_pair
            if self.dbg_names:
                self.dump_dbg()
            if self.final:
                fng = self.load_vec("final_norm", KC)
                with contextlib.ExitStack() as ph:
                    nb = self.norm_bufs(ph, with_out=True)
                    self.norm_phase(nb, cur[0], 0, LT, fng, None, out_dram=outT)
                    fw.barrier()
            else:
                with contextlib.ExitStack() as ph:
                    t = fw.sb(ph, [128, KC, 512], F32, "dump")
                    tb = Buf()
                    for (s, n) in tok_blocks(LT):
                        fw.dma(t[:, :, :n], cur[0][:, s:s + n].rearrange("(k p) t -> p k t", p=128), writes=[tb])
                        fw.dma(outT[:, s:s + n].rearrange("(k p) t -> p k t", p=128), t[:, :, :n], reads=[tb])
                    fw.barrier()
            fw.barrier()
        return nc

    def load_vec(self, name, ncol):
        fw = self.fw
        if name in self.vec_cache:
            return self.vec_cache[name]
        ap = self.inp(name, [128, ncol])
        t = self._persist.enter_context(self.nc.sbuf_tensor(f"v_{name}", [128, ncol], F32))
        b = Buf(name)
        fw.dma(t[:], ap[:, :], writes=[b])
        fw.barrier()
        self.vec_cache[name] = t
        return t

    def mod_phase(self, L):
        fw, nc = self.fw, self.nc
        mod_w = self.inp(f"l{L}_mod_w", [D, 3 * D])
        mod_b = self.load_vec(f"l{L}_mod_b", 48)
        ng = self.load_vec(f"l{L}_norm", KC)
        pers = self._persist
        modT = pers.enter_context(nc.sbuf_tensor(f"modT{L}", [128, 48, 2], F32))
        res = {}
        for nm in ("a_l", "b_l", "g_l", "a_c", "b_c", "g_c"):
            res[nm] = pers.enter_context(nc.sbuf_tensor(f"{nm}{L}", [128, KC], F32))
        mb = Buf("modT")
        with contextlib.ExitStack() as ph:
            wst = [fw.sb(ph, [128, KC, 128], F32, "mwst") for _ in range(2)]
            wb = [Buf("mw0"), Buf("mw1")]
            for fc in range(48):
                s = fc % 2
                fw.dma(wst[s][:], mod_w[:, fc * 128:(fc + 1) * 128].rearrange("(k p) c -> p k c", p=128), writes=[wb[s]])
                pi = fc % 2
                for k in range(KC):
                    fw.op("pe", lambda e, k=k, s=s, pi=pi: e.matmul(self.ps[pi][:, 0:2], wst[s][:, k, :], self.silu_c[:, k, :],
                                                                      start=(k == 0), stop=(k == KC - 1)),
                          reads=[wb[s]], writes=[self.psb[pi]], inc=(k == KC - 1))
                fw.op("dve", lambda e, fc=fc, pi=pi: e.tensor_scalar(modT[:, fc, :], self.ps[pi][:, 0:2], mod_b[:, fc:fc + 1], None, ALU.add),
                      reads=[self.psb[pi]], writes=[mb])
            for j, sfx in ((0, "_l"), (1, "_c")):
                fw.op("dve", lambda e, j=j, sfx=sfx: e.scalar_tensor_tensor(res["a" + sfx][:], modT[:, 16:32, j], 1.0, ng[:], ALU.add, ALU.mult),
                      reads=[mb], writes=[mb])
                fw.op("dve", lambda e, j=j, sfx=sfx: e.tensor_copy(res["b" + sfx][:], modT[:, 0:16, j]), reads=[mb], writes=[mb])
                fw.op("dve", lambda e, j=j, sfx=sfx: e.tensor_copy(res["g" + sfx][:], modT[:, 32:48, j]), reads=[mb], writes=[mb])
            fw.barrier()
        return res

    def norm_bufs(self, ph, with_out=False):
        fw = self.fw
        nb = {}
        nb["xt"] = [fw.sb(ph, [128, KC, 512], F32, "xt") for _ in range(2)]
        nb["xb"] = [Buf("xt0"), Buf("xt1")]
        nb["sq"] = [fw.sb(ph, [128, 512], BF16, "sq") for _ in range(2)]
        nb["sqb"] = [Buf(), Buf()]
        nb["rstd"] = fw.sb(ph, [128, 512], F32, "rstd")
        nb["rb"] = Buf("rstd")
        nb["tmp"] = [fw.sb(ph, [128, 512], F32, "ntmp") for _ in range(2)]
        nb["tmb"] = [Buf(), Buf()]
        if with_out:
            nb["ot"] = [fw.sb(ph, [128, KC, 512], F32, "nout")]
            nb["otb"] = [Buf()]
        nb["cnt"] = 0
        return nb

    def norm_phase(self, nb, src, t0, n, a, b, hT=None, hoff=0, hbufs=None, out_dram=None):
        fw = self.fw
        xt, xb, sq, sqb, rstd, rb, tmp, tmb = (nb[k] for k in ("xt", "xb", "sq", "sqb", "rstd", "rb", "tmp", "tmb"))
        if out_dram is not None:
            ot, otb = nb["ot"], nb["otb"]
        blks = tok_blocks(n)
        for (s, m) in blks:
            bi = nb["cnt"]
            nb["cnt"] += 1
            sl = bi % 2
            fw.dma(xt[sl][:, :, :m], src[:, t0 + s:t0 + s + m].rearrange("(k p) t -> p k t", p=128), writes=[xb[sl]])
            pi = 6 + (bi % 2)
            for k in range(KC):
                q = k % 2
                fw.op("act", lambda e, k=k, q=q: e.activation(out=sq[q][:, :m], in_=xt[sl][:, k, :m], func=AF.Square),
                      reads=[xb[sl]], writes=[sqb[q]])
                fw.op("pe", lambda e, k=k, q=q: e.matmul(self.ps[pi][:, :m], self.ones_bf[:], sq[q][:, :m], start=(k == 0), stop=(k == KC - 1)),
                      reads=[sqb[q]], writes=[self.psb[pi]])
            fw.op("act", lambda e: e.activation(out=rstd[:, :m], in_=self.ps[pi][:, :m], func=AF.Sqrt, scale=1.0 / D, bias=self.eps_t[:, 0:1]),
                  reads=[self.psb[pi]], writes=[rb])
            fw.op("dve", lambda e: e.reciprocal(rstd[:, :m], rstd[:, :m]), reads=[rb], writes=[rb])
            for k in range(KC):
                q = k % 2
                fw.op("dve", lambda e, k=k, q=q: e.tensor_tensor(tmp[q][:, :m], xt[sl][:, k, :m], rstd[:, :m], ALU.mult),
                      reads=[xb[sl], rb], writes=[tmb[q]])
                if out_dram is None:
                    hb = hbufs[(hoff + s) // 512]
                    fw.op("act", lambda e, k=k, q=q: e.activation(out=hT[:, k, hoff + s:hoff + s + m], in_=tmp[q][:, :m], func=AF.Identity,
                                                                   scale=a[:, k:k + 1], bias=b[:, k:k + 1]),
                          reads=[tmb[q]], writes=[hb])
                else:
                    fw.op("act", lambda e, k=k, q=q: e.activation(out=ot[0][:, k, :m], in_=tmp[q][:, :m], func=AF.Copy, scale=a[:, k:k + 1]),
                          reads=[tmb[q]], writes=[otb[0]])
            if out_dram is not None:
                fw.dma(out_dram[:, t0 + s:t0 + s + m].rearrange("(k p) t -> p k t", p=128), ot[0][:, :, :m], reads=[otb[0]])

    def proj_fm(self, ph, actT, abufs, kc, ntok, w, jobs, rope=None):
        fw = self.fw
        wst = [fw.sb(ph, [128, kc, 128], F32, "wst") for _ in range(2)]
        wsb = [Buf(), Buf()]
        wbf = [fw.sb(ph, [128, kc, 128], BF16, "wbf") for _ in range(2)]
        wbb = [Buf(), Buf()]
        has_rope = any(j["mode"] == "rope" for j in jobs)
        has_res = any(j["mode"] == "resid" for j in jobs)
        if has_rope:
            wsw = [fw.sb(ph, [128, kc, 128], BF16, "wsw") for _ in range(2)]
            wwb = [Buf(), Buf()]
            Ct = fw.sb(ph, [128, ntok], F32, "ropeC")
            St = fw.sb(ph, [128, ntok], F32, "ropeS")
            rpb_ = Buf("rope")
            fw.dma(Ct[:], rope[0], writes=[rpb_])
            fw.dma(St[:], rope[1], writes=[rpb_])
            t1 = [fw.sb(ph, [128, 512], F32, "rt1") for _ in range(2)]
            t2 = [fw.sb(ph, [128, 512], F32, "rt2") for _ in range(2)]
            t1b = [Buf(), Buf()]
            t2b = [Buf(), Buf()]
        odt = F32 if has_res else BF16
        ot = [fw.sb(ph, [128, ntok], odt, "pot") for _ in range(2)]
        otb = [Buf(), Buf()]
        if has_res:
            xs = [fw.sb(ph, [128, ntok], F32, "pxs") for _ in range(2)]
            xsb = [Buf(), Buf()]
        blks = tok_blocks(ntok)
        evac_rr = 0
        for ji, job in enumerate(jobs):
            s = ji % 2
            M = sum(n for (_, n) in job["segs"])
            off = 0
            for (c0, n) in job["segs"]:
                fw.dma(wst[s][:, :, off:off + n], w[:, c0:c0 + n].rearrange("(k p) c -> p k c", p=128), writes=[wsb[s]])
                off += n
            fw.op("pool", lambda e, s=s, M=M: e.tensor_copy(wbf[s][:, :, :M], wst[s][:, :, :M]), reads=[wsb[s]], writes=[wbb[s]])
            mode = job["mode"]
            if mode == "rope":
                hd = job["swap"]
                v_in = wst[s][:, :, :M].rearrange("p k (h two j) -> p k h two j", two=2, j=hd)
                v_out = wsw[s][:, :, :M].rearrange("p k (h two j) -> p k h two j", two=2, j=hd)
                fw.op("pool", lambda e, v_in=v_in, v_out=v_out: e.tensor_copy(v_out[:, :, :, 0, :], v_in[:, :, :, 1, :]),
                      reads=[wsb[s]], writes=[wwb[s]])
                fw.op("pool", lambda e, v_in=v_in, v_out=v_out: e.tensor_copy(v_out[:, :, :, 1, :], v_in[:, :, :, 0, :]),
                      reads=[wsb[s]], writes=[wwb[s]])
            if mode == "resid":
                fw.dma(xs[s][:, :job["xsrc"][1]], job["xsrc"][0], writes=[xsb[s]])
            sc = job.get("scale", 1.0)
            dst = job["dst"]
            for bi, (t0, m) in enumerate(blks):
                pa = (2 * bi) % 4 if mode == "rope" else (evac_rr % 4)
                pb = pa + 1
                for k in range(kc):
                    fw.op("pe", lambda e, k=k, pa=pa: e.matmul(self.ps[pa][:M, :m], wbf[s][:, k, :M], actT[:, k, t0:t0 + m],
                                                                start=(k == 0), stop=(k == kc - 1)),
                          reads=[wbb[s], abufs[t0 // 512]], writes=[self.psb[pa]], inc=(k == kc - 1))
                if mode == "rope":
                    for k in range(kc):
                        fw.op("pe", lambda e, k=k, pb=pb: e.matmul(self.ps[pb][:M, :m], wsw[s][:, k, :M], actT[:, k, t0:t0 + m],
                                                                    start=(k == 0), stop=(k == kc - 1)),
                              reads=[wwb[s], abufs[t0 // 512]], writes=[self.psb[pb]], inc=(k == kc - 1))
                if dst[0] == "sb":
                    o_ap = dst[1][:M, dst[2], t0:t0 + m]
                    o_b = dst[3][t0 // 512]
                else:
                    o_ap = ot[s][:M, t0:t0 + m]
                    o_b = otb[s]
                if mode == "plain":
                    if evac_rr % 2 == 0:
                        fw.op("act", lambda e, pa=pa, o_ap=o_ap: e.activation(out=o_ap, in_=self.ps[pa][:M, :m], func=AF.Copy, scale=float(sc)),
                              reads=[self.psb[pa]], writes=[o_b])
                    else:
                        fw.op("dve", lambda e, pa=pa, o_ap=o_ap: e.tensor_scalar(o_ap, self.ps[pa][:M, :m], float(sc), None, ALU.mult),
                              reads=[self.psb[pa]], writes=[o_b])
                elif mode == "silu":
                    fw.op("act", lambda e, pa=pa, o_ap=o_ap: e.activation(out=o_ap, in_=self.ps[pa][:M, :m], func=AF.Silu),
                          reads=[self.psb[pa]], writes=[o_b])
                elif mode == "rope":
                    q = bi % 2
                    fw.op("dve", lambda e, pa=pa, q=q: e.scalar_tensor_tensor(t1[q][:M, :m], self.ps[pa][:M, :m], float(sc), Ct[:M, t0:t0 + m], ALU.mult, ALU.mult),
                          reads=[self.psb[pa], rpb_], writes=[t1b[q]])
                    fw.op("dve", lambda e, pb=pb, q=q: e.scalar_tensor_tensor(t2[q][:M, :m], self.ps[pb][:M, :m], float(sc), St[:M, t0:t0 + m], ALU.mult, ALU.mult),
                          reads=[self.psb[pb], rpb_], writes=[t2b[q]])
                    fw.op("pool", lambda e, q=q, o_ap=o_ap: e.tensor_tensor(o_ap, t1[q][:M, :m], t2[q][:M, :m], ALU.add),
                          reads=[t1b[q], t2b[q]], writes=[o_b])
                elif mode == "resid":
                    gate = job["gate"]
                    for (g0, g1, gap) in gate:
                        a0, a1 = max(g0, t0), min(g1, t0 + m)
                        if a0 >= a1:
                            continue
                        fw.op("dve", lambda e, pa=pa, a0=a0, a1=a1, gap=gap: e.scalar_tensor_tensor(
                            ot[s][:M, a0:a1], self.ps[pa][:M, a0 - t0:a1 - t0], gap, xs[s][:M, a0:a1], ALU.mult, ALU.add),
                            reads=[self.psb[pa], xsb[s]], writes=[o_b])
                evac_rr += 1
            if dst[0] == "dram":
                for (dap, c0, c1) in dst[1]:
                    fw.dma(dap, ot[s][:M, c0:c1], reads=[otb[s]])

    def proj_tm(self, ph, actT, abufs, kc, ntok, w, segs_list, dst_tok0):
        fw = self.fw
        NC_ = 256
        wst = [fw.sb(ph, [128, kc, NC_], F32, "vwst") for _ in range(2)]
        wsb = [Buf(), Buf()]
        wbf = [fw.sb(ph, [128, kc, NC_], BF16, "vwbf") for _ in range(2)]
        wbb = [Buf(), Buf()]
        vt = [fw.sb(ph, [128, 4, NC_], BF16, "vt") for _ in range(2)]
        vtb = [Buf(), Buf()]
        ntile = (ntok + 127) // 128
        assert ntok % 128 == 0
        rr = 0
        for ji, (segs, dsts) in enumerate(segs_list):
            s = ji % 2
            M = sum(n for (_, n) in segs)
            off = 0
            for (c0, n) in segs:
                fw.dma(wst[s][:, :, off:off + n], w[:, c0:c0 + n].rearrange("(k p) c -> p k c", p=128), writes=[wsb[s]])
                off += n
            fw.op("pool", lambda e, s=s, M=M: e.tensor_copy(wbf[s][:, :, :M], wst[s][:, :, :M]), reads=[wsb[s]], writes=[wbb[s]])
            for g0 in range(0, ntile, 4):
                gn = min(4, ntile - g0)
                vs = (g0 // 4) % 2
                for ti in range(g0, g0 + gn):
                    pa = 4 + (rr % 2)
                    rr += 1
                    for k in range(kc):
                        fw.op("pe", lambda e, k=k, pa=pa, ti=ti: e.matmul(self.ps[pa][:, :M], actT[:, k, ti * 128:(ti + 1) * 128], wbf[s][:, k, :M],
                                                                          start=(k == 0), stop=(k == kc - 1)),
                              reads=[wbb[s], abufs[(ti * 128) // 512]], writes=[self.psb[pa]], inc=(k == kc - 1))
                    if rr % 2 == 0:
                        fw.op("act", lambda e, pa=pa, ti=ti: e.activation(out=vt[vs][:, ti - g0, :M], in_=self.ps[pa][:, :M], func=AF.Copy),
                              reads=[self.psb[pa]], writes=[vtb[vs]])
                    else:
                        fw.op("dve", lambda e, pa=pa, ti=ti: e.tensor_copy(vt[vs][:, ti - g0, :M], self.ps[pa][:, :M]),
                              reads=[self.psb[pa]], writes=[vtb[vs]])
                r0 = dst_tok0 + g0 * 128
                for di, dap in enumerate(dsts):
                    w_ = dap.shape[1]
                    fw.dma(dap[r0:r0 + gn * 128, :].rearrange("(a p) c -> p a c", p=128), vt[vs][:, :gn, di * w_:(di + 1) * w_], reads=[vtb[vs]])

    def attn_stream(self, N, kblocks, exp_scale, pv, ptr, ptb, sring):
        fw = self.fw
        nk = len(kblocks)

        def emit_s(i):
            kb = kblocks[i]
            si = sring[i % len(sring)]
            parts = list(kb["s"])
            if kb.get("mask") is not None:
                parts.append((self.ident_bf[:], kb["mask"][0], kb["mask"][1]))
            for j, (l, r, deps) in enumerate(parts):
                fw.op("pe", lambda e, l=l, r=r, j=j, si=si: e.matmul(self.ps[si][:, :N], l, r, start=(j == 0), stop=(j == len(parts) - 1)),
                      reads=list(deps), writes=[self.psb[si]], inc=(j == len(parts) - 1))
            pi = i % len(ptr)
            fw.op("act", lambda e, si=si, pi=pi: e.activation(out=ptr[pi][:, :N], in_=self.ps[si][:, :N], func=AF.Exp, scale=float(exp_scale)),
                  reads=[self.psb[si]], writes=[ptb[pi]])

        def emit_pv(i):
            kb = kblocks[i]
            pi = i % len(ptr)
            for j, ((pidx, M), l) in enumerate(zip(pv, kb["v"])):
                fw.op("pe", lambda e, pidx=pidx, M=M, l=l, pi=pi: e.matmul(self.ps[pidx][:M, :N], l, ptr[pi][:, :N], start=(i == 0), stop=(i == nk - 1)),
                      reads=[ptb[pi]] + list(kb["vdeps"]), writes=[self.psb[pidx]], inc=(j == len(pv) - 1))

        emit_s(0)
        for i in range(nk):
            if i + 1 < nk:
                emit_s(i + 1)
            emit_pv(i)

    def groups(self):
        LT = self.LT
        if LT <= 2048:
            return [[("l", 0, LT), ("c", 0, CT)]]
        h = LT // 2
        return [[("l", 0, h)], [("l", h, LT - h), ("c", 0, CT)]]

    def make_hT(self, ph, grp, cur, mv, xoth=None):
        fw = self.fw
        ntok = sum(n for (_, _, n) in grp)
        hT = fw.sb(ph, [128, KC, ntok], BF16, "hT")
        hb = [Buf(f"h{i}") for i in range((ntok + 511) // 512)]
        off = 0
        segs = []
        with contextlib.ExitStack() as ph2:
            nb = self.norm_bufs(ph2)
            for (kind, t0, n) in grp:
                if kind == "l":
                    self.norm_phase(nb, cur[0], t0, n, mv["a_l"], mv["b_l"], hT=hT, hoff=off, hbufs=hb)
                elif kind == "o":
                    self.norm_phase(nb, xoth, t0, n, mv["a_l"], mv["b_l"], hT=hT, hoff=off, hbufs=hb)
                else:
                    self.norm_phase(nb, cur[1], t0, n, mv["a_c"], mv["b_c"], hT=hT, hoff=off, hbufs=hb)
                segs.append((kind, t0, n, off))
                off += n
            fw.barrier()
        return hT, hb, ntok, segs

    def dst_rows(self, scr, r0, M, segs, own_only=True):
        LT = self.LT
        out = []
        for (kind, t0, n, off) in segs:
            if kind == "l":
                out.append((scr[r0:r0 + M, t0:t0 + n], off, off + n))
            elif kind == "c":
                out.append((scr[r0:r0 + M, LT:LT + n], off, off + n))
        return out

    def rope_aps(self, Ct, St, segs):
        LT = self.LT
        kind, t0, n, off = segs[0]
        base = {"l": 0, "c": LT, "o": LT + CT}[kind] + t0
        tot = sum(s[2] for s in segs)
        return (Ct[:, base:base + tot], St[:, base:base + tot])

    def out_proj(self, cur, dst, w_out_name, OGT, mv, need_ctx):
        fw = self.fw
        LT = self.LT
        w_out = self.inp(w_out_name, [D, D])
        for grp in self.groups():
            grp = [g for g in grp if need_ctx or g[0] == "l"]
            with contextlib.ExitStack() as ph:
                ntok = sum(n for (_, _, n) in grp)
                og = fw.sb(ph, [128, KC, ntok], BF16, "ogT")
                ob = [Buf() for _ in range((ntok + 511) // 512)]
                segs = []
                off = 0
                for (kind, t0, n) in grp:
                    base = t0 if kind == "l" else LT
                    for (s, m) in tok_blocks(n):
                        fw.dma(og[:, :, off + s:off + s + m], OGT.ap()[:, base + s:base + s + m].rearrange("(k p) t -> p k t", p=128),
                               writes=[ob[(off + s) // 512]])
                    segs.append((kind, t0, n, off))
                    off += n
                jobs = []
                for c in range(KC):
                    gate = []
                    for (kind, t0, n, o) in segs:
                        gate.append((o, o + n, (mv["g_l"] if kind == "l" else mv["g_c"])[:, c:c + 1]))
                    srcs = []
                    dsts = []
                    for (kind, t0, n, o) in segs:
                        sap = cur[0][c * 128:(c + 1) * 128, t0:t0 + n] if kind == "l" else cur[1][c * 128:(c + 1) * 128, 0:n]
                        dap = dst[0][c * 128:(c + 1) * 128, t0:t0 + n] if kind == "l" else dst[1][c * 128:(c + 1) * 128, 0:n]
                        srcs.append((sap, o, o + n))
                        dsts.append((dap, o, o + n))
                    jobs.append(dict(segs=[(c * 128, 128)], mode="resid", dst=("dram", dsts), xsrcs=srcs, gate=gate))
                self.proj_fm_resid(ph, og, ob, ntok, w_out, jobs)
                fw.barrier()

    def proj_fm_resid(self, ph, actT, abufs, ntok, w, jobs):
        fw = self.fw
        for j in jobs:
            j["xsrc"] = None
        self._resid_jobs(ph, actT, abufs, ntok, w, jobs)

    def _resid_jobs(self, ph, actT, abufs, ntok, w, jobs):
        fw = self.fw
        kc = KC
        wst = [fw.sb(ph, [128, kc, 128], F32, "wst") for _ in range(2)]
        wsb = [Buf(), Buf()]
        wbf = [fw.sb(ph, [128, kc, 128], BF16, "wbf") for _ in range(2)]
        wbb = [Buf(), Buf()]
        ot = [fw.sb(ph, [128, ntok], F32, "pot") for _ in range(2)]
        otb = [Buf(), Buf()]
        xs = [fw.sb(ph, [128, ntok], F32, "pxs") for _ in range(2)]
        xsb = [Buf(), Buf()]
        blks = tok_blocks(ntok)
        rr = 0
        for ji, job in enumerate(jobs):
            s = ji % 2
            (c0, n) = job["segs"][0]
            fw.dma(wst[s][:, :, :n], w[:, c0:c0 + n].rearrange("(k p) c -> p k c", p=128), writes=[wsb[s]])
            fw.op("pool", lambda e, s=s: e.tensor_copy(wbf[s][:], wst[s][:]), reads=[wsb[s]], writes=[wbb[s]])
            for (sap, a0, a1) in job["xsrcs"]:
                fw.dma(xs[s][:, a0:a1], sap, writes=[xsb[s]])
            for bi, (t0, m) in enumerate(blks):
                pa = rr % 4
                rr += 1
                for k in range(kc):
                    fw.op("pe", lambda e, k=k, pa=pa: e.matmul(self.ps[pa][:, :m], wbf[s][:, k, :], actT[:, k, t0:t0 + m],
                                                                start=(k == 0), stop=(k == kc - 1)),
                          reads=[wbb[s], abufs[t0 // 512]], writes=[self.psb[pa]], inc=(k == kc - 1))
                for (g0, g1, gap) in job["gate"]:
                    a0, a1 = max(g0, t0), min(g1, t0 + m)
                    if a0 >= a1:
                        continue
                    fw.op("dve", lambda e, pa=pa, a0=a0, a1=a1, gap=gap: e.scalar_tensor_tensor(
                        ot[s][:, a0:a1], self.ps[pa][:, a0 - t0:a1 - t0], gap, xs[s][:, a0:a1], ALU.mult, ALU.add),
                        reads=[self.psb[pa], xsb[s]], writes=[otb[s]])
            for (dap, a0, a1) in job["dst"][1]:
                fw.dma(dap, ot[s][:, a0:a1], reads=[otb[s]])

    def layer3(self, cur, dst, xoth, need_ctx):
        fw, nc = self.fw, self.nc
        LT, T = self.LT, self.T
        L = 3
        mv = self.mod_phase(L)
        w_in = self.inp("l3_w_in", [D, 8192])
        lamv = self.load_vec("l3_lam", 4)
        subg = self.load_vec("l3_subln", 2)
        Ct = self.inp("rope128C", [128, T])
        St = self.inp("rope128S", [128, T])
        lam_init = 0.8 - 0.6 * math.exp(-0.3 * 3)
        QT = self.scratch("QT3", [2048, T], BF16)
        KTc = [self.scratch(f"KT3_{c}", [128, T], BF16) for c in range(16)]
        KTallc = [self.scratch(f"KT3all_{c}", [256, T], BF16) for c in range(16)]
        Vc = [self.scratch(f"V3_{c}", [T, 128], BF16) for c in range(16)]
        Vallc = [self.scratch(f"V3all_{c}", [2 * T, 128], BF16) for c in range(16)]
        GT = self.scratch("GT3", [2048, T], BF16)
        OGT = self.scratch("OGT3", [2048, T], BF16)
        scale = 128 ** -0.5
        pers = self._persist
        neglam = pers.enter_context(nc.sbuf_tensor("neglam", [128, 1], F32))
        subw = pers.enter_context(nc.sbuf_tensor("subw", [128, 2], F32))
        with contextlib.ExitStack() as ph:
            pr = fw.sb(ph, [128, 2], F32, "lpr")
            ones_f = fw.sb(ph, [128, 128], F32, "onesf")
            ex = fw.sb(ph, [128, 2], F32, "lex")
            lb = Buf()
            fw.op("dve", lambda e: e.memset(ones_f[:], 1.0), writes=[lb])
            v4 = lamv[:].rearrange("p (a b) -> p a b", b=2)
            fw.op("dve", lambda e: e.tensor_tensor(pr[:], v4[:, :, 0], v4[:, :, 1], ALU.mult), reads=[lb], writes=[lb])
            fw.op("pe", lambda e: e.matmul(self.ps[0][:, 0:2], ones_f[:], pr[:], start=True, stop=True), reads=[lb], writes=[self.psb[0]])
            fw.op("act", lambda e: e.activation(out=ex[:], in_=self.ps[0][:, 0:2], func=AF.Exp), reads=[self.psb[0]], writes=[lb])
            fw.op("dve", lambda e: e.scalar_tensor_tensor(neglam[:], ex[:, 1:2], -lam_init, ex[:, 0:1], ALU.add, ALU.subtract), reads=[lb], writes=[lb])
            fw.op("dve", lambda e: e.tensor_scalar(subw[:], subg[:], 1.0 - lam_init, None, ALU.mult), reads=[lb], writes=[lb])
            fw.barrier()
            import os
            if os.environ.get("DBGLAM"):
                o = self.nc.dram_tensor("dbg_lam", [128, 8], F32, kind="ExternalOutput")
                self.dbg_out.append("dbg_lam")
                dd = fw.sb(ph, [128, 8], F32, "dd")
                db = Buf()
                fw.op("dve", lambda e: e.memset(dd[:], 0.0), writes=[db])
                fw.op("dve", lambda e: e.tensor_copy(dd[:, 0:1], neglam[:]), writes=[db])
                fw.op("dve", lambda e: e.tensor_copy(dd[:, 1:3], ex[:]), writes=[db])
                fw.op("dve", lambda e: e.tensor_copy(dd[:, 3:5], pr[:]), writes=[db])
                fw.op("dve", lambda e: e.tensor_copy(dd[:, 5:7], subw[:]), writes=[db])
                fw.dma(o.ap()[:, :], dd[:], reads=[db])
                fw.barrier()
        for grp in self.groups():
            with contextlib.ExitStack() as ph:
                hT, hb, ntok, segs = self.make_hT(ph, grp, cur, mv)
                with contextlib.ExitStack() as ph2:
                    jobs = []
                    for c in range(16):
                        jobs.append(dict(segs=[(c * 128, 128)], mode="rope", swap=64, dst=("dram", self.dst_rows(QT.ap(), c * 128, 128, segs))))
                    for c in range(16):
                        jobs.append(dict(segs=[(2048 + c * 128, 128)], mode="rope", swap=64, dst=("dram", self.dst_rows(KTc[c].ap(), 0, 128, segs))))
                    for c in range(16):
                        jobs.append(dict(segs=[(6144 + c * 128, 128)], mode="silu", dst=("dram", self.dst_rows(GT.ap(), c * 128, 128, segs))))
                    self.proj_fm(ph2, hT, hb, KC, ntok, w_in, jobs, rope=self.rope_aps(Ct, St, segs))
                    fw.barrier()
                with contextlib.ExitStack() as ph2:
                    for (kind, t0, n, off) in segs:
                        base = t0 if kind == "l" else LT
                        sub = [([(4096 + j * 256, 256)], [Vc[2 * j].ap(), Vc[2 * j + 1].ap()]) for j in range(8)]
                        self.proj_tm(ph2, _View(hT, off), hb[off // 512:], KC, n, w_in, sub, base)
                    fw.barrier()
        for c in range(16):
            fw.collective(KTc[c].ap().opt(), KTallc[c].ap().opt())
            fw.collective(Vc[c].ap().opt(), Vallc[c].ap().opt())
        fw.barrier()
        qsegs = [("l", s, n) for (s, n) in tok_blocks(LT)]
        if need_ctx:
            qsegs.append(("c", 0, CT))
        nkb_l = LT // 128
        with contextlib.ExitStack() as ph:
            NKB = 2 * nkb_l + CT // 128
            Kt = [[fw.sb(ph, [128, NKB * 128], BF16, "K3") for _ in range(2)] for _ in range(2)]
            Kb = [[[Buf() for _ in range(3)] for _ in range(2)] for _ in range(2)]
            Vt = [fw.sb(ph, [128, NKB, 256], BF16, "V3") for _ in range(2)]
            Vb = [[Buf() for _ in range(3)] for _ in range(2)]
            Qt = [fw.sb(ph, [128, 512], BF16, "Q3") for _ in range(4)]
            Qb = [Buf() for _ in range(4)]
            Gt = [fw.sb(ph, [128, 2, 512], BF16, "G3") for _ in range(2)]
            Gb = [Buf() for _ in range(2)]
            ptr = [fw.sb(ph, [128, 512], BF16, "P3") for _ in range(3)]
            ptb = [Buf() for _ in range(3)]
            o1 = [fw.sb(ph, [128, 2, 512], F32, "o1") for _ in range(2)]
            o1b = [Buf() for _ in range(2)]
            rc = [fw.sb(ph, [128, 512], F32, "rc") for _ in range(2)]
            rcb = [Buf() for _ in range(2)]
            tt = [fw.sb(ph, [128, 512], F32, "tt") for _ in range(2)]
            ttb = [Buf() for _ in range(2)]
            sqt = [fw.sb(ph, [128, 512], BF16, "sq3") for _ in range(2)]
            sqb = [Buf() for _ in range(2)]
            rs = fw.sb(ph, [128, 512], F32, "rs3")
            rsb = Buf()
            ogt = [fw.sb(ph, [128, 2, 512], BF16, "og3") for _ in range(2)]
            ogb = [Buf() for _ in range(2)]
            def load_head(h):
                sl = h % 2
                for i in range(2):
                    r0 = (2 * h + i) * 128
                    c = 2 * h + i
                    for rk in range(2):
                        fw.dma(Kt[sl][i][:, rk * LT:(rk + 1) * LT], KTallc[c].ap()[rk * 128:(rk + 1) * 128, 0:LT], writes=[Kb[sl][i][rk]])
                    fw.dma(Kt[sl][i][:, 2 * LT:2 * LT + CT], KTc[c].ap()[:, LT:T], writes=[Kb[sl][i][2]])
                for cc in range(2):
                    c = 2 * h + cc
                    for rk in range(2):
                        fw.dma(Vt[sl][:, rk * nkb_l:(rk + 1) * nkb_l, cc * 128:(cc + 1) * 128],
                               Vallc[c].ap()[rk * T:rk * T + LT, :].rearrange("(b p) c -> p b c", p=128), writes=[Vb[sl][rk]])
                    fw.dma(Vt[sl][:, 2 * nkb_l:NKB, cc * 128:(cc + 1) * 128], Vc[c].ap()[LT:T, :].rearrange("(b p) c -> p b c", p=128), writes=[Vb[sl][2]])

            items = [(h, qs_) for h in range(8) for qs_ in qsegs]

            def load_q(qi):
                h, (kind, s0, N) = items[qi]
                tb = s0 if kind == "l" else LT
                gs = qi % 2
                fw.dma(Gt[gs][:, :, :N], GT.ap()[h * 256:(h + 1) * 256, tb:tb + N].rearrange("(c p) t -> p c t", p=128), writes=[Gb[gs]])
                for i in range(2):
                    qs = (2 * qi + i) % 4
                    r0 = (2 * h + i) * 128
                    fw.dma(Qt[qs][:, :N], QT.ap()[r0:r0 + 128, tb:tb + N], writes=[Qb[qs]])

            acc_rr = 0
            load_head(0)
            load_q(0)
            for qi, (h, (kind, s0, N)) in enumerate(items):
                sl = h % 2
                tb = s0 if kind == "l" else LT
                gs = qi % 2
                if qi % len(qsegs) == 0 and h + 1 < 8:
                    load_head(h + 1)
                if qi + 1 < len(items):
                    load_q(qi + 1)
                for i in range(2):
                        qs = (2 * qi + i) % 4
                        if kind == "l":
                            kbl = list(range(NKB))
                        else:
                            kbl = list(range(2 * nkb_l, NKB))
                        a0 = 2 + 3 * (acc_rr % 2)
                        acc_rr += 1
                        pv = [(a0, 128), (a0 + 1, 128), (a0 + 2, 128)]
                        kblocks = []
                        for kb in kbl:
                            part = 0 if kb < nkb_l else (1 if kb < 2 * nkb_l else 2)
                            kblocks.append(dict(
                                s=[(Kt[sl][i][:, kb * 128:(kb + 1) * 128], Qt[qs][:, :N], [Kb[sl][i][part], Qb[qs]])],
                                v=[Vt[sl][:, kb, 0:128], Vt[sl][:, kb, 128:256], self.ones_bf[:]],
                                vdeps=[Vb[sl][part]]))
                        self.attn_stream(N, kblocks, scale, pv, ptr, ptb, [0, 1])
                        ri = i
                        fw.op("dve", lambda e: e.reciprocal(rc[ri][:, :N], self.ps[a0 + 2][:, :N]), reads=[self.psb[a0 + 2]], writes=[rcb[ri]])
                        os_ = qi % 2
                        if i == 0:
                            for c in range(2):
                                fw.op("dve", lambda e: e.tensor_tensor(o1[os_][:, c, :N], self.ps[a0 + c][:, :N], rc[0][:, :N], ALU.mult),
                                      reads=[self.psb[a0 + c], rcb[0]], writes=[o1b[os_]])
                        else:
                            for c in range(2):
                                fw.op("dve", lambda e: e.tensor_tensor(tt[c][:, :N], self.ps[a0 + c][:, :N], rc[1][:, :N], ALU.mult),
                                      reads=[self.psb[a0 + c], rcb[1]], writes=[ttb[c]])
                                fw.op("dve", lambda e: e.scalar_tensor_tensor(o1[os_][:, c, :N], tt[c][:, :N], neglam[:, 0:1], o1[os_][:, c, :N], ALU.mult, ALU.add),
                                      reads=[ttb[c], o1b[os_]], writes=[o1b[os_]])
                                fw.op("act", lambda e: e.activation(out=sqt[c][:, :N], in_=o1[os_][:, c, :N], func=AF.Square),
                                      reads=[o1b[os_]], writes=[sqb[c]])
                            for c in range(2):
                                fw.op("pe", lambda e: e.matmul(self.ps[0][:, :N], self.ones_bf[:], sqt[c][:, :N], start=(c == 0), stop=(c == 1)),
                                      reads=[sqb[c]], writes=[self.psb[0]], inc=(c == 1))
                            fw.op("act", lambda e: e.activation(out=rs[:, :N], in_=self.ps[0][:, :N], func=AF.Sqrt, scale=1.0 / 256, bias=self.eps_t[:, 0:1]),
                                  reads=[self.psb[0]], writes=[rsb])
                            fw.op("dve", lambda e: e.reciprocal(rs[:, :N], rs[:, :N]), reads=[rsb], writes=[rsb])
                            for c in range(2):
                                fw.op("dve", lambda e: e.scalar_tensor_tensor(tt[c][:, :N], o1[os_][:, c, :N], subw[:, c:c + 1], rs[:, :N], ALU.mult, ALU.mult),
                                      reads=[o1b[os_], rsb], writes=[ttb[c]])
                                fw.op("pool", lambda e: e.tensor_tensor(ogt[os_][:, c, :N], tt[c][:, :N], Gt[gs][:, c, :N], ALU.mult),
                                      reads=[ttb[c], Gb[gs]], writes=[ogb[os_]])
                            fw.dma(OGT.ap()[h * 256:(h + 1) * 256, tb:tb + N].rearrange("(c p) t -> p c t", p=128), ogt[os_][:, :, :N], reads=[ogb[os_]])
            fw.barrier()
        self.out_proj(cur, dst, "l3_w_out", OGT, mv, need_ctx)


    def simple_bufs(self, ph, M):
        fw = self.fw
        sbf = {}
        sbf["Qt"] = [fw.sb(ph, [128, 512], BF16, "Qt") for _ in range(2)]
        sbf["Qb"] = [Buf(), Buf()]
        sbf["Gt"] = [fw.sb(ph, [128, 512], BF16, "Gt") for _ in range(2)]
        sbf["Gb"] = [Buf(), Buf()]
        sbf["ptr"] = [fw.sb(ph, [128, 512], BF16, "Pt") for _ in range(3)]
        sbf["ptb"] = [Buf() for _ in range(3)]
        sbf["rc"] = [fw.sb(ph, [128, 512], F32, "rc") for _ in range(2)]
        sbf["rcb"] = [Buf(), Buf()]
        sbf["tt"] = [fw.sb(ph, [128, 512], F32, "tt") for _ in range(2)]
        sbf["ttb"] = [Buf(), Buf()]
        sbf["og"] = [fw.sb(ph, [128, 512], BF16, "og") for _ in range(2)]
        sbf["ogb"] = [Buf(), Buf()]
        return sbf

    def simple_finalize(self, sbf, qi, M, N, po, pd, extra, Gt_ap, Gbuf, out_ap):
        fw = self.fw
        s = qi % 2
        rc, rcb, tt, ttb, og, ogb = (sbf[k] for k in ("rc", "rcb", "tt", "ttb", "og", "ogb"))
        if extra is not None:
            fw.op("dve", lambda e: e.tensor_scalar(rc[s][:M, :N], self.ps[pd][:M, :N], extra, None, ALU.add), reads=[self.psb[pd]], writes=[rcb[s]])
            fw.op("dve", lambda e: e.reciprocal(rc[s][:M, :N], rc[s][:M, :N]), reads=[rcb[s]], writes=[rcb[s]])
        else:
            fw.op("dve", lambda e: e.reciprocal(rc[s][:M, :N], self.ps[pd][:M, :N]), reads=[self.psb[pd]], writes=[rcb[s]])
        fw.op("dve", lambda e: e.tensor_tensor(tt[s][:M, :N], self.ps[po][:M, :N], rc[s][:M, :N], ALU.mult), reads=[self.psb[po], rcb[s]], writes=[ttb[s]])
        fw.op("pool", lambda e: e.tensor_tensor(og[s][:M, :N], tt[s][:M, :N], Gt_ap, ALU.mult), reads=[ttb[s], Gbuf], writes=[ogb[s]])
        fw.dma(out_ap, og[s][:M, :N], reads=[ogb[s]])

    def layer1(self, cur, dst, xoth, need_ctx):
        fw, nc = self.fw, self.nc
        LT, T = self.LT, self.T
        mv = self.mod_phase(1)
        w_in = self.inp("l1_w_in", [D, 4608])
        Ct = self.inp("rope64C", [128, T + LT])
        St = self.inp("rope64S", [128, T + LT])
        sink = self.load_vec("l1_sink_rep", 32)
        masks_in = self.inp("swa_mask", [128, 8, 512], BF16)
        QT = self.scratch("QT1", [2048, T], BF16)
        KT = self.scratch("KT1", [256, T], BF16)
        V = self.scratch("V1", [T, 256], BF16)
        GT = self.scratch("GT1", [2048, T], BF16)
        OGT = self.scratch("OGT1", [2048, T], BF16)
        KH = self.scratch("KH1", [256, 256], BF16)
        KHall = self.scratch("KH1all", [512, 256], BF16)
        VH = self.scratch("VH1", [256, 256], BF16)
        VHall = self.scratch("VH1all", [512, 256], BF16)
        pers = self._persist
        es_ = pers.enter_context(nc.sbuf_tensor("sinkexp", [128, 32], F32))
        sb_ = Buf()
        fw.op("act", lambda e: e.activation(out=es_[:], in_=sink[:], func=AF.Exp), writes=[sb_])
        fw.barrier()
        for grp in self.groups():
            with contextlib.ExitStack() as ph:
                hT, hb, ntok, segs = self.make_hT(ph, grp, cur, mv)
                with contextlib.ExitStack() as ph2:
                    jobs = []
                    for c in range(16):
                        jobs.append(dict(segs=[(c * 128, 128)], mode="rope", swap=32, scale=0.125, dst=("dram", self.dst_rows(QT.ap(), c * 128, 128, segs))))
                    for c in range(2):
                        jobs.append(dict(segs=[(2048 + c * 128, 128)], mode="rope", swap=32, dst=("dram", self.dst_rows(KT.ap(), c * 128, 128, segs))))
                    for c in range(16):
                        jobs.append(dict(segs=[(2560 + c * 128, 128)], mode="silu", dst=("dram", self.dst_rows(GT.ap(), c * 128, 128, segs))))
                    self.proj_fm(ph2, hT, hb, KC, ntok, w_in, jobs, rope=self.rope_aps(Ct, St, segs))
                    fw.barrier()
                with contextlib.ExitStack() as ph2:
                    for (kind, t0, n, off) in segs:
                        base = t0 if kind == "l" else LT
                        self.proj_tm(ph2, _View(hT, off), hb[off // 512:], KC, n, w_in, [([(2304, 256)], [V.ap()])], base)
                    fw.barrier()
        fw.dma(KH.ap()[:, 0:128], KT.ap()[:, 0:128])
        fw.dma(KH.ap()[:, 128:256], KT.ap()[:, LT - 128:LT])
        fw.dma(VH.ap()[0:128, :], V.ap()[0:128, :])
        fw.dma(VH.ap()[128:256, :], V.ap()[LT - 128:LT, :])
        fw.barrier()
        fw.collective(KH.ap().opt(), KHall.ap().opt())
        fw.collective(VH.ap().opt(), VHall.ap().opt())
        fw.barrier()
        nbl = LT // 128
        NB = nbl + 2 + 2
        nR = LT // 512
        with contextlib.ExitStack() as ph:
            msk = fw.sb(ph, [128, 8, 512], BF16, "swamask")
            mb = Buf()
            fw.dma(msk[:], masks_in[:, :, :], writes=[mb])
            Kt = [fw.sb(ph, [64, NB * 128], BF16, "K1") for _ in range(2)]
            Kb = [Buf(), Buf()]
            Vt = [fw.sb(ph, [128, NB, 64], BF16, "V1") for _ in range(2)]
            Vb = [Buf(), Buf()]
            sbf = self.simple_bufs(ph, 64)
            Qt, Qb, Gt, Gb, ptr, ptb = (sbf[k] for k in ("Qt", "Qb", "Gt", "Gb", "ptr", "ptb"))

            def load_kv(g):
                s = g % 2
                r0 = 64 * g
                fw.dma(Kt[s][:, 0:128], KHall.ap()[r0:r0 + 64, 128:256], writes=[Kb[s]])
                fw.dma(Kt[s][:, 128:128 + LT], KT.ap()[r0:r0 + 64, 0:LT], writes=[Kb[s]])
                fw.dma(Kt[s][:, 128 + LT:256 + LT], KHall.ap()[256 + r0:256 + r0 + 64, 0:128], writes=[Kb[s]])
                fw.dma(Kt[s][:, 256 + LT:256 + LT + CT], KT.ap()[r0:r0 + 64, LT:T], writes=[Kb[s]])
                fw.dma(Vt[s][:, 0, :], VHall.ap()[128:256, r0:r0 + 64], writes=[Vb[s]])
                fw.dma(Vt[s][:, 1:1 + nbl, :], V.ap()[0:LT, r0:r0 + 64].rearrange("(b p) c -> p b c", p=128), writes=[Vb[s]])
                fw.dma(Vt[s][:, 1 + nbl, :], VHall.ap()[256:384, r0:r0 + 64], writes=[Vb[s]])
                fw.dma(Vt[s][:, 2 + nbl:NB, :], V.ap()[LT:T, r0:r0 + 64].rearrange("(b p) c -> p b c", p=128), writes=[Vb[s]])

            qsegs = [("l", R) for R in range(nR)] + ([("c", 0)] if need_ctx else [])
            items = [(h, qs_) for h in range(32) for qs_ in qsegs]

            def load_q(qi):
                h, (kind, R) = items[qi]
                tb, N = (R * 512, 512) if kind == "l" else (LT, CT)
                s = qi % 2
                fw.dma(Qt[s][:64, :N], QT.ap()[64 * h:64 * h + 64, tb:tb + N], writes=[Qb[s]])
                fw.dma(Gt[s][:64, :N], GT.ap()[64 * h:64 * h + 64, tb:tb + N], writes=[Gb[s]])

            load_kv(0)
            load_q(0)
            for qi, (h, (kind, R)) in enumerate(items):
                g = h // 8
                ks = g % 2
                s = qi % 2
                tb, N = (R * 512, 512) if kind == "l" else (LT, CT)
                if qi % (8 * len(qsegs)) == 0 and g + 1 < 4:
                    load_kv(g + 1)
                if qi + 1 < len(items):
                    load_q(qi + 1)
                kblocks = []
                if kind == "l":
                    for jo in range(6):
                        blk = 4 * R + jo
                        mi = jo
                        if R == 0 and jo == 0:
                            mi = 6
                        if R == nR - 1 and jo == 5:
                            mi = 7
                        kblocks.append(dict(s=[(Kt[ks][:, blk * 128:(blk + 1) * 128], Qt[s][:64, :N], [Kb[ks], Qb[s]])],
                                            mask=(msk[:, mi, :N], [mb]),
                                            v=[Vt[ks][:, blk, :], self.ones_bf[:, 0:64]], vdeps=[Vb[ks]]))
                for cb in range(2):
                    blk = 2 + nbl + cb
                    kblocks.append(dict(s=[(Kt[ks][:, blk * 128:(blk + 1) * 128], Qt[s][:64, :N], [Kb[ks], Qb[s]])],
                                        v=[Vt[ks][:, blk, :], self.ones_bf[:, 0:64]], vdeps=[Vb[ks]]))
                a0 = 2 + 2 * (qi % 3)
                self.attn_stream(N, kblocks, 1.0, [(a0, 64), (a0 + 1, 64)], ptr, ptb, [0, 1])
                self.simple_finalize(sbf, qi, 64, N, a0, a0 + 1, es_[:64, h:h + 1], Gt[s][:64, :N], Gb[s], OGT.ap()[64 * h:64 * h + 64, tb:tb + N])
            fw.barrier()
        self.out_proj(cur, dst, "l1_w_out", OGT, mv, need_ctx)

    def layer2(self, cur, dst, xoth, need_ctx):
        fw, nc = self.fw, self.nc
        LT, T = self.LT, self.T
        mv = self.mod_phase(2)
        w_in = self.inp("l2_w_in", [D, 8192])
        bias_in = self.inp("na_bias", [32, 128, 24, 512], BF16)
        QT = self.scratch("QT2", [2048, T], BF16)
        KT = self.scratch("KT2", [2048, T], BF16)
        V = self.scratch("V2", [T, 2048], BF16)
        GT = self.scratch("GT2", [2048, T], BF16)
        OGT = self.scratch("OGT2", [2048, T], BF16)
        KH = [self.scratch(f"KH2_{i}", [1024, 512], BF16) for i in range(2)]
        KHall = [self.scratch(f"KH2all_{i}", [2048, 512], BF16) for i in range(2)]
        VH = [self.scratch(f"VH2_{i}", [256, 2048], BF16) for i in range(2)]
        VHall = [self.scratch(f"VH2all_{i}", [512, 2048], BF16) for i in range(2)]
        for grp in self.groups():
            with contextlib.ExitStack() as ph:
                hT, hb, ntok, segs = self.make_hT(ph, grp, cur, mv)
                with contextlib.ExitStack() as ph2:
                    jobs = []
                    for c in range(16):
                        jobs.append(dict(segs=[(c * 128, 128)], mode="plain", scale=0.125, dst=("dram", self.dst_rows(QT.ap(), c * 128, 128, segs))))
                    for c in range(16):
                        jobs.append(dict(segs=[(2048 + c * 128, 128)], mode="plain", dst=("dram", self.dst_rows(KT.ap(), c * 128, 128, segs))))
                    for c in range(16):
                        jobs.append(dict(segs=[(6144 + c * 128, 128)], mode="silu", dst=("dram", self.dst_rows(GT.ap(), c * 128, 128, segs))))
                    self.proj_fm(ph2, hT, hb, KC, ntok, w_in, jobs)
                    fw.barrier()
                with contextlib.ExitStack() as ph2:
                    for (kind, t0, n, off) in segs:
                        base = t0 if kind == "l" else LT
                        sub = [([(4096 + j * 256, 256)], [V.ap()[:, j * 256:(j + 1) * 256]]) for j in range(8)]
                        self.proj_tm(ph2, _View(hT, off), hb[off // 512:], KC, n, w_in, sub, base)
                    fw.barrier()
        for i in range(2):
            fw.dma(KH[i].ap()[:, 0:256], KT.ap()[i * 1024:(i + 1) * 1024, 0:256])
            fw.dma(KH[i].ap()[:, 256:512], KT.ap()[i * 1024:(i + 1) * 1024, LT - 256:LT])
        fw.dma(VH[0].ap()[:, :], V.ap()[0:256, :])
        fw.dma(VH[1].ap()[:, :], V.ap()[LT - 256:LT, :])
        fw.barrier()
        for i in range(2):
            fw.collective(KH[i].ap().opt(), KHall[i].ap().opt())
            fw.collective(VH[i].ap().opt(), VHall[i].ap().opt())
        fw.barrier()
        nbl = LT // 128
        NB = nbl + 4 + 2
        nR = LT // 512
        with contextlib.ExitStack() as ph:
            Kt = [fw.sb(ph, [64, NB * 128], BF16, "K2") for _ in range(2)]
            Kb = [Buf(), Buf()]
            Vt = [fw.sb(ph, [128, NB, 64], BF16, "V2") for _ in range(2)]
            Vb = [Buf(), Buf()]
            Bt = [fw.sb(ph, [128, 24, 512], BF16, "B2") for _ in range(2)]
            Bb = [Buf(), Buf()]
            sbf = self.simple_bufs(ph, 64)
            Qt, Qb, Gt, Gb, ptr, ptb = (sbf[k] for k in ("Qt", "Qb", "Gt", "Gb", "ptr", "ptb"))

            def load_kv(h):
                s = h % 2
                r0 = 64 * h
                ci, wi = r0 // 1024, r0 % 1024
                fw.dma(Kt[s][:, 0:256], KHall[ci].ap()[wi:wi + 64, 256:512], writes=[Kb[s]])
                fw.dma(Kt[s][:, 256:256 + LT], KT.ap()[r0:r0 + 64, 0:LT], writes=[Kb[s]])
                fw.dma(Kt[s][:, 256 + LT:512 + LT], KHall[ci].ap()[1024 + wi:1024 + wi + 64, 0:256], writes=[Kb[s]])
                fw.dma(Kt[s][:, 512 + LT:512 + LT + CT], KT.ap()[r0:r0 + 64, LT:T], writes=[Kb[s]])
                fw.dma(Vt[s][:, 0:2, :], VHall[1].ap()[0:256, r0:r0 + 64].rearrange("(b p) c -> p b c", p=128), writes=[Vb[s]])
                fw.dma(Vt[s][:, 2:2 + nbl, :], V.ap()[0:LT, r0:r0 + 64].rearrange("(b p) c -> p b c", p=128), writes=[Vb[s]])
                fw.dma(Vt[s][:, 2 + nbl:4 + nbl, :], VHall[0].ap()[256:512, r0:r0 + 64].rearrange("(b p) c -> p b c", p=128), writes=[Vb[s]])
                fw.dma(Vt[s][:, 4 + nbl:NB, :], V.ap()[LT:T, r0:r0 + 64].rearrange("(b p) c -> p b c", p=128), writes=[Vb[s]])
                fw.dma(Bt[s][:], bias_in[h, :, :, :], writes=[Bb[s]])

            qsegs = [("l", R) for R in range(nR)] + ([("c", 0)] if need_ctx else [])
            items = [(h, qs_) for h in range(32) for qs_ in qsegs]

            def load_q(qi):
                h, (kind, R) = items[qi]
                tb, N = (R * 512, 512) if kind == "l" else (LT, CT)
                s = qi % 2
                fw.dma(Qt[s][:64, :N], QT.ap()[64 * h:64 * h + 64, tb:tb + N], writes=[Qb[s]])
                fw.dma(Gt[s][:64, :N], GT.ap()[64 * h:64 * h + 64, tb:tb + N], writes=[Gb[s]])

            load_kv(0)
            load_q(0)
            for qi, (h, (kind, R)) in enumerate(items):
                ks = h % 2
                s = qi % 2
                tb, N = (R * 512, 512) if kind == "l" else (LT, CT)
                if qi % len(qsegs) == 0 and h + 1 < 32:
                    load_kv(h + 1)
                if qi + 1 < len(items):
                    load_q(qi + 1)
                kblocks = []
                if kind == "l":
                    var = 0 if R == 0 else (2 if R == nR - 1 else 1)
                    for jo in range(8):
                        blk = 4 * R + jo
                        kblocks.append(dict(s=[(Kt[ks][:, blk * 128:(blk + 1) * 128], Qt[s][:64, :N], [Kb[ks], Qb[s]])],
                                            mask=(Bt[ks][:, var * 8 + jo, :N], [Bb[ks]]),
                                            v=[Vt[ks][:, blk, :], self.ones_bf[:, 0:64]], vdeps=[Vb[ks]]))
                for cb in range(2):
                    blk = 4 + nbl + cb
                    kblocks.append(dict(s=[(Kt[ks][:, blk * 128:(blk + 1) * 128], Qt[s][:64, :N], [Kb[ks], Qb[s]])],
                                        v=[Vt[ks][:, blk, :], self.ones_bf[:, 0:64]], vdeps=[Vb[ks]]))
                a0 = 2 + 2 * (qi % 3)
                self.attn_stream(N, kblocks, 1.0, [(a0, 64), (a0 + 1, 64)], ptr, ptb, [0, 1])
                self.simple_finalize(sbf, qi, 64, N, a0, a0 + 1, None, Gt[s][:64, :N], Gb[s], OGT.ap()[64 * h:64 * h + 64, tb:tb + N])
            fw.barrier()
        self.out_proj(cur, dst, "l2_w_out", OGT, mv, need_ctx)

    def layer0(self, cur, dst, xoth, need_ctx):
        fw, nc = self.fw, self.nc
        LT, T = self.LT, self.T
        TK = T + LT
        mv = self.mod_phase(0)
        w_in = self.inp("l0_w_in", [D, 3136])
        w_qb = self.inp("l0_w_qb", [512, 3072])
        w_kvb = self.inp("l0_w_kvb", [512, 4096])
        qng = self.load_vec("l0_q_norm", 4)
        kvng = self.load_vec("l0_kv_norm", 4)
        Ct = self.inp("rope64C", [128, T + LT])
        St = self.inp("rope64S", [128, T + LT])
        QN = self.scratch("QN0", [2048, T], BF16)
        QR = self.scratch("QR0", [1024, T], BF16)
        KN = self.scratch("KN0", [2048, TK], BF16)
        KR = self.scratch("KR0", [64, TK], BF16)
        V = self.scratch("V0", [TK, 2048], BF16)
        GT = self.scratch("GT0", [2048, T], BF16)
        OGT = self.scratch("OGT0", [2048, T], BF16)
        scale = 192 ** -0.5
        own_groups = self.groups()
        oth_groups = [[("o", s, n)] for (s, n) in tok_blocks(LT, 2048)]
        for grp in own_groups + oth_groups:
            own = grp[0][0] != "o"
            with contextlib.ExitStack() as ph:
                ntok = sum(n for (_, _, n) in grp)
                nblk = (ntok + 511) // 512
                qcT = fw.sb(ph, [128, 4, ntok], BF16, "qcT") if own else None
                kvcT = fw.sb(ph, [128, 4, ntok], BF16, "kvcT")
                qcb = [Buf() for _ in range(nblk)]
                kvb_ = [Buf() for _ in range(nblk)]
                with contextlib.ExitStack() as phh:
                    hT, hb, ntok, segs = self.make_hT(phh, grp, cur, mv, xoth=xoth)

                    def tkdst(scr, r0, M):
                        out = []
                        for (kind, t0, n, off) in segs:
                            base = {"l": t0, "c": LT, "o": T + t0}[kind]
                            out.append((scr[r0:r0 + M, base:base + n], off, off + n))
                        return out

                    with contextlib.ExitStack() as ph2:
                        jobs = []
                        if own:
                            for c in range(4):
                                jobs.append(dict(segs=[(c * 128, 128)], mode="plain", dst=("sb", qcT, c, qcb)))
                        for c in range(4):
                            jobs.append(dict(segs=[(512 + c * 128, 128)], mode="plain", dst=("sb", kvcT, c, kvb_)))
                        jobs.append(dict(segs=[(1024, 64)], mode="rope", swap=32, dst=("dram", tkdst(KR.ap(), 0, 64))))
                        if own:
                            for c in range(16):
                                jobs.append(dict(segs=[(1088 + c * 128, 128)], mode="silu", dst=("dram", self.dst_rows(GT.ap(), c * 128, 128, segs))))
                        self.proj_fm(ph2, hT, hb, KC, ntok, w_in, jobs, rope=self.rope_aps(Ct, St, segs))
                        fw.barrier()
                with contextlib.ExitStack() as ph2:
                    sq = [fw.sb(ph2, [128, 512], BF16, "lsq") for _ in range(2)]
                    sqb = [Buf(), Buf()]
                    rstd = fw.sb(ph2, [128, 512], F32, "lrstd")
                    rb = Buf()
                    tmp = [fw.sb(ph2, [128, 512], F32, "ltmp") for _ in range(2)]
                    tmb = [Buf(), Buf()]
                    todo = ([(qcT, qcb, qng)] if own else []) + [(kvcT, kvb_, kvng)]
                    cnt = 0
                    for (tT, tb_, gv) in todo:
                        for (t0, m) in tok_blocks(ntok):
                            bb = tb_[t0 // 512]
                            pi = 6 + (cnt % 2)
                            cnt += 1
                            for k in range(4):
                                q = k % 2
                                fw.op("act", lambda e: e.activation(out=sq[q][:, :m], in_=tT[:, k, t0:t0 + m], func=AF.Square), reads=[bb], writes=[sqb[q]])
                                fw.op("pe", lambda e: e.matmul(self.ps[pi][:, :m], self.ones_bf[:], sq[q][:, :m], start=(k == 0), stop=(k == 3)),
                                      reads=[sqb[q]], writes=[self.psb[pi]])
                            fw.op("act", lambda e: e.activation(out=rstd[:, :m], in_=self.ps[pi][:, :m], func=AF.Sqrt, scale=1.0 / 512, bias=self.eps_t[:, 0:1]),
                                  reads=[self.psb[pi]], writes=[rb])
                            fw.op("dve", lambda e: e.reciprocal(rstd[:, :m], rstd[:, :m]), reads=[rb], writes=[rb])
                            for k in range(4):
                                q = k % 2
                                fw.op("dve", lambda e: e.tensor_tensor(tmp[q][:, :m], tT[:, k, t0:t0 + m], rstd[:, :m], ALU.mult), reads=[bb, rb], writes=[tmb[q]])
                                fw.op("act", lambda e: e.activation(out=tT[:, k, t0:t0 + m], in_=tmp[q][:, :m], func=AF.Copy, scale=gv[:, k:k + 1]),
                                      reads=[tmb[q]], writes=[bb])
                    fw.barrier()
                with contextlib.ExitStack() as ph2:
                    if own:
                        jobs = []
                        for h in range(16):
                            jobs.append(dict(segs=[(192 * h, 128)], mode="plain", dst=("dram", self.dst_rows(QN.ap(), 128 * h, 128, segs))))
                        for j in range(8):
                            jobs.append(dict(segs=[(192 * (2 * j) + 128, 64), (192 * (2 * j + 1) + 128, 64)], mode="rope", swap=32,
                                             dst=("dram", self.dst_rows(QR.ap(), 128 * j, 128, segs))))
                        self.proj_fm(ph2, qcT, qcb, 4, ntok, w_qb, jobs, rope=self.rope_aps(Ct, St, segs))
                        fw.barrier()
                with contextlib.ExitStack() as ph2:
                    jobs = []
                    for h in range(16):
                        jobs.append(dict(segs=[(256 * h, 128)], mode="plain", dst=("dram", tkdst(KN.ap(), 128 * h, 128))))
                    self.proj_fm(ph2, kvcT, kvb_, 4, ntok, w_kvb, jobs)
                    fw.barrier()
                with contextlib.ExitStack() as ph2:
                    for (kind, t0, n, off) in segs:
                        base = {"l": t0, "c": LT, "o": T + t0}[kind]
                        sub = [([(256 * (2 * j) + 128, 128), (256 * (2 * j + 1) + 128, 128)], [V.ap()[:, j * 256:(j + 1) * 256]]) for j in range(8)]
                        self.proj_tm(ph2, _View(kvcT, off), kvb_[off // 512:], 4, n, w_kvb, sub, base)
                    fw.barrier()
        NKB = TK // 128
        cb0 = LT // 128
        with contextlib.ExitStack() as ph:
            Krt = fw.sb(ph, [64, TK], BF16, "KR")
            Krb = Buf()
            fw.dma(Krt[:], KR.ap()[:, :], writes=[Krb])
            Kt = [fw.sb(ph, [128, TK], BF16, "K0") for _ in range(2)]
            Kb = [Buf(), Buf()]
            Vt = [fw.sb(ph, [128, NKB, 128], BF16, "V0") for _ in range(2)]
            Vb = [Buf(), Buf()]
            Qr = [fw.sb(ph, [64, 512], BF16, "Qr") for _ in range(2)]
            Qrb = [Buf(), Buf()]
            sbf = self.simple_bufs(ph, 128)
            Qt, Qb, Gt, Gb, ptr, ptb = (sbf[k] for k in ("Qt", "Qb", "Gt", "Gb", "ptr", "ptb"))

            def load_kv(h):
                s = h % 2
                half = TK // 2
                for a in range(2):
                    fw.dma(Kt[s][:, a * half:(a + 1) * half], KN.ap()[128 * h:128 * h + 128, a * half:(a + 1) * half], writes=[Kb[s]])
                    fw.dma(Vt[s][:, a * (NKB // 2):(a + 1) * (NKB // 2), :],
                           V.ap()[a * half:(a + 1) * half, 128 * h:128 * h + 128].rearrange("(b p) c -> p b c", p=128), writes=[Vb[s]])

            qsegs = [("l", s_, n_) for (s_, n_) in tok_blocks(LT)] + ([("c", LT, CT)] if need_ctx else [])
            items = [(h, qs_) for h in range(16) for qs_ in qsegs]

            def load_q(qi):
                h, (kind, tb, N) = items[qi]
                s = qi % 2
                fw.dma(Qt[s][:, :N], QN.ap()[128 * h:128 * h + 128, tb:tb + N], writes=[Qb[s]])
                fw.dma(Qr[s][:, :N], QR.ap()[64 * h:64 * h + 64, tb:tb + N], writes=[Qrb[s]])
                fw.dma(Gt[s][:, :N], GT.ap()[128 * h:128 * h + 128, tb:tb + N], writes=[Gb[s]])

            load_kv(0)
            load_q(0)
            for qi, (h, (kind, tb, N)) in enumerate(items):
                ks = h % 2
                s = qi % 2
                if qi % len(qsegs) == 0 and h + 1 < 16:
                    load_kv(h + 1)
                if qi + 1 < len(items):
                    load_q(qi + 1)
                kbl = list(range(NKB)) if kind == "l" else [cb0, cb0 + 1]
                kblocks = []
                for blk in kbl:
                    kblocks.append(dict(s=[(Kt[ks][:, blk * 128:(blk + 1) * 128], Qt[s][:, :N], [Kb[ks], Qb[s]]),
                                           (Krt[:, blk * 128:(blk + 1) * 128], Qr[s][:, :N], [Krb, Qrb[s]])],
                                        v=[Vt[ks][:, blk, :], self.ones_bf[:]], vdeps=[Vb[ks]]))
                a0 = 2 + 2 * (qi % 3)
                self.attn_stream(N, kblocks, scale, [(a0, 128), (a0 + 1, 128)], ptr, ptb, [0, 1])
                self.simple_finalize(sbf, qi, 128, N, a0, a0 + 1, None, Gt[s][:, :N], Gb[s], OGT.ap()[128 * h:128 * h + 128, tb:tb + N])
            fw.barrier()
        self.out_proj(cur, dst, "l0_w_out", OGT, mv, need_ctx)


class _View:
    def __init__(self, t, off):
        self.t = t
        self.off = off

    def __getitem__(self, idx):
        p, k, sl = idx
        return self.t[p, k, self.off + sl.start:self.off + sl.stop]


def _vl(v):
    v = np.asarray(v, np.float32)
    n = v.shape[0] // 128
    return np.ascontiguousarray(v.reshape(n, 128).T)


def _rope_tables(pos, d_rot, n_ctx, pos_oth=None):
    d_axis = d_rot // 2
    inv = (10000.0 ** (-np.arange(0, d_axis, 2, dtype=np.float32) / d_axis)).astype(np.float32)

    def tab(p):
        row = (p // GRID_W).astype(np.float32)
        col = (p % GRID_W).astype(np.float32)
        ang = np.concatenate([row[:, None] * inv, col[:, None] * inv], axis=-1).astype(np.float32)
        c = np.cos(ang).T
        s = np.sin(ang).T
        return np.concatenate([c, c], 0), np.concatenate([-s, s], 0)

    parts_c, parts_s = [], []
    c, s = tab(pos)
    parts_c.append(c)
    parts_s.append(s)
    parts_c.append(np.ones((d_rot, n_ctx), np.float32))
    parts_s.append(np.zeros((d_rot, n_ctx), np.float32))
    if pos_oth is not None:
        c, s = tab(pos_oth)
        parts_c.append(c)
        parts_s.append(s)
    C = np.concatenate(parts_c, 1)
    S = np.concatenate(parts_s, 1)
    rep = 128 // d_rot
    return (np.ascontiguousarray(np.tile(C, (rep, 1)), np.float32), np.ascontiguousarray(np.tile(S, (rep, 1)), np.float32))


def _prep(inputs, prog, b, r):
    LT = prog.LT
    m = {}
    xT = np.asarray(inputs["x"][b], np.float32).T
    pos_own = np.arange(r * LT, (r + 1) * LT)
    pos_oth = np.arange((1 - r) * LT, (2 - r) * LT)
    for name, (shape, dt) in prog.inputs.items():
        if name == "ident":
            v = np.eye(128, dtype=np.float32)
        elif name == "xT_own":
            v = xT[:, r * LT:(r + 1) * LT]
        elif name == "xT_oth":
            v = xT[:, (1 - r) * LT:(2 - r) * LT]
        elif name == "ctxT":
            v = np.asarray(inputs["ctx"][b], np.float32).T
        elif name == "cvec":
            v = np.stack([_vl(inputs["c"][b]), _vl(inputs["c_ctx"])], axis=-1)
        elif name == "l3_lam":
            v = np.stack([np.asarray(inputs[k], np.float32) for k in ("l3_lam_q1", "l3_lam_k1", "l3_lam_q2", "l3_lam_k2")], axis=1)
        elif name == "rope128C":
            v = _rope_tables(pos_own, 128, CT)[0]
        elif name == "rope128S":
            v = _rope_tables(pos_own, 128, CT)[1]
        elif name == "rope64C":
            v = _rope_tables(pos_own, 64, CT, pos_oth)[0]
        elif name == "rope64S":
            v = _rope_tables(pos_own, 64, CT, pos_oth)[1]
        elif name in inputs and tuple(np.shape(inputs[name])) == shape:
            v = inputs[name]
        elif name in inputs and np.ndim(inputs[name]) == 1:
            v = _vl(inputs[name])
        else:
            v = _special(inputs, prog, name, b, r)
        if dt == BF16:
            v = np.ascontiguousarray(np.asarray(v).astype(ml_dtypes.bfloat16))
        else:
            v = np.ascontiguousarray(np.asarray(v, np.float32))
        assert v.shape == shape, (name, v.shape, shape)
        m[name] = v
    return m


def _special(inputs, prog, name, b, r):
    LT = prog.LT
    if name == "l1_sink_rep":
        return np.tile(np.asarray(inputs["l1_sink"], np.float32)[None, :], (128, 1))
    if name == "swa_mask":
        kk = np.arange(128)[:, None]
        qq = np.arange(512)[None, :]
        m = np.full((128, 8, 512), NEG, np.float32)
        for jo in range(6):
            ok = np.abs(qq - (128 * jo - 128 + kk)) <= 128
            m[:, jo, :] = np.where(ok, 0.0, NEG)
        if r == 1:
            m[:, 6, :] = m[:, 0, :]
        if r == 0:
            m[:, 7, :] = m[:, 5, :]
        return m
    if name == "na_bias":
        rpb = np.asarray(inputs["l2_rpb"], np.float32)
        rows_half = LT // GRID_W
        rows = 2 * rows_half
        nR = LT // 512
        kk = np.arange(128)[:, None]
        qq = np.arange(512)[None, :]
        out = np.empty((32, 128, 24, 512), ml_dtypes.bfloat16)
        for v, R in enumerate((0, min(1, nR - 1), nR - 1)):
            for jo in range(8):
                qr = r * rows_half + 8 * R + qq // GRID_W
                qc = qq % GRID_W
                kr = r * rows_half + 8 * R - 4 + 2 * jo + kk // GRID_W
                kc = kk % GRID_W
                rs = np.clip(qr - 4, 0, rows - 8)
                cs = np.clip(qc - 8, 0, GRID_W - 16)
                ok = (kr >= 0) & (kr < rows) & (kr >= rs) & (kr < rs + 8) & (kc >= cs) & (kc < cs + 16)
                dr = np.clip(kr - qr + 7, 0, 14)
                dc = np.clip(kc - qc, -15, 15) + 15
                dr, dc, ok = np.broadcast_arrays(dr, dc, ok)
                t = rpb[:, dr, dc]
                t = np.where(ok[None], t, NEG)
                out[:, :, v * 8 + jo, :] = t.astype(ml_dtypes.bfloat16)
        return out
    raise KeyError(name)


_CACHE = {}


def run(inputs, layers=(0, 1, 2, 3), final=True, need_ctx_last=False, dbg=()):
    x = np.asarray(inputs["x"])
    B, SEQ = x.shape[0], x.shape[1]
    LT = SEQ // 2
    key = (LT, tuple(layers), final, need_ctx_last, B, tuple(dbg))
    if key not in _CACHE:
        prog = Prog(LT, list(layers), final, need_ctx_last, B)
        prog.dbg_names = list(dbg)
        prog.build()
        _CACHE[key] = prog
    prog = _CACHE[key]
    in_maps = []
    for core in range(2 * B):
        in_maps.append(_prep(inputs, prog, core // 2, core % 2))
    res = run_bass_kernel_spmd(prog.nc, in_maps, core_ids=list(range(2 * B)))
    if dbg:
        global DBG
        DBG = [{n: np.asarray(res.results[core][n]) for n in prog.dbg_out} for core in range(2 * B)]
    out = np.empty((B, SEQ, D), np.float32)
    for core in range(2 * B):
        b, r = core // 2, core % 2
        out[b, r * LT:(r + 1) * LT, :] = res.results[core]["outT"].T
    return out


def kernel(**inputs):
    return run(inputs)
```

```python
import contextlib
import math
import numpy as np
import ml_dtypes
import concourse.bass as bass
import concourse.mybir as mybir
from concourse.bass_utils import run_bass_kernel_spmd

F32 = mybir.dt.float32
BF16 = mybir.dt.bfloat16
AF = mybir.ActivationFunctionType
ALU = mybir.AluOpType

D = 2048
KC = 16
CT = 256
GRID_W = 64
EPS = 1e-6
NEG = -30000.0


class Buf:
    __slots__ = ("w", "r", "name")

    def __init__(self, name=""):
        self.w = None
        self.r = {}
        self.name = name


class FW:
    def __init__(self, nc, es, n_dma=24):
        self.nc = nc
        self.engs = dict(pe=nc.tensor, act=nc.scalar, dve=nc.vector, pool=nc.gpsimd, sp=nc.sync)
        self.sem = {k: es.enter_context(nc.semaphore("s_" + k)) for k in self.engs}
        self.cnt = {k: 0 for k in self.engs}
        self.seen = {k: {} for k in self.engs}
        self.dsem = [es.enter_context(nc.semaphore(f"d{i}")) for i in range(n_dma)]
        self.dtot = [0] * n_dma
        self.drr = 0
        self.csem = es.enter_context(nc.semaphore("ccs"))
        self.ccnt = 0
        self.pend = {k: [] for k in self.engs}
        self.uid = 0
        self.groups = [[0, 1], [2, 3], [4, 5], [6, 7]]
        self.nodrain = False

    def _semof(self, ev):
        kind, key, _ = ev
        if kind == "e":
            return self.sem[key]
        if kind == "d":
            return self.dsem[key]
        return self.csem

    def _wait(self, ek, ev):
        if ev is None:
            return
        kind, key, val = ev
        if kind == "p":
            assert key == ek, f"wait on pending event of {key} from {ek}"
            return
        if kind == "e" and key == ek:
            if ek == "pe" or self.nodrain or self.cnt[ek] - val >= 3:
                return
        if self.seen[ek].get((kind, key), 0) >= val:
            return
        self.engs[ek].wait_ge(self._semof(ev), val)
        self.seen[ek][(kind, key)] = val

    def _deps(self, ek, reads, writes):
        for b in reads:
            self._wait(ek, b.w)
        for b in writes:
            self._wait(ek, b.w)
            for ev in list(b.r.values()):
                self._wait(ek, ev)

    def op(self, ek, fn, reads=(), writes=(), inc=True, nodrain=False):
        self.nodrain = nodrain
        self._deps(ek, reads, writes)
        self.nodrain = False
        ins = fn(self.engs[ek])
        if not inc:
            self.pend[ek].append((tuple(reads), tuple(writes)))
            pe = ("p", ek, 0)
            for b in reads:
                b.r[ek] = pe
            for b in writes:
                b.w = pe
                b.r = {}
            return ins
        self.cnt[ek] += 1
        ins.then_inc(self.sem[ek], 1)
        ev = ("e", ek, self.cnt[ek])
        for (r, w) in self.pend[ek]:
            for b in r:
                if b.r.get(ek) == ("p", ek, 0):
                    b.r[ek] = ev
            for b in w:
                if b.w == ("p", ek, 0):
                    b.w = ev
        self.pend[ek] = []
        for b in reads:
            b.r[ek] = ev
        for b in writes:
            b.w = ev
            b.r = {}
        return ins

    def dma(self, out, in_, reads=(), writes=(), q="sp"):
        i = self.drr
        self.drr = (i + 1) % len(self.dsem)
        if self.dtot[i]:
            self._wait(q, ("d", i, self.dtot[i]))
        self._deps(q, reads, writes)
        ins = self.engs[q].dma_start(out=out, in_=in_)
        self.dtot[i] += 16
        ins.then_inc(self.dsem[i], 16)
        ev = ("d", i, self.dtot[i])
        for b in reads:
            b.r[("d", i)] = ev
        for b in writes:
            b.w = ev
            b.r = {}

    def collective(self, in_ap, out_ap):
        import os
        if os.environ.get("NOCC"):
            n0 = in_ap.shape[0]
            self.dma(out_ap[0:n0], in_ap)
            self.dma(out_ap[n0:2 * n0], in_ap)
            return
        q = "pool"
        if self.ccnt:
            self._wait(q, ("c", 0, self.ccnt))
        ins = self.nc.gpsimd.collective_compute(
            "AllGather", ALU.bypass, replica_groups=self.groups,
            ins=[in_ap], outs=[out_ap])
        self.ccnt += 1
        ins.then_inc(self.csem)

    def barrier(self):
        for ek in self.engs:
            assert not self.pend[ek], ek
        evs = [("e", k, self.cnt[k]) for k in self.engs if self.cnt[k]]
        evs += [("d", i, t) for i, t in enumerate(self.dtot) if t]
        if self.ccnt:
            evs.append(("c", 0, self.ccnt))
        for ek in self.engs:
            for ev in evs:
                self._wait(ek, ev)

    def sb(self, ph, shape, dtype, name):
        self.uid += 1
        return ph.enter_context(self.nc.sbuf_tensor(f"{name}_{self.uid}", list(shape), dtype))


def tok_blocks(n, bs=512):
    out = []
    s = 0
    while s < n:
        out.append((s, min(bs, n - s)))
        s += bs
    return out


class Prog:
    def __init__(self, LT, layers, final, need_ctx_last, B=4):
        self.B = B
        self.LT = LT
        self.T = LT + CT
        self.layers = layers
        self.final = final
        self.need_ctx_last = need_ctx_last
        self.nc = bass.Bass("TRN2", target_bir_lowering=False)
        self.inputs = {}
        self.uid = 0
        self.scr = {}
        self.inp_aps = {}
        self.vec_cache = {}
        self.dbg_names = []
        self.dbg_out = []

    def inp(self, name, shape, dtype=F32):
        if name in self.inp_aps:
            assert self.inputs[name][0] == tuple(shape), name
            return self.inp_aps[name]
        t = self.nc.dram_tensor(name, list(shape), dtype, kind="ExternalInput")
        self.inputs[name] = (tuple(shape), dtype)
        self.inp_aps[name] = t.ap()
        return self.inp_aps[name]

    def scratch(self, name, shape, dtype):
        t = self.nc.dram_tensor(name, list(shape), dtype)
        self.scr[name] = t
        return t

    def dump_dbg(self):
        fw = self.fw
        for name in self.dbg_names:
            t = self.scr[name]
            o = self.nc.dram_tensor("dbg_" + name, list(t.shape), t.dtype, kind="ExternalOutput")
            self.dbg_out.append("dbg_" + name)
            rows = t.shape[0]
            for r0 in range(0, rows, 128):
                r1 = min(rows, r0 + 128)
                fw.dma(o.ap()[r0:r1, :], t.ap()[r0:r1, :])
        fw.barrier()

    def build(self):
        nc = self.nc
        LT, T = self.LT, self.T
        es = contextlib.ExitStack()
        with es:
            fw = self.fw = FW(nc, es)
            fw.groups = [[2 * i, 2 * i + 1] for i in range(self.B)]
            self._persist = es
            self.ps = [es.enter_context(nc.psum_tensor(f"ps{i}", [128, 512], F32)) for i in range(8)]
            self.psb = [Buf(f"ps{i}") for i in range(8)]
            self.ones_bf = fw.sb(es, [128, 128], BF16, "ones")
            self.ident_bf = fw.sb(es, [128, 128], BF16, "ident")
            self.cb = Buf("const")
            ident_in = self.inp("ident", [128, 128], F32)
            idf = self.ident_f = fw.sb(es, [128, 128], F32, "identf")
            fw.dma(idf[:], ident_in[:, :], writes=[self.cb])
            fw.op("dve", lambda e: e.tensor_copy(self.ident_bf[:], idf[:]), reads=[self.cb], writes=[self.cb])
            fw.op("dve", lambda e: e.memset(self.ones_bf[:], 1.0), writes=[self.cb])
            self.ones_f = fw.sb(es, [128, 128], F32, "onesf32")
            fw.op("dve", lambda e: e.memset(self.ones_f[:], 1.0), writes=[self.cb])
            self.eps_t = fw.sb(es, [128, 1], F32, "epst")
            fw.op("dve", lambda e: e.memset(self.eps_t[:], EPS), writes=[self.cb])
            xT_own = self.inp("xT_own", [D, LT])
            xT_oth = self.inp("xT_oth", [D, LT]) if 0 in self.layers else None
            ctxT = self.inp("ctxT", [D, CT])
            cvec = self.inp("cvec", [128, KC, 2])
            self.silu_c = fw.sb(es, [128, KC, 2], F32, "siluc")
            cv = fw.sb(es, [128, KC, 2], F32, "cv")
            fw.dma(cv[:], cvec[:, :, :], writes=[self.cb])
            fw.op("act", lambda e: e.activation(out=self.silu_c[:], in_=cv[:], func=AF.Silu), reads=[self.cb], writes=[self.cb])
            fw.barrier()
            XA = self.scratch("XA", [D, T], F32)
            XB = self.scratch("XB", [D, T], F32)
            cur = (xT_own, ctxT)
            nxt = [XA, XB]
            outT = self.nc.dram_tensor("outT", [D, LT], F32, kind="ExternalOutput").ap()
            for li, L in enumerate(self.layers):
                dst = nxt[li % 2]
                dst_pair = (dst.ap()[:, 0:LT], dst.ap()[:, LT:T])
                last = (L == 3) and not self.need_ctx_last
                getattr(self, f"layer{L}")(cur, dst_pair, xT_oth, need_ctx=not last)
                cur = dst_pair
            if self.dbg_names:
                self.dump_dbg()
            if self.final:
                fng = self.load_vec("final_norm", KC)
                with contextlib.ExitStack() as ph:
                    nb = self.norm_bufs(ph, with_out=True)
                    self.norm_phase(nb, cur[0], 0, LT, fng, None, out_dram=outT)
                    fw.barrier()
            else:
                with contextlib.ExitStack() as ph:
                    t = fw.sb(ph, [128, KC, 512], F32, "dump")
                    tb = Buf()
                    for (s, n) in tok_blocks(LT):
                        fw.dma(t[:, :, :n], cur[0][:, s:s + n].rearrange("(k p) t -> p k t", p=128), writes=[tb])
                        fw.dma(outT[:, s:s + n].rearrange("(k p) t -> p k t", p=128), t[:, :, :n], reads=[tb])
                    fw.barrier()
            fw.barrier()
        return nc

    def load_vec(self, name, ncol):
        fw = self.fw
        if name in self.vec_cache:
            return self.vec_cache[name]
        ap = self.inp(name, [128, ncol])
        t = self._persist.enter_context(self.nc.sbuf_tensor(f"v_{name}", [128, ncol], F32))
        b = Buf(name)
        fw.dma(t[:], ap[:, :], writes=[b])
        fw.barrier()
        self.vec_cache[name] = t
        return t

    def mod_phase(self, L):
        fw, nc = self.fw, self.nc
        mod_w = self.inp(f"l{L}_mod_w", [D, 3 * D])
        mod_b = self.load_vec(f"l{L}_mod_b", 48)
        ng = self.load_vec(f"l{L}_norm", KC)
        pers = self._persist
        modT = pers.enter_context(nc.sbuf_tensor(f"modT{L}", [128, 48, 2], F32))
        res = {}
        for nm in ("a_l", "b_l", "g_l", "a_c", "b_c", "g_c"):
            res[nm] = pers.enter_context(nc.sbuf_tensor(f"{nm}{L}", [128, KC], F32))
        mb = Buf("modT")
        with contextlib.ExitStack() as ph:
            wst = [fw.sb(ph, [128, KC, 512], F32, "mwst") for _ in range(2)]
            wb = [Buf("mw0"), Buf("mw1")]
            mrow = fw.sb(ph, [2, 3 * D], F32, "mrow")
            mrb = Buf("mrow")
            for fc in range(12):
                s = fc % 2
                fw.dma(wst[s][:], mod_w[:, fc * 512:(fc + 1) * 512].rearrange("(k p) c -> p k c", p=128), writes=[wb[s]])
                pi = fc % 2
                for k in range(KC):
                    fw.op("pe", lambda e: e.matmul(self.ps[pi][0:2, :], self.silu_c[:, k, :], wst[s][:, k, :], start=(k == 0), stop=(k == KC - 1)),
                          reads=[wb[s]], writes=[self.psb[pi]], inc=(k == KC - 1))
                fw.op("dve", lambda e: e.tensor_copy(mrow[0:2, fc * 512:(fc + 1) * 512], self.ps[pi][0:2, :]), reads=[self.psb[pi]], writes=[mrb])
            for fc in range(48):
                fw.op("pe", lambda e: e.matmul(self.ps[2][:, 2 * fc:2 * fc + 2], mrow[0:2, fc * 128:(fc + 1) * 128], self.ident_f[0:2, 0:2], start=True, stop=True),
                      reads=[mrb], writes=[self.psb[2]], inc=(fc == 47))
            pv_ = self.ps[2][:, 0:96].rearrange("p (f j) -> p f j", j=2)
            for j in range(2):
                fw.op("dve", lambda e: e.tensor_tensor(modT[:, :, j], pv_[:, :, j], mod_b[:], ALU.add), reads=[self.psb[2]], writes=[mb])
            for j, sfx in ((0, "_l"), (1, "_c")):
                fw.op("dve", lambda e, j=j, sfx=sfx: e.scalar_tensor_tensor(res["a" + sfx][:], modT[:, 16:32, j], 1.0, ng[:], ALU.add, ALU.mult),
                      reads=[mb], writes=[mb])
                fw.op("dve", lambda e, j=j, sfx=sfx: e.tensor_copy(res["b" + sfx][:], modT[:, 0:16, j]), reads=[mb], writes=[mb])
                fw.op("dve", lambda e, j=j, sfx=sfx: e.tensor_copy(res["g" + sfx][:], modT[:, 32:48, j]), reads=[mb], writes=[mb])
            fw.barrier()
        return res

    def norm_bufs(self, ph, with_out=False):
        fw = self.fw
        nb = {}
        nb["xt"] = [fw.sb(ph, [128, KC, 512], F32, "xt") for _ in range(2)]
        nb["xb"] = [Buf("xt0"), Buf("xt1")]
        nb["sq"] = [fw.sb(ph, [128, 512], BF16, "sq") for _ in range(2)]
        nb["sqb"] = [Buf(), Buf()]
        nb["rstd"] = fw.sb(ph, [128, 512], F32, "rstd")
        nb["rb"] = Buf("rstd")
        nb["tmp"] = [fw.sb(ph, [128, 512], F32, "ntmp") for _ in range(2)]
        nb["tmb"] = [Buf(), Buf()]
        if with_out:
            nb["ot"] = [fw.sb(ph, [128, KC, 512], F32, "nout")]
            nb["otb"] = [Buf()]
        nb["cnt"] = 0
        return nb

    def norm_phase(self, nb, src, t0, n, a, b, hT=None, hoff=0, hbufs=None, out_dram=None):
        fw = self.fw
        xt, xb, sq, sqb, rstd, rb, tmp, tmb = (nb[k] for k in ("xt", "xb", "sq", "sqb", "rstd", "rb", "tmp", "tmb"))
        if out_dram is not None:
            ot, otb = nb["ot"], nb["otb"]
        blks = tok_blocks(n)
        for (s, m) in blks:
            bi = nb["cnt"]
            nb["cnt"] += 1
            sl = bi % 2
            fw.dma(xt[sl][:, :, :m], src[:, t0 + s:t0 + s + m].rearrange("(k p) t -> p k t", p=128), writes=[xb[sl]])
            pi = 6 + (bi % 2)
            for k in range(KC):
                q = k % 2
                fw.op("act", lambda e, k=k, q=q: e.activation(out=sq[q][:, :m], in_=xt[sl][:, k, :m], func=AF.Square),
                      reads=[xb[sl]], writes=[sqb[q]])
                fw.op("pe", lambda e, k=k, q=q: e.matmul(self.ps[pi][:, :m], self.ones_bf[:], sq[q][:, :m], start=(k == 0), stop=(k == KC - 1)),
                      reads=[sqb[q]], writes=[self.psb[pi]])
            fw.op("act", lambda e: e.activation(out=rstd[:, :m], in_=self.ps[pi][:, :m], func=AF.Sqrt, scale=1.0 / D, bias=self.eps_t[:, 0:1]),
                  reads=[self.psb[pi]], writes=[rb])
            fw.op("dve", lambda e: e.reciprocal(rstd[:, :m], rstd[:, :m]), reads=[rb], writes=[rb])
            for k in range(KC):
                q = k % 2
                fw.op("dve", lambda e, k=k, q=q: e.tensor_tensor(tmp[q][:, :m], xt[sl][:, k, :m], rstd[:, :m], ALU.mult),
                      reads=[xb[sl], rb], writes=[tmb[q]])
                if out_dram is None:
                    hb = hbufs[(hoff + s) // 512]
                    fw.op("act", lambda e, k=k, q=q: e.activation(out=hT[:, k, hoff + s:hoff + s + m], in_=tmp[q][:, :m], func=AF.Identity,
                                                                   scale=a[:, k:k + 1], bias=b[:, k:k + 1]),
                          reads=[tmb[q]], writes=[hb])
                else:
                    fw.op("act", lambda e, k=k, q=q: e.activation(out=ot[0][:, k, :m], in_=tmp[q][:, :m], func=AF.Copy, scale=a[:, k:k + 1]),
                          reads=[tmb[q]], writes=[otb[0]])
            if out_dram is not None:
                fw.dma(out_dram[:, t0 + s:t0 + s + m].rearrange("(k p) t -> p k t", p=128), ot[0][:, :, :m], reads=[otb[0]])

    def proj_fm(self, ph, actT, abufs, kc, ntok, w, jobs, rope=None):
        fw = self.fw
        wst = [fw.sb(ph, [128, kc, 128], F32, "wst") for _ in range(2)]
        wsb = [Buf(), Buf()]
        wbf = [fw.sb(ph, [128, kc, 128], BF16, "wbf") for _ in range(2)]
        wbb = [Buf(), Buf()]
        has_rope = any(j["mode"] == "rope" for j in jobs)
        has_res = any(j["mode"] == "resid" for j in jobs)
        if has_rope:
            wsw = [fw.sb(ph, [128, kc, 128], BF16, "wsw") for _ in range(2)]
            wwb = [Buf(), Buf()]
            Ct = fw.sb(ph, [128, ntok], F32, "ropeC")
            St = fw.sb(ph, [128, ntok], F32, "ropeS")
            rpb_ = Buf("rope")
            fw.dma(Ct[:], rope[0], writes=[rpb_])
            fw.dma(St[:], rope[1], writes=[rpb_])
            t1 = [fw.sb(ph, [128, 512], F32, "rt1") for _ in range(2)]
            t2 = [fw.sb(ph, [128, 512], F32, "rt2") for _ in range(2)]
            t1b = [Buf(), Buf()]
            t2b = [Buf(), Buf()]
        odt = F32 if has_res else BF16
        ot = [fw.sb(ph, [128, ntok], odt, "pot") for _ in range(2)]
        otb = [Buf(), Buf()]
        if has_res:
            xs = [fw.sb(ph, [128, ntok], F32, "pxs") for _ in range(2)]
            xsb = [Buf(), Buf()]
        blks = tok_blocks(ntok)
        evac_rr = 0
        def issue_loads(ji):
            job = jobs[ji]
            s = ji % 2
            M = sum(n for (_, n) in job["segs"])
            off = 0
            for (c0, n) in job["segs"]:
                fw.dma(wst[s][:, :, off:off + n], w[:, c0:c0 + n].rearrange("(k p) c -> p k c", p=128), writes=[wsb[s]])
                off += n
            fw.op("pool", lambda e: e.tensor_copy(wbf[s][:, :, :M], wst[s][:, :, :M]), reads=[wsb[s]], writes=[wbb[s]])
            if job["mode"] == "rope":
                hd = job["swap"]
                v_in = wst[s][:, :, :M].rearrange("p k (h two j) -> p k h two j", two=2, j=hd)
                v_out = wsw[s][:, :, :M].rearrange("p k (h two j) -> p k h two j", two=2, j=hd)
                fw.op("pool", lambda e: e.tensor_copy(v_out[:, :, :, 0, :], v_in[:, :, :, 1, :]), reads=[wsb[s]], writes=[wwb[s]])
                fw.op("pool", lambda e: e.tensor_copy(v_out[:, :, :, 1, :], v_in[:, :, :, 0, :]), reads=[wsb[s]], writes=[wwb[s]])

        issue_loads(0)
        for ji, job in enumerate(jobs):
            s = ji % 2
            M = sum(n for (_, n) in job["segs"])
            mode = job["mode"]
            if ji + 1 < len(jobs):
                issue_loads(ji + 1)
            sc = job.get("scale", 1.0)
            dst = job["dst"]
            for bi, (t0, m) in enumerate(blks):
                pa = (2 * bi) % 4 if mode == "rope" else (evac_rr % 4)
                pb = pa + 1
                for k in range(kc):
                    fw.op("pe", lambda e, k=k, pa=pa: e.matmul(self.ps[pa][:M, :m], wbf[s][:, k, :M], actT[:, k, t0:t0 + m],
                                                                start=(k == 0), stop=(k == kc - 1)),
                          reads=[wbb[s], abufs[t0 // 512]], writes=[self.psb[pa]], inc=(k == kc - 1))
                if mode == "rope":
                    for k in range(kc):
                        fw.op("pe", lambda e, k=k, pb=pb: e.matmul(self.ps[pb][:M, :m], wsw[s][:, k, :M], actT[:, k, t0:t0 + m],
                                                                    start=(k == 0), stop=(k == kc - 1)),
                              reads=[wwb[s], abufs[t0 // 512]], writes=[self.psb[pb]], inc=(k == kc - 1))
                if dst[0] == "sb":
                    o_ap = dst[1][:M, dst[2], t0:t0 + m]
                    o_b = dst[3][t0 // 512]
                else:
                    o_ap = ot[s][:M, t0:t0 + m]
                    o_b = otb[s]
                if mode == "plain":
                    if evac_rr % 2 == 0:
                        fw.op("act", lambda e, pa=pa, o_ap=o_ap: e.activation(out=o_ap, in_=self.ps[pa][:M, :m], func=AF.Copy, scale=float(sc)),
                              reads=[self.psb[pa]], writes=[o_b])
                    else:
                        fw.op("dve", lambda e, pa=pa, o_ap=o_ap: e.tensor_scalar(o_ap, self.ps[pa][:M, :m], float(sc), None, ALU.mult),
                              reads=[self.psb[pa]], writes=[o_b])
                elif mode == "silu":
                    fw.op("act", lambda e, pa=pa, o_ap=o_ap: e.activation(out=o_ap, in_=self.ps[pa][:M, :m], func=AF.Silu),
                          reads=[self.psb[pa]], writes=[o_b])
                elif mode == "rope":
                    q = bi % 2
                    fw.op("dve", lambda e, pa=pa, q=q: e.scalar_tensor_tensor(t1[q][:M, :m], self.ps[pa][:M, :m], float(sc), Ct[:M, t0:t0 + m], ALU.mult, ALU.mult),
                          reads=[self.psb[pa], rpb_], writes=[t1b[q]])
                    fw.op("dve", lambda e, pb=pb, q=q: e.scalar_tensor_tensor(t2[q][:M, :m], self.ps[pb][:M, :m], float(sc), St[:M, t0:t0 + m], ALU.mult, ALU.mult),
                          reads=[self.psb[pb], rpb_], writes=[t2b[q]])
                    fw.op("pool", lambda e, q=q, o_ap=o_ap: e.tensor_tensor(o_ap, t1[q][:M, :m], t2[q][:M, :m], ALU.add),
                          reads=[t1b[q], t2b[q]], writes=[o_b])
                elif mode == "resid":
                    gate = job["gate"]
                    for (g0, g1, gap) in gate:
                        a0, a1 = max(g0, t0), min(g1, t0 + m)
                        if a0 >= a1:
                            continue
                        fw.op("dve", lambda e, pa=pa, a0=a0, a1=a1, gap=gap: e.scalar_tensor_tensor(
                            ot[s][:M, a0:a1], self.ps[pa][:M, a0 - t0:a1 - t0], gap, xs[s][:M, a0:a1], ALU.mult, ALU.add),
                            reads=[self.psb[pa], xsb[s]], writes=[o_b])
                evac_rr += 1
            if dst[0] == "dram":
                for (dap, c0, c1) in dst[1]:
                    fw.dma(dap, ot[s][:M, c0:c1], reads=[otb[s]])

    def proj_tm(self, ph, actT, abufs, kc, ntok, w, segs_list, dst_tok0):
        fw = self.fw
        NC_ = 256
        wst = [fw.sb(ph, [128, kc, NC_], F32, "vwst") for _ in range(2)]
        wsb = [Buf(), Buf()]
        wbf = [fw.sb(ph, [128, kc, NC_], BF16, "vwbf") for _ in range(2)]
        wbb = [Buf(), Buf()]
        vt = [fw.sb(ph, [128, 4, NC_], BF16, "vt") for _ in range(2)]
        vtb = [Buf(), Buf()]
        ntile = (ntok + 127) // 128
        assert ntok % 128 == 0
        rr = 0
        def issue_loads(ji):
            segs, _ = segs_list[ji]
            s = ji % 2
            M = sum(n for (_, n) in segs)
            off = 0
            for (c0, n) in segs:
                fw.dma(wst[s][:, :, off:off + n], w[:, c0:c0 + n].rearrange("(k p) c -> p k c", p=128), writes=[wsb[s]])
                off += n
            fw.op("pool", lambda e: e.tensor_copy(wbf[s][:, :, :M], wst[s][:, :, :M]), reads=[wsb[s]], writes=[wbb[s]])

        issue_loads(0)
        for ji, (segs, dsts) in enumerate(segs_list):
            s = ji % 2
            M = sum(n for (_, n) in segs)
            if ji + 1 < len(segs_list):
                issue_loads(ji + 1)
            for g0 in range(0, ntile, 4):
                gn = min(4, ntile - g0)
                vs = (g0 // 4) % 2
                for ti in range(g0, g0 + gn):
                    pa = 4 + (rr % 2)
                    rr += 1
                    for k in range(kc):
                        fw.op("pe", lambda e, k=k, pa=pa, ti=ti: e.matmul(self.ps[pa][:, :M], actT[:, k, ti * 128:(ti + 1) * 128], wbf[s][:, k, :M],
                                                                          start=(k == 0), stop=(k == kc - 1)),
                              reads=[wbb[s], abufs[(ti * 128) // 512]], writes=[self.psb[pa]], inc=(k == kc - 1))
                    if rr % 2 == 0:
                        fw.op("act", lambda e, pa=pa, ti=ti: e.activation(out=vt[vs][:, ti - g0, :M], in_=self.ps[pa][:, :M], func=AF.Copy),
                              reads=[self.psb[pa]], writes=[vtb[vs]])
                    else:
                        fw.op("dve", lambda e, pa=pa, ti=ti: e.tensor_copy(vt[vs][:, ti - g0, :M], self.ps[pa][:, :M]),
                              reads=[self.psb[pa]], writes=[vtb[vs]])
                r0 = dst_tok0 + g0 * 128
                for di, dap in enumerate(dsts):
                    w_ = dap.shape[1]
                    fw.dma(dap[r0:r0 + gn * 128, :].rearrange("(a p) c -> p a c", p=128), vt[vs][:, :gn, di * w_:(di + 1) * w_], reads=[vtb[vs]])

    def attn_stream(self, N, kblocks, exp_scale, pv, ptr, ptb, sring, den=None):
        fw = self.fw
        nk = len(kblocks)

        def emit_s(i):
            kb = kblocks[i]
            si = sring[i % len(sring)]
            parts = list(kb["s"])
            if kb.get("mask") is not None:
                parts.append((self.ident_bf[:], kb["mask"][0], kb["mask"][1]))
            for j, (l, r, deps) in enumerate(parts):
                fw.op("pe", lambda e, l=l, r=r, j=j, si=si: e.matmul(self.ps[si][:, :N], l, r, start=(j == 0), stop=(j == len(parts) - 1)),
                      reads=list(deps), writes=[self.psb[si]], inc=(j == len(parts) - 1))
            pi = i % len(ptr)
            fw.op("act", lambda e, si=si, pi=pi: e.activation(out=ptr[pi][:, :N], in_=self.ps[si][:, :N], func=AF.Exp, scale=float(exp_scale)),
                  reads=[self.psb[si]], writes=[ptb[pi]])

        def emit_pv(i):
            kb = kblocks[i]
            pi = i % len(ptr)
            for j, ((pidx, M), l) in enumerate(zip(pv, kb["v"])):
                fw.op("pe", lambda e, pidx=pidx, M=M, l=l, pi=pi: e.matmul(self.ps[pidx][:M, :N], l, ptr[pi][:, :N], start=(i == 0), stop=(i == nk - 1)),
                      reads=[ptb[pi]] + list(kb["vdeps"]), writes=[self.psb[pidx]], inc=(j == len(pv) - 1))

        used = {}

        def emit_den(i):
            pi = i % len(ptr)
            ek = "pool" if i % 3 == 2 else "dve"
            acc, ab = den["acc"][ek]
            if ek not in used:
                used[ek] = True
                fw.op(ek, lambda e: e.tensor_copy(acc[:, :N], ptr[pi][:, :N]), reads=[ptb[pi]], writes=[ab], nodrain=True)
            else:
                fw.op(ek, lambda e: e.tensor_tensor(acc[:, :N], acc[:, :N], ptr[pi][:, :N], ALU.add), reads=[ptb[pi], ab], writes=[ab], nodrain=True)

        emit_s(0)
        for i in range(nk):
            if i + 1 < nk:
                emit_s(i + 1)
            emit_pv(i)
            if den is not None:
                emit_den(i)
        if den is not None:
            pidx, M = den["out"]
            eks = list(used)
            for j, ek in enumerate(eks):
                acc, ab = den["acc"][ek]
                fw.op("pe", lambda e: e.matmul(self.ps[pidx][:M, :N], self.ones_f[:, :M], acc[:, :N], start=(j == 0), stop=(j == len(eks) - 1)),
                      reads=[ab], writes=[self.psb[pidx]], inc=(j == len(eks) - 1))

    def groups(self):
        LT = self.LT
        if LT <= 2048:
            return [[("l", 0, LT), ("c", 0, CT)]]
        h = LT // 2
        return [[("l", 0, h)], [("l", h, LT - h), ("c", 0, CT)]]

    def make_hT(self, ph, grp, cur, mv, xoth=None):
        fw = self.fw
        ntok = sum(n for (_, _, n) in grp)
        hT = fw.sb(ph, [128, KC, ntok], BF16, "hT")
        hb = [Buf(f"h{i}") for i in range((ntok + 511) // 512)]
        off = 0
        segs = []
        with contextlib.ExitStack() as ph2:
            nb = self.norm_bufs(ph2)
            for (kind, t0, n) in grp:
                if kind == "l":
                    self.norm_phase(nb, cur[0], t0, n, mv["a_l"], mv["b_l"], hT=hT, hoff=off, hbufs=hb)
                elif kind == "o":
                    self.norm_phase(nb, xoth, t0, n, mv["a_l"], mv["b_l"], hT=hT, hoff=off, hbufs=hb)
                else:
                    self.norm_phase(nb, cur[1], t0, n, mv["a_c"], mv["b_c"], hT=hT, hoff=off, hbufs=hb)
                segs.append((kind, t0, n, off))
                off += n
            fw.barrier()
        return hT, hb, ntok, segs

    def dst_rows(self, scr, r0, M, segs, own_only=True):
        LT = self.LT
        out = []
        for (kind, t0, n, off) in segs:
            if kind == "l":
                out.append((scr[r0:r0 + M, t0:t0 + n], off, off + n))
            elif kind == "c":
                out.append((scr[r0:r0 + M, LT:LT + n], off, off + n))
        return out

    def rope_aps(self, Ct, St, segs):
        LT = self.LT
        kind, t0, n, off = segs[0]
        base = {"l": 0, "c": LT, "o": LT + CT}[kind] + t0
        tot = sum(s[2] for s in segs)
        return (Ct[:, base:base + tot], St[:, base:base + tot])

    def out_proj(self, cur, dst, w_out_name, OGT, mv, need_ctx):
        fw = self.fw
        LT = self.LT
        w_out = self.inp(w_out_name, [D, D])
        for grp in self.groups():
            grp = [g for g in grp if need_ctx or g[0] == "l"]
            with contextlib.ExitStack() as ph:
                ntok = sum(n for (_, _, n) in grp)
                og = fw.sb(ph, [128, KC, ntok], BF16, "ogT")
                ob = [Buf() for _ in range((ntok + 511) // 512)]
                segs = []
                off = 0
                for (kind, t0, n) in grp:
                    base = t0 if kind == "l" else LT
                    for (s, m) in tok_blocks(n):
                        fw.dma(og[:, :, off + s:off + s + m], OGT.ap()[:, base + s:base + s + m].rearrange("(k p) t -> p k t", p=128),
                               writes=[ob[(off + s) // 512]])
                    segs.append((kind, t0, n, off))
                    off += n
                jobs = []
                for c in range(KC):
                    gate = []
                    for (kind, t0, n, o) in segs:
                        gate.append((o, o + n, (mv["g_l"] if kind == "l" else mv["g_c"])[:, c:c + 1]))
                    srcs = []
                    dsts = []
                    for (kind, t0, n, o) in segs:
                        sap = cur[0][c * 128:(c + 1) * 128, t0:t0 + n] if kind == "l" else cur[1][c * 128:(c + 1) * 128, 0:n]
                        dap = dst[0][c * 128:(c + 1) * 128, t0:t0 + n] if kind == "l" else dst[1][c * 128:(c + 1) * 128, 0:n]
                        srcs.append((sap, o, o + n))
                        dsts.append((dap, o, o + n))
                    jobs.append(dict(segs=[(c * 128, 128)], mode="resid", dst=("dram", dsts), xsrcs=srcs, gate=gate))
                self.proj_fm_resid(ph, og, ob, ntok, w_out, jobs)
                fw.barrier()

    def proj_fm_resid(self, ph, actT, abufs, ntok, w, jobs):
        fw = self.fw
        for j in jobs:
            j["xsrc"] = None
        self._resid_jobs(ph, actT, abufs, ntok, w, jobs)

    def _resid_jobs(self, ph, actT, abufs, ntok, w, jobs):
        fw = self.fw
        kc = KC
        wst = [fw.sb(ph, [128, kc, 128], F32, "wst") for _ in range(2)]
        wsb = [Buf(), Buf()]
        wbf = [fw.sb(ph, [128, kc, 128], BF16, "wbf") for _ in range(2)]
        wbb = [Buf(), Buf()]
        ot = [fw.sb(ph, [128, ntok], F32, "pot") for _ in range(2)]
        otb = [Buf(), Buf()]
        xs = [fw.sb(ph, [128, ntok], F32, "pxs") for _ in range(2)]
        xsb = [Buf(), Buf()]
        blks = tok_blocks(ntok)
        rr = 0
        def issue_loads(ji):
            job = jobs[ji]
            s = ji % 2
            (c0, n) = job["segs"][0]
            fw.dma(wst[s][:, :, :n], w[:, c0:c0 + n].rearrange("(k p) c -> p k c", p=128), writes=[wsb[s]])
            fw.op("pool", lambda e: e.tensor_copy(wbf[s][:], wst[s][:]), reads=[wsb[s]], writes=[wbb[s]])
            for (sap, a0, a1) in job["xsrcs"]:
                fw.dma(xs[s][:, a0:a1], sap, writes=[xsb[s]])

        issue_loads(0)
        for ji, job in enumerate(jobs):
            s = ji % 2
            if ji + 1 < len(jobs):
                issue_loads(ji + 1)
            for bi, (t0, m) in enumerate(blks):
                pa = rr % 4
                rr += 1
                for k in range(kc):
                    fw.op("pe", lambda e, k=k, pa=pa: e.matmul(self.ps[pa][:, :m], wbf[s][:, k, :], actT[:, k, t0:t0 + m],
                                                                start=(k == 0), stop=(k == kc - 1)),
                          reads=[wbb[s], abufs[t0 // 512]], writes=[self.psb[pa]], inc=(k == kc - 1))
                for (g0, g1, gap) in job["gate"]:
                    a0, a1 = max(g0, t0), min(g1, t0 + m)
                    if a0 >= a1:
                        continue
                    fw.op("dve", lambda e, pa=pa, a0=a0, a1=a1, gap=gap: e.scalar_tensor_tensor(
                        ot[s][:, a0:a1], self.ps[pa][:, a0 - t0:a1 - t0], gap, xs[s][:, a0:a1], ALU.mult, ALU.add),
                        reads=[self.psb[pa], xsb[s]], writes=[otb[s]])
            for (dap, a0, a1) in job["dst"][1]:
                fw.dma(dap, ot[s][:, a0:a1], reads=[otb[s]])

    def layer3(self, cur, dst, xoth, need_ctx):
        fw, nc = self.fw, self.nc
        LT, T = self.LT, self.T
        L = 3
        mv = self.mod_phase(L)
        w_in = self.inp("l3_w_in", [D, 8192])
        lamv = self.load_vec("l3_lam", 4)
        subg = self.load_vec("l3_subln", 2)
        Ct = self.inp("rope128C", [128, T])
        St = self.inp("rope128S", [128, T])
        lam_init = 0.8 - 0.6 * math.exp(-0.3 * 3)
        QT = self.scratch("QT3", [2048, T], BF16)
        KTc = [self.scratch(f"KT3_{c}", [128, T], BF16) for c in range(16)]
        KTallc = [self.scratch(f"KT3all_{c}", [256, T], BF16) for c in range(16)]
        Vc = [self.scratch(f"V3_{c}", [T, 128], BF16) for c in range(16)]
        Vallc = [self.scratch(f"V3all_{c}", [2 * T, 128], BF16) for c in range(16)]
        GT = self.scratch("GT3", [2048, T], BF16)
        OGT = self.scratch("OGT3", [2048, T], BF16)
        scale = 128 ** -0.5
        pers = self._persist
        neglam = pers.enter_context(nc.sbuf_tensor("neglam", [128, 1], F32))
        subw = pers.enter_context(nc.sbuf_tensor("subw", [128, 2], F32))
        with contextlib.ExitStack() as ph:
            pr = fw.sb(ph, [128, 2], F32, "lpr")
            ones_f = fw.sb(ph, [128, 128], F32, "onesf")
            ex = fw.sb(ph, [128, 2], F32, "lex")
            lb = Buf()
            fw.op("dve", lambda e: e.memset(ones_f[:], 1.0), writes=[lb])
            v4 = lamv[:].rearrange("p (a b) -> p a b", b=2)
            fw.op("dve", lambda e: e.tensor_tensor(pr[:], v4[:, :, 0], v4[:, :, 1], ALU.mult), reads=[lb], writes=[lb])
            fw.op("pe", lambda e: e.matmul(self.ps[0][:, 0:2], ones_f[:], pr[:], start=True, stop=True), reads=[lb], writes=[self.psb[0]])
            fw.op("act", lambda e: e.activation(out=ex[:], in_=self.ps[0][:, 0:2], func=AF.Exp), reads=[self.psb[0]], writes=[lb])
            fw.op("dve", lambda e: e.scalar_tensor_tensor(neglam[:], ex[:, 1:2], -lam_init, ex[:, 0:1], ALU.add, ALU.subtract), reads=[lb], writes=[lb])
            fw.op("dve", lambda e: e.tensor_scalar(subw[:], subg[:], 1.0 - lam_init, None, ALU.mult), reads=[lb], writes=[lb])
            fw.barrier()
            import os
            if os.environ.get("DBGLAM"):
                o = self.nc.dram_tensor("dbg_lam", [128, 8], F32, kind="ExternalOutput")
                self.dbg_out.append("dbg_lam")
                dd = fw.sb(ph, [128, 8], F32, "dd")
                db = Buf()
                fw.op("dve", lambda e: e.memset(dd[:], 0.0), writes=[db])
                fw.op("dve", lambda e: e.tensor_copy(dd[:, 0:1], neglam[:]), writes=[db])
                fw.op("dve", lambda e: e.tensor_copy(dd[:, 1:3], ex[:]), writes=[db])
                fw.op("dve", lambda e: e.tensor_copy(dd[:, 3:5], pr[:]), writes=[db])
                fw.op("dve", lambda e: e.tensor_copy(dd[:, 5:7], subw[:]), writes=[db])
                fw.dma(o.ap()[:, :], dd[:], reads=[db])
                fw.barrier()
        for grp in self.groups():
            with contextlib.ExitStack() as ph:
                hT, hb, ntok, segs = self.make_hT(ph, grp, cur, mv)
                with contextlib.ExitStack() as ph2:
                    jobs = []
                    for c in range(16):
                        jobs.append(dict(segs=[(c * 128, 128)], mode="rope", swap=64, dst=("dram", self.dst_rows(QT.ap(), c * 128, 128, segs))))
                    for c in range(16):
                        jobs.append(dict(segs=[(2048 + c * 128, 128)], mode="rope", swap=64, dst=("dram", self.dst_rows(KTc[c].ap(), 0, 128, segs))))
                    for c in range(16):
                        jobs.append(dict(segs=[(6144 + c * 128, 128)], mode="silu", dst=("dram", self.dst_rows(GT.ap(), c * 128, 128, segs))))
                    self.proj_fm(ph2, hT, hb, KC, ntok, w_in, jobs, rope=self.rope_aps(Ct, St, segs))
                    fw.barrier()
                with contextlib.ExitStack() as ph2:
                    for (kind, t0, n, off) in segs:
                        base = t0 if kind == "l" else LT
                        sub = [([(4096 + j * 256, 256)], [Vc[2 * j].ap(), Vc[2 * j + 1].ap()]) for j in range(8)]
                        self.proj_tm(ph2, _View(hT, off), hb[off // 512:], KC, n, w_in, sub, base)
                    fw.barrier()
        for c in range(16):
            fw.collective(KTc[c].ap().opt(), KTallc[c].ap().opt())
            fw.collective(Vc[c].ap().opt(), Vallc[c].ap().opt())
        fw.barrier()
        qsegs = [("l", s, n) for (s, n) in tok_blocks(LT)]
        if need_ctx:
            qsegs.append(("c", 0, CT))
        nkb_l = LT // 128
        with contextlib.ExitStack() as ph:
            NKB = 2 * nkb_l + CT // 128
            Kt = [[fw.sb(ph, [128, NKB * 128], BF16, "K3") for _ in range(2)] for _ in range(2)]
            Kb = [[[Buf() for _ in range(3)] for _ in range(2)] for _ in range(2)]
            Vt = [fw.sb(ph, [128, NKB, 256], BF16, "V3") for _ in range(2)]
            Vb = [[Buf() for _ in range(3)] for _ in range(2)]
            Qt = [fw.sb(ph, [128, 512], BF16, "Q3") for _ in range(4)]
            Qb = [Buf() for _ in range(4)]
            Gt = [fw.sb(ph, [128, 2, 512], BF16, "G3") for _ in range(2)]
            Gb = [Buf() for _ in range(2)]
            ptr = [fw.sb(ph, [128, 512], BF16, "P3") for _ in range(3)]
            ptb = [Buf() for _ in range(3)]
            o1 = [fw.sb(ph, [128, 2, 512], F32, "o1") for _ in range(2)]
            o1b = [Buf() for _ in range(2)]
            rc = [fw.sb(ph, [128, 512], F32, "rc") for _ in range(2)]
            rcb = [Buf() for _ in range(2)]
            tt = [fw.sb(ph, [128, 512], F32, "tt") for _ in range(2)]
            ttb = [Buf() for _ in range(2)]
            sqt = [fw.sb(ph, [128, 512], BF16, "sq3") for _ in range(2)]
            sqb = [Buf() for _ in range(2)]
            rs = fw.sb(ph, [128, 512], F32, "rs3")
            rsb = Buf()
            ogt = [fw.sb(ph, [128, 2, 512], BF16, "og3") for _ in range(2)]
            ogb = [Buf() for _ in range(2)]
            dacc = {"dve": (fw.sb(ph, [128, 512], F32, "daccD"), Buf()), "pool": (fw.sb(ph, [128, 512], F32, "daccP"), Buf())}
            def load_head(h):
                sl = h % 2
                for i in range(2):
                    r0 = (2 * h + i) * 128
                    c = 2 * h + i
                    for rk in range(2):
                        fw.dma(Kt[sl][i][:, rk * LT:(rk + 1) * LT], KTallc[c].ap()[rk * 128:(rk + 1) * 128, 0:LT], writes=[Kb[sl][i][rk]])
                    fw.dma(Kt[sl][i][:, 2 * LT:2 * LT + CT], KTc[c].ap()[:, LT:T], writes=[Kb[sl][i][2]])
                for cc in range(2):
                    c = 2 * h + cc
                    for rk in range(2):
                        fw.dma(Vt[sl][:, rk * nkb_l:(rk + 1) * nkb_l, cc * 128:(cc + 1) * 128],
                               Vallc[c].ap()[rk * T:rk * T + LT, :].rearrange("(b p) c -> p b c", p=128), writes=[Vb[sl][rk]])
                    fw.dma(Vt[sl][:, 2 * nkb_l:NKB, cc * 128:(cc + 1) * 128], Vc[c].ap()[LT:T, :].rearrange("(b p) c -> p b c", p=128), writes=[Vb[sl][2]])

            items = [(h, qs_) for h in range(8) for qs_ in qsegs]

            def load_q(qi):
                h, (kind, s0, N) = items[qi]
                tb = s0 if kind == "l" else LT
                gs = qi % 2
                fw.dma(Gt[gs][:, :, :N], GT.ap()[h * 256:(h + 1) * 256, tb:tb + N].rearrange("(c p) t -> p c t", p=128), writes=[Gb[gs]])
                for i in range(2):
                    qs = (2 * qi + i) % 4
                    r0 = (2 * h + i) * 128
                    fw.dma(Qt[qs][:, :N], QT.ap()[r0:r0 + 128, tb:tb + N], writes=[Qb[qs]])

            acc_rr = 0
            load_head(0)
            load_q(0)
            for qi, (h, (kind, s0, N)) in enumerate(items):
                sl = h % 2
                tb = s0 if kind == "l" else LT
                gs = qi % 2
                if qi % len(qsegs) == 0 and h + 1 < 8:
                    load_head(h + 1)
                if qi + 1 < len(items):
                    load_q(qi + 1)
                for i in range(2):
                        qs = (2 * qi + i) % 4
                        if kind == "l":
                            kbl = list(range(NKB))
                        else:
                            kbl = list(range(2 * nkb_l, NKB))
                        a0 = 2 + 3 * (acc_rr % 2)
                        acc_rr += 1
                        pv = [(a0, 128), (a0 + 1, 128), (a0 + 2, 128)]
                        kblocks = []
                        for kb in kbl:
                            part = 0 if kb < nkb_l else (1 if kb < 2 * nkb_l else 2)
                            kblocks.append(dict(
                                s=[(Kt[sl][i][:, kb * 128:(kb + 1) * 128], Qt[qs][:, :N], [Kb[sl][i][part], Qb[qs]])],
                                v=[Vt[sl][:, kb, 0:128], Vt[sl][:, kb, 128:256], self.ones_bf[:]],
                                vdeps=[Vb[sl][part]]))
                        self.attn_stream(N, kblocks, scale, pv, ptr, ptb, [0, 1])
                        ri = i
                        fw.op("dve", lambda e: e.reciprocal(rc[ri][:, :N], self.ps[a0 + 2][:, :N]), reads=[self.psb[a0 + 2]], writes=[rcb[ri]])
                        os_ = qi % 2
                        if i == 0:
                            for c in range(2):
                                fw.op("dve", lambda e: e.tensor_tensor(o1[os_][:, c, :N], self.ps[a0 + c][:, :N], rc[0][:, :N], ALU.mult),
                                      reads=[self.psb[a0 + c], rcb[0]], writes=[o1b[os_]])
                        else:
                            for c in range(2):
                                fw.op("dve", lambda e: e.tensor_tensor(tt[c][:, :N], self.ps[a0 + c][:, :N], rc[1][:, :N], ALU.mult),
                                      reads=[self.psb[a0 + c], rcb[1]], writes=[ttb[c]])
                                fw.op("dve", lambda e: e.scalar_tensor_tensor(o1[os_][:, c, :N], tt[c][:, :N], neglam[:, 0:1], o1[os_][:, c, :N], ALU.mult, ALU.add),
                                      reads=[ttb[c], o1b[os_]], writes=[o1b[os_]])
                                fw.op("act", lambda e: e.activation(out=sqt[c][:, :N], in_=o1[os_][:, c, :N], func=AF.Square),
                                      reads=[o1b[os_]], writes=[sqb[c]])
                            for c in range(2):
                                fw.op("pe", lambda e: e.matmul(self.ps[0][:, :N], self.ones_bf[:], sqt[c][:, :N], start=(c == 0), stop=(c == 1)),
                                      reads=[sqb[c]], writes=[self.psb[0]], inc=(c == 1))
                            fw.op("act", lambda e: e.activation(out=rs[:, :N], in_=self.ps[0][:, :N], func=AF.Sqrt, scale=1.0 / 256, bias=self.eps_t[:, 0:1]),
                                  reads=[self.psb[0]], writes=[rsb])
                            fw.op("dve", lambda e: e.reciprocal(rs[:, :N], rs[:, :N]), reads=[rsb], writes=[rsb])
                            for c in range(2):
                                fw.op("dve", lambda e: e.scalar_tensor_tensor(tt[c][:, :N], o1[os_][:, c, :N], subw[:, c:c + 1], rs[:, :N], ALU.mult, ALU.mult),
                                      reads=[o1b[os_], rsb], writes=[ttb[c]])
                                fw.op("pool", lambda e: e.tensor_tensor(ogt[os_][:, c, :N], tt[c][:, :N], Gt[gs][:, c, :N], ALU.mult),
                                      reads=[ttb[c], Gb[gs]], writes=[ogb[os_]])
                            fw.dma(OGT.ap()[h * 256:(h + 1) * 256, tb:tb + N].rearrange("(c p) t -> p c t", p=128), ogt[os_][:, :, :N], reads=[ogb[os_]])
            fw.barrier()
        self.out_proj(cur, dst, "l3_w_out", OGT, mv, need_ctx)


    def simple_bufs(self, ph, M):
        fw = self.fw
        sbf = {}
        sbf["Qt"] = [fw.sb(ph, [128, 512], BF16, "Qt") for _ in range(2)]
        sbf["Qb"] = [Buf(), Buf()]
        sbf["Gt"] = [fw.sb(ph, [128, 512], BF16, "Gt") for _ in range(2)]
        sbf["Gb"] = [Buf(), Buf()]
        sbf["ptr"] = [fw.sb(ph, [128, 512], BF16, "Pt") for _ in range(3)]
        sbf["ptb"] = [Buf() for _ in range(3)]
        sbf["rc"] = [fw.sb(ph, [128, 512], F32, "rc") for _ in range(2)]
        sbf["rcb"] = [Buf(), Buf()]
        sbf["tt"] = [fw.sb(ph, [128, 512], F32, "tt") for _ in range(2)]
        sbf["ttb"] = [Buf(), Buf()]
        sbf["og"] = [fw.sb(ph, [128, 512], BF16, "og") for _ in range(2)]
        sbf["ogb"] = [Buf(), Buf()]
        return sbf

    def simple_finalize(self, sbf, qi, M, N, po, pd, extra, Gt_ap, Gbuf, out_ap):
        fw = self.fw
        s = qi % 2
        rc, rcb, tt, ttb, og, ogb = (sbf[k] for k in ("rc", "rcb", "tt", "ttb", "og", "ogb"))
        if extra is not None:
            fw.op("dve", lambda e: e.tensor_scalar(rc[s][:M, :N], self.ps[pd][:M, :N], extra, None, ALU.add), reads=[self.psb[pd]], writes=[rcb[s]])
            fw.op("dve", lambda e: e.reciprocal(rc[s][:M, :N], rc[s][:M, :N]), reads=[rcb[s]], writes=[rcb[s]])
        else:
            fw.op("dve", lambda e: e.reciprocal(rc[s][:M, :N], self.ps[pd][:M, :N]), reads=[self.psb[pd]], writes=[rcb[s]])
        fw.op("dve", lambda e: e.tensor_tensor(tt[s][:M, :N], self.ps[po][:M, :N], rc[s][:M, :N], ALU.mult), reads=[self.psb[po], rcb[s]], writes=[ttb[s]])
        fw.op("pool", lambda e: e.tensor_tensor(og[s][:M, :N], tt[s][:M, :N], Gt_ap, ALU.mult), reads=[ttb[s], Gbuf], writes=[ogb[s]])
        fw.dma(out_ap, og[s][:M, :N], reads=[ogb[s]])

    def layer1(self, cur, dst, xoth, need_ctx):
        fw, nc = self.fw, self.nc
        LT, T = self.LT, self.T
        mv = self.mod_phase(1)
        w_in = self.inp("l1_w_in", [D, 4608])
        Ct = self.inp("rope64C", [128, T + LT])
        St = self.inp("rope64S", [128, T + LT])
        sink = self.load_vec("l1_sink_rep", 32)
        masks_in = self.inp("swa_mask", [128, 8, 512], BF16)
        QT = self.scratch("QT1", [2048, T], BF16)
        KT = self.scratch("KT1", [256, T], BF16)
        V = self.scratch("V1", [T, 256], BF16)
        GT = self.scratch("GT1", [2048, T], BF16)
        OGT = self.scratch("OGT1", [2048, T], BF16)
        KH = self.scratch("KH1", [256, 256], BF16)
        KHall = self.scratch("KH1all", [512, 256], BF16)
        VH = self.scratch("VH1", [256, 256], BF16)
        VHall = self.scratch("VH1all", [512, 256], BF16)
        pers = self._persist
        es_ = pers.enter_context(nc.sbuf_tensor("sinkexp", [128, 32], F32))
        sb_ = Buf()
        fw.op("act", lambda e: e.activation(out=es_[:], in_=sink[:], func=AF.Exp), writes=[sb_])
        fw.barrier()
        for grp in self.groups():
            with contextlib.ExitStack() as ph:
                hT, hb, ntok, segs = self.make_hT(ph, grp, cur, mv)
                with contextlib.ExitStack() as ph2:
                    jobs = []
                    for c in range(16):
                        jobs.append(dict(segs=[(c * 128, 128)], mode="rope", swap=32, scale=0.125, dst=("dram", self.dst_rows(QT.ap(), c * 128, 128, segs))))
                    for c in range(2):
                        jobs.append(dict(segs=[(2048 + c * 128, 128)], mode="rope", swap=32, dst=("dram", self.dst_rows(KT.ap(), c * 128, 128, segs))))
                    for c in range(16):
                        jobs.append(dict(segs=[(2560 + c * 128, 128)], mode="silu", dst=("dram", self.dst_rows(GT.ap(), c * 128, 128, segs))))
                    self.proj_fm(ph2, hT, hb, KC, ntok, w_in, jobs, rope=self.rope_aps(Ct, St, segs))
                    fw.barrier()
                with contextlib.ExitStack() as ph2:
                    for (kind, t0, n, off) in segs:
                        base = t0 if kind == "l" else LT
                        self.proj_tm(ph2, _View(hT, off), hb[off // 512:], KC, n, w_in, [([(2304, 256)], [V.ap()])], base)
                    fw.barrier()
        fw.dma(KH.ap()[:, 0:128], KT.ap()[:, 0:128])
        fw.dma(KH.ap()[:, 128:256], KT.ap()[:, LT - 128:LT])
        fw.dma(VH.ap()[0:128, :], V.ap()[0:128, :])
        fw.dma(VH.ap()[128:256, :], V.ap()[LT - 128:LT, :])
        fw.barrier()
        fw.collective(KH.ap().opt(), KHall.ap().opt())
        fw.collective(VH.ap().opt(), VHall.ap().opt())
        fw.barrier()
        nbl = LT // 128
        NB = nbl + 2 + 2
        nR = LT // 512
        with contextlib.ExitStack() as ph:
            msk = fw.sb(ph, [128, 8, 512], BF16, "swamask")
            mb = Buf()
            fw.dma(msk[:], masks_in[:, :, :], writes=[mb])
            Kt = [fw.sb(ph, [64, NB * 128], BF16, "K1") for _ in range(2)]
            Kb = [Buf(), Buf()]
            Vt = [fw.sb(ph, [128, NB, 64], BF16, "V1") for _ in range(2)]
            Vb = [Buf(), Buf()]
            sbf = self.simple_bufs(ph, 64)
            Qt, Qb, Gt, Gb, ptr, ptb = (sbf[k] for k in ("Qt", "Qb", "Gt", "Gb", "ptr", "ptb"))

            def load_kv(g):
                s = g % 2
                r0 = 64 * g
                fw.dma(Kt[s][:, 0:128], KHall.ap()[r0:r0 + 64, 128:256], writes=[Kb[s]])
                fw.dma(Kt[s][:, 128:128 + LT], KT.ap()[r0:r0 + 64, 0:LT], writes=[Kb[s]])
                fw.dma(Kt[s][:, 128 + LT:256 + LT], KHall.ap()[256 + r0:256 + r0 + 64, 0:128], writes=[Kb[s]])
                fw.dma(Kt[s][:, 256 + LT:256 + LT + CT], KT.ap()[r0:r0 + 64, LT:T], writes=[Kb[s]])
                fw.dma(Vt[s][:, 0, :], VHall.ap()[128:256, r0:r0 + 64], writes=[Vb[s]])
                fw.dma(Vt[s][:, 1:1 + nbl, :], V.ap()[0:LT, r0:r0 + 64].rearrange("(b p) c -> p b c", p=128), writes=[Vb[s]])
                fw.dma(Vt[s][:, 1 + nbl, :], VHall.ap()[256:384, r0:r0 + 64], writes=[Vb[s]])
                fw.dma(Vt[s][:, 2 + nbl:NB, :], V.ap()[LT:T, r0:r0 + 64].rearrange("(b p) c -> p b c", p=128), writes=[Vb[s]])

            qsegs = [("l", R) for R in range(nR)] + ([("c", 0)] if need_ctx else [])
            items = [(h, qs_) for h in range(32) for qs_ in qsegs]

            def load_q(qi):
                h, (kind, R) = items[qi]
                tb, N = (R * 512, 512) if kind == "l" else (LT, CT)
                s = qi % 2
                fw.dma(Qt[s][:64, :N], QT.ap()[64 * h:64 * h + 64, tb:tb + N], writes=[Qb[s]])
                fw.dma(Gt[s][:64, :N], GT.ap()[64 * h:64 * h + 64, tb:tb + N], writes=[Gb[s]])

            load_kv(0)
            load_q(0)
            for qi, (h, (kind, R)) in enumerate(items):
                g = h // 8
                ks = g % 2
                s = qi % 2
                tb, N = (R * 512, 512) if kind == "l" else (LT, CT)
                if qi % (8 * len(qsegs)) == 0 and g + 1 < 4:
                    load_kv(g + 1)
                if qi + 1 < len(items):
                    load_q(qi + 1)
                kblocks = []
                if kind == "l":
                    for jo in range(6):
                        blk = 4 * R + jo
                        mi = jo
                        if R == 0 and jo == 0:
                            mi = 6
                        if R == nR - 1 and jo == 5:
                            mi = 7
                        kblocks.append(dict(s=[(Kt[ks][:, blk * 128:(blk + 1) * 128], Qt[s][:64, :N], [Kb[ks], Qb[s]])],
                                            mask=(msk[:, mi, :N], [mb]),
                                            v=[Vt[ks][:, blk, :], self.ones_bf[:, 0:64]], vdeps=[Vb[ks]]))
                for cb in range(2):
                    blk = 2 + nbl + cb
                    kblocks.append(dict(s=[(Kt[ks][:, blk * 128:(blk + 1) * 128], Qt[s][:64, :N], [Kb[ks], Qb[s]])],
                                        v=[Vt[ks][:, blk, :], self.ones_bf[:, 0:64]], vdeps=[Vb[ks]]))
                a0 = 2 + 2 * (qi % 3)
                self.attn_stream(N, kblocks, 1.0, [(a0, 64), (a0 + 1, 64)], ptr, ptb, [0, 1])
                self.simple_finalize(sbf, qi, 64, N, a0, a0 + 1, es_[:64, h:h + 1], Gt[s][:64, :N], Gb[s], OGT.ap()[64 * h:64 * h + 64, tb:tb + N])
            fw.barrier()
        self.out_proj(cur, dst, "l1_w_out", OGT, mv, need_ctx)

    def layer2(self, cur, dst, xoth, need_ctx):
        fw, nc = self.fw, self.nc
        LT, T = self.LT, self.T
        mv = self.mod_phase(2)
        w_in = self.inp("l2_w_in", [D, 8192])
        bias_in = self.inp("na_bias", [32, 128, 24, 512], BF16)
        QT = self.scratch("QT2", [2048, T], BF16)
        KT = self.scratch("KT2", [2048, T], BF16)
        V = self.scratch("V2", [T, 2048], BF16)
        GT = self.scratch("GT2", [2048, T], BF16)
        OGT = self.scratch("OGT2", [2048, T], BF16)
        KH = [self.scratch(f"KH2_{i}", [1024, 512], BF16) for i in range(2)]
        KHall = [self.scratch(f"KH2all_{i}", [2048, 512], BF16) for i in range(2)]
        VH = [self.scratch(f"VH2_{i}", [256, 2048], BF16) for i in range(2)]
        VHall = [self.scratch(f"VH2all_{i}", [512, 2048], BF16) for i in range(2)]
        for grp in self.groups():
            with contextlib.ExitStack() as ph:
                hT, hb, ntok, segs = self.make_hT(ph, grp, cur, mv)
                with contextlib.ExitStack() as ph2:
                    jobs = []
                    for c in range(16):
                        jobs.append(dict(segs=[(c * 128, 128)], mode="plain", scale=0.125, dst=("dram", self.dst_rows(QT.ap(), c * 128, 128, segs))))
                    for c in range(16):
                        jobs.append(dict(segs=[(2048 + c * 128, 128)], mode="plain", dst=("dram", self.dst_rows(KT.ap(), c * 128, 128, segs))))
                    for c in range(16):
                        jobs.append(dict(segs=[(6144 + c * 128, 128)], mode="silu", dst=("dram", self.dst_rows(GT.ap(), c * 128, 128, segs))))
                    self.proj_fm(ph2, hT, hb, KC, ntok, w_in, jobs)
                    fw.barrier()
                with contextlib.ExitStack() as ph2:
                    for (kind, t0, n, off) in segs:
                        base = t0 if kind == "l" else LT
                        sub = [([(4096 + j * 256, 256)], [V.ap()[:, j * 256:(j + 1) * 256]]) for j in range(8)]
                        self.proj_tm(ph2, _View(hT, off), hb[off // 512:], KC, n, w_in, sub, base)
                    fw.barrier()
        for i in range(2):
            fw.dma(KH[i].ap()[:, 0:256], KT.ap()[i * 1024:(i + 1) * 1024, 0:256])
            fw.dma(KH[i].ap()[:, 256:512], KT.ap()[i * 1024:(i + 1) * 1024, LT - 256:LT])
        fw.dma(VH[0].ap()[:, :], V.ap()[0:256, :])
        fw.dma(VH[1].ap()[:, :], V.ap()[LT - 256:LT, :])
        fw.barrier()
        for i in range(2):
            fw.collective(KH[i].ap().opt(), KHall[i].ap().opt())
            fw.collective(VH[i].ap().opt(), VHall[i].ap().opt())
        fw.barrier()
        nbl = LT // 128
        NB = nbl + 4 + 2
        nR = LT // 512
        with contextlib.ExitStack() as ph:
            Kt = [fw.sb(ph, [64, NB * 128], BF16, "K2") for _ in range(2)]
            Kb = [Buf(), Buf()]
            Vt = [fw.sb(ph, [128, NB, 64], BF16, "V2") for _ in range(2)]
            Vb = [Buf(), Buf()]
            Bt = [fw.sb(ph, [128, 24, 512], BF16, "B2") for _ in range(2)]
            Bb = [Buf(), Buf()]
            sbf = self.simple_bufs(ph, 64)
            Qt, Qb, Gt, Gb, ptr, ptb = (sbf[k] for k in ("Qt", "Qb", "Gt", "Gb", "ptr", "ptb"))

            def load_kv(h):
                s = h % 2
                r0 = 64 * h
                ci, wi = r0 // 1024, r0 % 1024
                fw.dma(Kt[s][:, 0:256], KHall[ci].ap()[wi:wi + 64, 256:512], writes=[Kb[s]])
                fw.dma(Kt[s][:, 256:256 + LT], KT.ap()[r0:r0 + 64, 0:LT], writes=[Kb[s]])
                fw.dma(Kt[s][:, 256 + LT:512 + LT], KHall[ci].ap()[1024 + wi:1024 + wi + 64, 0:256], writes=[Kb[s]])
                fw.dma(Kt[s][:, 512 + LT:512 + LT + CT], KT.ap()[r0:r0 + 64, LT:T], writes=[Kb[s]])
                fw.dma(Vt[s][:, 0:2, :], VHall[1].ap()[0:256, r0:r0 + 64].rearrange("(b p) c -> p b c", p=128), writes=[Vb[s]])
                fw.dma(Vt[s][:, 2:2 + nbl, :], V.ap()[0:LT, r0:r0 + 64].rearrange("(b p) c -> p b c", p=128), writes=[Vb[s]])
                fw.dma(Vt[s][:, 2 + nbl:4 + nbl, :], VHall[0].ap()[256:512, r0:r0 + 64].rearrange("(b p) c -> p b c", p=128), writes=[Vb[s]])
                fw.dma(Vt[s][:, 4 + nbl:NB, :], V.ap()[LT:T, r0:r0 + 64].rearrange("(b p) c -> p b c", p=128), writes=[Vb[s]])
                fw.dma(Bt[s][:], bias_in[h, :, :, :], writes=[Bb[s]])

            qsegs = [("l", R) for R in range(nR)] + ([("c", 0)] if need_ctx else [])
            items = [(h, qs_) for h in range(32) for qs_ in qsegs]

            def load_q(qi):
                h, (kind, R) = items[qi]
                tb, N = (R * 512, 512) if kind == "l" else (LT, CT)
                s = qi % 2
                fw.dma(Qt[s][:64, :N], QT.ap()[64 * h:64 * h + 64, tb:tb + N], writes=[Qb[s]])
                fw.dma(Gt[s][:64, :N], GT.ap()[64 * h:64 * h + 64, tb:tb + N], writes=[Gb[s]])

            load_kv(0)
            load_q(0)
            for qi, (h, (kind, R)) in enumerate(items):
                ks = h % 2
                s = qi % 2
                tb, N = (R * 512, 512) if kind == "l" else (LT, CT)
                if qi % len(qsegs) == 0 and h + 1 < 32:
                    load_kv(h + 1)
                if qi + 1 < len(items):
                    load_q(qi + 1)
                kblocks = []
                if kind == "l":
                    var = 0 if R == 0 else (2 if R == nR - 1 else 1)
                    for jo in range(8):
                        blk = 4 * R + jo
                        kblocks.append(dict(s=[(Kt[ks][:, blk * 128:(blk + 1) * 128], Qt[s][:64, :N], [Kb[ks], Qb[s]])],
                                            mask=(Bt[ks][:, var * 8 + jo, :N], [Bb[ks]]),
                                            v=[Vt[ks][:, blk, :], self.ones_bf[:, 0:64]], vdeps=[Vb[ks]]))
                for cb in range(2):
                    blk = 4 + nbl + cb
                    kblocks.append(dict(s=[(Kt[ks][:, blk * 128:(blk + 1) * 128], Qt[s][:64, :N], [Kb[ks], Qb[s]])],
                                        v=[Vt[ks][:, blk, :], self.ones_bf[:, 0:64]], vdeps=[Vb[ks]]))
                a0 = 2 + 2 * (qi % 3)
                self.attn_stream(N, kblocks, 1.0, [(a0, 64), (a0 + 1, 64)], ptr, ptb, [0, 1])
                self.simple_finalize(sbf, qi, 64, N, a0, a0 + 1, None, Gt[s][:64, :N], Gb[s], OGT.ap()[64 * h:64 * h + 64, tb:tb + N])
            fw.barrier()
        self.out_proj(cur, dst, "l2_w_out", OGT, mv, need_ctx)

    def layer0(self, cur, dst, xoth, need_ctx):
        fw, nc = self.fw, self.nc
        LT, T = self.LT, self.T
        TK = T + LT
        mv = self.mod_phase(0)
        w_in = self.inp("l0_w_in", [D, 3136])
        w_qb = self.inp("l0_w_qb", [512, 3072])
        w_kvb = self.inp("l0_w_kvb", [512, 4096])
        qng = self.load_vec("l0_q_norm", 4)
        kvng = self.load_vec("l0_kv_norm", 4)
        Ct = self.inp("rope64C", [128, T + LT])
        St = self.inp("rope64S", [128, T + LT])
        QN = self.scratch("QN0", [2048, T], BF16)
        QR = self.scratch("QR0", [1024, T], BF16)
        KN = self.scratch("KN0", [2048, TK], BF16)
        KR = self.scratch("KR0", [64, TK], BF16)
        V = self.scratch("V0", [TK, 2048], BF16)
        GT = self.scratch("GT0", [2048, T], BF16)
        OGT = self.scratch("OGT0", [2048, T], BF16)
        scale = 192 ** -0.5
        own_groups = self.groups()
        oth_groups = [[("o", s, n)] for (s, n) in tok_blocks(LT, 2048)]
        for grp in own_groups + oth_groups:
            own = grp[0][0] != "o"
            with contextlib.ExitStack() as ph:
                ntok = sum(n for (_, _, n) in grp)
                nblk = (ntok + 511) // 512
                qcT = fw.sb(ph, [128, 4, ntok], BF16, "qcT") if own else None
                kvcT = fw.sb(ph, [128, 4, ntok], BF16, "kvcT")
                qcb = [Buf() for _ in range(nblk)]
                kvb_ = [Buf() for _ in range(nblk)]
                with contextlib.ExitStack() as phh:
                    hT, hb, ntok, segs = self.make_hT(phh, grp, cur, mv, xoth=xoth)

                    def tkdst(scr, r0, M):
                        out = []
                        for (kind, t0, n, off) in segs:
                            base = {"l": t0, "c": LT, "o": T + t0}[kind]
                            out.append((scr[r0:r0 + M, base:base + n], off, off + n))
                        return out

                    with contextlib.ExitStack() as ph2:
                        jobs = []
                        if own:
                            for c in range(4):
                                jobs.append(dict(segs=[(c * 128, 128)], mode="plain", dst=("sb", qcT, c, qcb)))
                        for c in range(4):
                            jobs.append(dict(segs=[(512 + c * 128, 128)], mode="plain", dst=("sb", kvcT, c, kvb_)))
                        jobs.append(dict(segs=[(1024, 64)], mode="rope", swap=32, dst=("dram", tkdst(KR.ap(), 0, 64))))
                        if own:
                            for c in range(16):
                                jobs.append(dict(segs=[(1088 + c * 128, 128)], mode="silu", dst=("dram", self.dst_rows(GT.ap(), c * 128, 128, segs))))
                        self.proj_fm(ph2, hT, hb, KC, ntok, w_in, jobs, rope=self.rope_aps(Ct, St, segs))
                        fw.barrier()
                with contextlib.ExitStack() as ph2:
                    sq = [fw.sb(ph2, [128, 512], BF16, "lsq") for _ in range(2)]
                    sqb = [Buf(), Buf()]
                    rstd = fw.sb(ph2, [128, 512], F32, "lrstd")
                    rb = Buf()
                    tmp = [fw.sb(ph2, [128, 512], F32, "ltmp") for _ in range(2)]
                    tmb = [Buf(), Buf()]
                    todo = ([(qcT, qcb, qng)] if own else []) + [(kvcT, kvb_, kvng)]
                    cnt = 0
                    for (tT, tb_, gv) in todo:
                        for (t0, m) in tok_blocks(ntok):
                            bb = tb_[t0 // 512]
                            pi = 6 + (cnt % 2)
                            cnt += 1
                            for k in range(4):
                                q = k % 2
                                fw.op("act", lambda e: e.activation(out=sq[q][:, :m], in_=tT[:, k, t0:t0 + m], func=AF.Square), reads=[bb], writes=[sqb[q]])
                                fw.op("pe", lambda e: e.matmul(self.ps[pi][:, :m], self.ones_bf[:], sq[q][:, :m], start=(k == 0), stop=(k == 3)),
                                      reads=[sqb[q]], writes=[self.psb[pi]])
                            fw.op("act", lambda e: e.activation(out=rstd[:, :m], in_=self.ps[pi][:, :m], func=AF.Sqrt, scale=1.0 / 512, bias=self.eps_t[:, 0:1]),
                                  reads=[self.psb[pi]], writes=[rb])
                            fw.op("dve", lambda e: e.reciprocal(rstd[:, :m], rstd[:, :m]), reads=[rb], writes=[rb])
                            for k in range(4):
                                q = k % 2
                                fw.op("dve", lambda e: e.tensor_tensor(tmp[q][:, :m], tT[:, k, t0:t0 + m], rstd[:, :m], ALU.mult), reads=[bb, rb], writes=[tmb[q]])
                                fw.op("act", lambda e: e.activation(out=tT[:, k, t0:t0 + m], in_=tmp[q][:, :m], func=AF.Copy, scale=gv[:, k:k + 1]),
                                      reads=[tmb[q]], writes=[bb])
                    fw.barrier()
                with contextlib.ExitStack() as ph2:
                    if own:
                        jobs = []
                        for h in range(16):
                            jobs.append(dict(segs=[(192 * h, 128)], mode="plain", dst=("dram", self.dst_rows(QN.ap(), 128 * h, 128, segs))))
                        for j in range(8):
                            jobs.append(dict(segs=[(192 * (2 * j) + 128, 64), (192 * (2 * j + 1) + 128, 64)], mode="rope", swap=32,
                                             dst=("dram", self.dst_rows(QR.ap(), 128 * j, 128, segs))))
                        self.proj_fm(ph2, qcT, qcb, 4, ntok, w_qb, jobs, rope=self.rope_aps(Ct, St, segs))
                        fw.barrier()
                with contextlib.ExitStack() as ph2:
                    jobs = []
                    for h in range(16):
                        jobs.append(dict(segs=[(256 * h, 128)], mode="plain", dst=("dram", tkdst(KN.ap(), 128 * h, 128))))
                    self.proj_fm(ph2, kvcT, kvb_, 4, ntok, w_kvb, jobs)
                    fw.barrier()
                with contextlib.ExitStack() as ph2:
                    for (kind, t0, n, off) in segs:
                        base = {"l": t0, "c": LT, "o": T + t0}[kind]
                        sub = [([(256 * (2 * j) + 128, 128), (256 * (2 * j + 1) + 128, 128)], [V.ap()[:, j * 256:(j + 1) * 256]]) for j in range(8)]
                        self.proj_tm(ph2, _View(kvcT, off), kvb_[off // 512:], 4, n, w_kvb, sub, base)
                    fw.barrier()
        NKB = TK // 128
        cb0 = LT // 128
        with contextlib.ExitStack() as ph:
            Krt = fw.sb(ph, [128, TK], BF16, "KR")
            Krb = Buf()
            fw.op("pool", lambda e: e.memset(Krt[:], 0.0), writes=[Krb])
            fw.dma(Krt[0:64, :], KR.ap()[:, :], writes=[Krb])
            Kt = [fw.sb(ph, [128, TK], BF16, "K0") for _ in range(2)]
            Kb = [Buf(), Buf()]
            Vt = [fw.sb(ph, [128, NKB, 128], BF16, "V0") for _ in range(2)]
            Vb = [Buf(), Buf()]
            dacc = {"dve": (fw.sb(ph, [128, 512], F32, "daccD"), Buf()), "pool": (fw.sb(ph, [128, 512], F32, "daccP"), Buf())}
            Qr = [fw.sb(ph, [128, 512], BF16, "Qr") for _ in range(2)]
            Qrb = [Buf(), Buf()]
            for s_ in range(2):
                fw.op("pool", lambda e: e.memset(Qr[s_][:], 0.0), writes=[Qrb[s_]])
            sbf = self.simple_bufs(ph, 128)
            Qt, Qb, Gt, Gb, ptr, ptb = (sbf[k] for k in ("Qt", "Qb", "Gt", "Gb", "ptr", "ptb"))

            def load_kv(h):
                s = h % 2
                half = TK // 2
                for a in range(2):
                    fw.dma(Kt[s][:, a * half:(a + 1) * half], KN.ap()[128 * h:128 * h + 128, a * half:(a + 1) * half], writes=[Kb[s]])
                    fw.dma(Vt[s][:, a * (NKB // 2):(a + 1) * (NKB // 2), :],
                           V.ap()[a * half:(a + 1) * half, 128 * h:128 * h + 128].rearrange("(b p) c -> p b c", p=128), writes=[Vb[s]])

            qsegs = [("l", s_, n_) for (s_, n_) in tok_blocks(LT)] + ([("c", LT, CT)] if need_ctx else [])
            items = [(h, qs_) for h in range(16) for qs_ in qsegs]

            def load_q(qi):
                h, (kind, tb, N) = items[qi]
                s = qi % 2
                fw.dma(Qt[s][:, :N], QN.ap()[128 * h:128 * h + 128, tb:tb + N], writes=[Qb[s]])
                fw.dma(Qr[s][0:64, :N], QR.ap()[64 * h:64 * h + 64, tb:tb + N], writes=[Qrb[s]])
                fw.dma(Gt[s][:, :N], GT.ap()[128 * h:128 * h + 128, tb:tb + N], writes=[Gb[s]])

            load_kv(0)
            load_q(0)
            for qi, (h, (kind, tb, N)) in enumerate(items):
                ks = h % 2
                s = qi % 2
                if qi % len(qsegs) == 0 and h + 1 < 16:
                    load_kv(h + 1)
                if qi + 1 < len(items):
                    load_q(qi + 1)
                kbl = list(range(NKB)) if kind == "l" else [cb0, cb0 + 1]
                kblocks = []
                for blk in kbl:
                    kblocks.append(dict(s=[(Kt[ks][:, blk * 128:(blk + 1) * 128], Qt[s][:, :N], [Kb[ks], Qb[s]]),
                                           (Krt[:, blk * 128:(blk + 1) * 128], Qr[s][:, :N], [Krb, Qrb[s]])],
                                        v=[Vt[ks][:, blk, :], self.ones_bf[:]], vdeps=[Vb[ks]]))
                a0 = 2 + 2 * (qi % 3)
                self.attn_stream(N, kblocks, scale, [(a0, 128), (a0 + 1, 128)], ptr, ptb, [0, 1])
                self.simple_finalize(sbf, qi, 128, N, a0, a0 + 1, None, Gt[s][:, :N], Gb[s], OGT.ap()[128 * h:128 * h + 128, tb:tb + N])
            fw.barrier()
        self.out_proj(cur, dst, "l0_w_out", OGT, mv, need_ctx)


class _View:
    def __init__(self, t, off):
        self.t = t
        self.off = off

    def __getitem__(self, idx):
        p, k, sl = idx
        return self.t[p, k, self.off + sl.start:self.off + sl.stop]


def _vl(v):
    v = np.asarray(v, np.float32)
    n = v.shape[0] // 128
    return np.ascontiguousarray(v.reshape(n, 128).T)


def _rope_tables(pos, d_rot, n_ctx, pos_oth=None):
    d_axis = d_rot // 2
    inv = (10000.0 ** (-np.arange(0, d_axis, 2, dtype=np.float32) / d_axis)).astype(np.float32)

    def tab(p):
        row = (p // GRID_W).astype(np.float32)
        col = (p % GRID_W).astype(np.float32)
        ang = np.concatenate([row[:, None] * inv, col[:, None] * inv], axis=-1).astype(np.float32)
        c = np.cos(ang).T
        s = np.sin(ang).T
        return np.concatenate([c, c], 0), np.concatenate([-s, s], 0)

    parts_c, parts_s = [], []
    c, s = tab(pos)
    parts_c.append(c)
    parts_s.append(s)
    parts_c.append(np.ones((d_rot, n_ctx), np.float32))
    parts_s.append(np.zeros((d_rot, n_ctx), np.float32))
    if pos_oth is not None:
        c, s = tab(pos_oth)
        parts_c.append(c)
        parts_s.append(s)
    C = np.concatenate(parts_c, 1)
    S = np.concatenate(parts_s, 1)
    rep = 128 // d_rot
    return (np.ascontiguousarray(np.tile(C, (rep, 1)), np.float32), np.ascontiguousarray(np.tile(S, (rep, 1)), np.float32))


def _prep(inputs, prog, b, r):
    LT = prog.LT
    m = {}
    xT = np.asarray(inputs["x"][b], np.float32).T
    pos_own = np.arange(r * LT, (r + 1) * LT)
    pos_oth = np.arange((1 - r) * LT, (2 - r) * LT)
    for name, (shape, dt) in prog.inputs.items():
        if name == "ident":
            v = np.eye(128, dtype=np.float32)
        elif name == "xT_own":
            v = xT[:, r * LT:(r + 1) * LT]
        elif name == "xT_oth":
            v = xT[:, (1 - r) * LT:(2 - r) * LT]
        elif name == "ctxT":
            v = np.asarray(inputs["ctx"][b], np.float32).T
        elif name == "cvec":
            v = np.stack([_vl(inputs["c"][b]), _vl(inputs["c_ctx"])], axis=-1)
        elif name == "l3_lam":
            v = np.stack([np.asarray(inputs[k], np.float32) for k in ("l3_lam_q1", "l3_lam_k1", "l3_lam_q2", "l3_lam_k2")], axis=1)
        elif name == "rope128C":
            v = _rope_tables(pos_own, 128, CT)[0]
        elif name == "rope128S":
            v = _rope_tables(pos_own, 128, CT)[1]
        elif name == "rope64C":
            v = _rope_tables(pos_own, 64, CT, pos_oth)[0]
        elif name == "rope64S":
            v = _rope_tables(pos_own, 64, CT, pos_oth)[1]
        elif name in inputs and tuple(np.shape(inputs[name])) == shape:
            v = inputs[name]
        elif name in inputs and np.ndim(inputs[name]) == 1:
            v = _vl(inputs[name])
        else:
            v = _special(inputs, prog, name, b, r)
        if dt == BF16:
            v = np.ascontiguousarray(np.asarray(v).astype(ml_dtypes.bfloat16))
        else:
            v = np.ascontiguousarray(np.asarray(v, np.float32))
        assert v.shape == shape, (name, v.shape, shape)
        m[name] = v
    return m


def _special(inputs, prog, name, b, r):
    LT = prog.LT
    if name == "l1_sink_rep":
        return np.tile(np.asarray(inputs["l1_sink"], np.float32)[None, :], (128, 1))
    if name == "swa_mask":
        kk = np.arange(128)[:, None]
        qq = np.arange(512)[None, :]
        m = np.full((128, 8, 512), NEG, np.float32)
        for jo in range(6):
            ok = np.abs(qq - (128 * jo - 128 + kk)) <= 128
            m[:, jo, :] = np.where(ok, 0.0, NEG)
        if r == 1:
            m[:, 6, :] = m[:, 0, :]
        if r == 0:
            m[:, 7, :] = m[:, 5, :]
        return m
    if name == "na_bias":
        rpb = np.asarray(inputs["l2_rpb"], np.float32)
        rows_half = LT // GRID_W
        rows = 2 * rows_half
        nR = LT // 512
        kk = np.arange(128)[:, None]
        qq = np.arange(512)[None, :]
        out = np.empty((32, 128, 24, 512), ml_dtypes.bfloat16)
        for v, R in enumerate((0, min(1, nR - 1), nR - 1)):
            for jo in range(8):
                qr = r * rows_half + 8 * R + qq // GRID_W
                qc = qq % GRID_W
                kr = r * rows_half + 8 * R - 4 + 2 * jo + kk // GRID_W
                kc = kk % GRID_W
                rs = np.clip(qr - 4, 0, rows - 8)
                cs = np.clip(qc - 8, 0, GRID_W - 16)
                ok = (kr >= 0) & (kr < rows) & (kr >= rs) & (kr < rs + 8) & (kc >= cs) & (kc < cs + 16)
                dr = np.clip(kr - qr + 7, 0, 14)
                dc = np.clip(kc - qc, -15, 15) + 15
                dr, dc, ok = np.broadcast_arrays(dr, dc, ok)
                t = rpb[:, dr, dc]
                t = np.where(ok[None], t, NEG)
                out[:, :, v * 8 + jo, :] = t.astype(ml_dtypes.bfloat16)
        return out
    raise KeyError(name)


_CACHE = {}


def run(inputs, layers=(0, 1, 2, 3), final=True, need_ctx_last=False, dbg=()):
    x = np.asarray(inputs["x"])
    B, SEQ = x.shape[0], x.shape[1]
    LT = SEQ // 2
    key = (LT, tuple(layers), final, need_ctx_last, B, tuple(dbg))
    if key not in _CACHE:
        prog = Prog(LT, list(layers), final, need_ctx_last, B)
        prog.dbg_names = list(dbg)
        prog.build()
        _CACHE[key] = prog
    prog = _CACHE[key]
    in_maps = []
    for core in range(2 * B):
        in_maps.append(_prep(inputs, prog, core // 2, core % 2))
    res = run_bass_kernel_spmd(prog.nc, in_maps, core_ids=list(range(2 * B)))
    if dbg:
        global DBG
        DBG = [{n: np.asarray(res.results[core][n]) for n in prog.dbg_out} for core in range(2 * B)]
    out = np.empty((B, SEQ, D), np.float32)
    for core in range(2 * B):
        b, r = core // 2, core % 2
        out[b, r * LT:(r + 1) * LT, :] = res.results[core]["outT"].T
    return out


def kernel(**inputs):
    return run(inputs)
```

```python
import contextlib
import math
import numpy as np
import ml_dtypes
import concourse.bass as bass
import concourse.mybir as mybir
from concourse.bass_utils import run_bass_kernel_spmd

F32 = mybir.dt.float32
BF16 = mybir.dt.bfloat16
AF = mybir.ActivationFunctionType
ALU = mybir.AluOpType

D = 2048
KC = 16
CT = 256
GRID_W = 64
EPS = 1e-6
NEG = -30000.0
import os as _os
DENACC = _os.environ.get("DENACC", "1") != "0"
NPTR = int(_os.environ.get("NPTR", "6"))


class Buf:
    __slots__ = ("w", "r", "name")

    def __init__(self, name=""):
        self.w = None
        self.r = {}
        self.name = name


class FW:
    def __init__(self, nc, es, n_dma=24):
        self.nc = nc
        self.engs = dict(pe=nc.tensor, act=nc.scalar, dve=nc.vector, pool=nc.gpsimd, sp=nc.sync)
        self.sem = {k: es.enter_context(nc.semaphore("s_" + k)) for k in self.engs}
        self.cnt = {k: 0 for k in self.engs}
        self.seen = {k: {} for k in self.engs}
        self.dsem = [es.enter_context(nc.semaphore(f"d{i}")) for i in range(n_dma)]
        self.dtot = [0] * n_dma
        self.drr = 0
        self.csem = es.enter_context(nc.semaphore("ccs"))
        self.ccnt = 0
        self.pend = {k: [] for k in self.engs}
        self.uid = 0
        self.groups = [[0, 1], [2, 3], [4, 5], [6, 7]]
        self.nodrain = False

    def _semof(self, ev):
        kind, key, _ = ev
        if kind == "e":
            return self.sem[key]
        if kind == "d":
            return self.dsem[key]
        return self.csem

    def _wait(self, ek, ev):
        if ev is None:
            return
        kind, key, val = ev
        if kind == "p":
            assert key == ek, f"wait on pending event of {key} from {ek}"
            return
        if kind == "e" and key == ek:
            if ek == "pe" or self.nodrain or self.cnt[ek] - val >= 3:
                return
        if self.seen[ek].get((kind, key), 0) >= val:
            return
        self.engs[ek].wait_ge(self._semof(ev), val)
        self.seen[ek][(kind, key)] = val

    def _deps(self, ek, reads, writes):
        for b in reads:
            self._wait(ek, b.w)
        for b in writes:
            self._wait(ek, b.w)
            for ev in list(b.r.values()):
                self._wait(ek, ev)

    def op(self, ek, fn, reads=(), writes=(), inc=True, nodrain=False):
        self.nodrain = nodrain
        self._deps(ek, reads, writes)
        self.nodrain = False
        ins = fn(self.engs[ek])
        if not inc:
            self.pend[ek].append((tuple(reads), tuple(writes)))
            pe = ("p", ek, 0)
            for b in reads:
                b.r[ek] = pe
            for b in writes:
                b.w = pe
                b.r = {}
            return ins
        self.cnt[ek] += 1
        ins.then_inc(self.sem[ek], 1)
        ev = ("e", ek, self.cnt[ek])
        for (r, w) in self.pend[ek]:
            for b in r:
                if b.r.get(ek) == ("p", ek, 0):
                    b.r[ek] = ev
            for b in w:
                if b.w == ("p", ek, 0):
                    b.w = ev
        self.pend[ek] = []
        for b in reads:
            b.r[ek] = ev
        for b in writes:
            b.w = ev
            b.r = {}
        return ins

    def dma(self, out, in_, reads=(), writes=(), q="sp"):
        i = self.drr
        self.drr = (i + 1) % len(self.dsem)
        if self.dtot[i]:
            self._wait(q, ("d", i, self.dtot[i]))
        self._deps(q, reads, writes)
        ins = self.engs[q].dma_start(out=out, in_=in_)
        self.dtot[i] += 16
        ins.then_inc(self.dsem[i], 16)
        ev = ("d", i, self.dtot[i])
        for b in reads:
            b.r[("d", i)] = ev
        for b in writes:
            b.w = ev
            b.r = {}

    def collective(self, in_ap, out_ap):
        import os
        if os.environ.get("NOCC"):
            n0 = in_ap.shape[0]
            self.dma(out_ap[0:n0], in_ap)
            self.dma(out_ap[n0:2 * n0], in_ap)
            return
        q = "pool"
        if self.ccnt:
            self._wait(q, ("c", 0, self.ccnt))
        ins = self.nc.gpsimd.collective_compute(
            "AllGather", ALU.bypass, replica_groups=self.groups,
            ins=[in_ap], outs=[out_ap])
        self.ccnt += 1
        ins.then_inc(self.csem)

    def barrier(self):
        for ek in self.engs:
            assert not self.pend[ek], ek
        evs = [("e", k, self.cnt[k]) for k in self.engs if self.cnt[k]]
        evs += [("d", i, t) for i, t in enumerate(self.dtot) if t]
        if self.ccnt:
            evs.append(("c", 0, self.ccnt))
        for ek in self.engs:
            for ev in evs:
                self._wait(ek, ev)

    def sb(self, ph, shape, dtype, name):
        self.uid += 1
        return ph.enter_context(self.nc.sbuf_tensor(f"{name}_{self.uid}", list(shape), dtype))


def tok_blocks(n, bs=512):
    out = []
    s = 0
    while s < n:
        out.append((s, min(bs, n - s)))
        s += bs
    return out


class Prog:
    def __init__(self, LT, layers, final, need_ctx_last, B=4):
        self.B = B
        self.LT = LT
        self.T = LT + CT
        self.layers = layers
        self.final = final
        self.need_ctx_last = need_ctx_last
        self.nc = bass.Bass("TRN2", target_bir_lowering=False)
        self.inputs = {}
        self.uid = 0
        self.scr = {}
        self.inp_aps = {}
        self.vec_cache = {}
        self.dbg_names = []
        self.dbg_out = []

    def inp(self, name, shape, dtype=F32):
        if name in self.inp_aps:
            assert self.inputs[name][0] == tuple(shape), name
            return self.inp_aps[name]
        t = self.nc.dram_tensor(name, list(shape), dtype, kind="ExternalInput")
        self.inputs[name] = (tuple(shape), dtype)
        self.inp_aps[name] = t.ap()
        return self.inp_aps[name]

    def scratch(self, name, shape, dtype):
        t = self.nc.dram_tensor(name, list(shape), dtype)
        self.scr[name] = t
        return t

    def dump_dbg(self):
        fw = self.fw
        for name in self.dbg_names:
            t = self.scr[name]
            o = self.nc.dram_tensor("dbg_" + name, list(t.shape), t.dtype, kind="ExternalOutput")
            self.dbg_out.append("dbg_" + name)
            rows = t.shape[0]
            for r0 in range(0, rows, 128):
                r1 = min(rows, r0 + 128)
                fw.dma(o.ap()[r0:r1, :], t.ap()[r0:r1, :])
        fw.barrier()

    def build(self):
        nc = self.nc
        LT, T = self.LT, self.T
        es = contextlib.ExitStack()
        with es:
            fw = self.fw = FW(nc, es)
            fw.groups = [[2 * i, 2 * i + 1] for i in range(self.B)]
            self._persist = es
            self.ps = [es.enter_context(nc.psum_tensor(f"ps{i}", [128, 512], F32)) for i in range(8)]
            self.psb = [Buf(f"ps{i}") for i in range(8)]
            self.ones_bf = fw.sb(es, [128, 128], BF16, "ones")
            self.ident_bf = fw.sb(es, [128, 128], BF16, "ident")
            self.cb = Buf("const")
            ident_in = self.inp("ident", [128, 128], F32)
            idf = self.ident_f = fw.sb(es, [128, 128], F32, "identf")
            fw.dma(idf[:], ident_in[:, :], writes=[self.cb])
            fw.op("dve", lambda e: e.tensor_copy(self.ident_bf[:], idf[:]), reads=[self.cb], writes=[self.cb])
            fw.op("dve", lambda e: e.memset(self.ones_bf[:], 1.0), writes=[self.cb])
            self.ones_f = fw.sb(es, [128, 128], F32, "onesf32")
            fw.op("dve", lambda e: e.memset(self.ones_f[:], 1.0), writes=[self.cb])
            self.eps_t = fw.sb(es, [128, 1], F32, "epst")
            fw.op("dve", lambda e: e.memset(self.eps_t[:], EPS), writes=[self.cb])
            xT_own = self.inp("xT_own", [D, LT])
            xT_oth = self.inp("xT_oth", [D, LT]) if 0 in self.layers else None
            ctxT = self.inp("ctxT", [D, CT])
            cvec = self.inp("cvec", [128, KC, 2])
            self.silu_c = fw.sb(es, [128, KC, 2], F32, "siluc")
            cv = fw.sb(es, [128, KC, 2], F32, "cv")
            fw.dma(cv[:], cvec[:, :, :], writes=[self.cb])
            fw.op("act", lambda e: e.activation(out=self.silu_c[:], in_=cv[:], func=AF.Silu), reads=[self.cb], writes=[self.cb])
            fw.barrier()
            XA = self.scratch("XA", [D, T], F32)
            XB = self.scratch("XB", [D, T], F32)
            cur = (xT_own, ctxT)
            nxt = [XA, XB]
            outT = self.nc.dram_tensor("outT", [D, LT], F32, kind="ExternalOutput").ap()
            for li, L in enumerate(self.layers):
                dst = nxt[li % 2]
                dst_pair = (dst.ap()[:, 0:LT], dst.ap()[:, LT:T])
                last = (L == 3) and not self.need_ctx_last
                getattr(self, f"layer{L}")(cur, dst_pair, xT_oth, need_ctx=not last)
                cur = dst_pair
            if self.dbg_names:
                self.dump_dbg()
            if self.final:
                fng = self.load_vec("final_norm", KC)
                with contextlib.ExitStack() as ph:
                    nb = self.norm_bufs(ph, with_out=True)
                    self.norm_phase(nb, cur[0], 0, LT, fng, None, out_dram=outT)
                    fw.barrier()
            else:
                with contextlib.ExitStack() as ph:
                    t = fw.sb(ph, [128, KC, 512], F32, "dump")
                    tb = Buf()
                    for (s, n) in tok_blocks(LT):
                        fw.dma(t[:, :, :n], cur[0][:, s:s + n].rearrange("(k p) t -> p k t", p=128), writes=[tb])
                        fw.dma(outT[:, s:s + n].rearrange("(k p) t -> p k t", p=128), t[:, :, :n], reads=[tb])
                    fw.barrier()
            fw.barrier()
        return nc

    def load_vec(self, name, ncol):
        fw = self.fw
        if name in self.vec_cache:
            return self.vec_cache[name]
        ap = self.inp(name, [128, ncol])
        t = self._persist.enter_context(self.nc.sbuf_tensor(f"v_{name}", [128, ncol], F32))
        b = Buf(name)
        fw.dma(t[:], ap[:, :], writes=[b])
        fw.barrier()
        self.vec_cache[name] = t
        return t

    def mod_phase(self, L):
        fw, nc = self.fw, self.nc
        mod_w = self.inp(f"l{L}_mod_w", [D, 3 * D])
        mod_b = self.load_vec(f"l{L}_mod_b", 48)
        ng = self.load_vec(f"l{L}_norm", KC)
        pers = self._persist
        modT = pers.enter_context(nc.sbuf_tensor(f"modT{L}", [128, 48, 2], F32))
        res = {}
        for nm in ("a_l", "b_l", "g_l", "a_c", "b_c", "g_c"):
            res[nm] = pers.enter_context(nc.sbuf_tensor(f"{nm}{L}", [128, KC], F32))
        mb = Buf("modT")
        with contextlib.ExitStack() as ph:
            wst = [fw.sb(ph, [128, KC, 512], F32, "mwst") for _ in range(2)]
            wb = [Buf("mw0"), Buf("mw1")]
            mrow = fw.sb(ph, [2, 3 * D], F32, "mrow")
            mrb = Buf("mrow")
            for fc in range(12):
                s = fc % 2
                fw.dma(wst[s][:], mod_w[:, fc * 512:(fc + 1) * 512].rearrange("(k p) c -> p k c", p=128), writes=[wb[s]])
                pi = fc % 2
                for k in range(KC):
                    fw.op("pe", lambda e: e.matmul(self.ps[pi][0:2, :], self.silu_c[:, k, :], wst[s][:, k, :], start=(k == 0), stop=(k == KC - 1)),
                          reads=[wb[s]], writes=[self.psb[pi]], inc=(k == KC - 1))
                fw.op("dve", lambda e: e.tensor_copy(mrow[0:2, fc * 512:(fc + 1) * 512], self.ps[pi][0:2, :]), reads=[self.psb[pi]], writes=[mrb])
            for fc in range(48):
                fw.op("pe", lambda e: e.matmul(self.ps[2][:, 2 * fc:2 * fc + 2], mrow[0:2, fc * 128:(fc + 1) * 128], self.ident_f[0:2, 0:2], start=True, stop=True),
                      reads=[mrb], writes=[self.psb[2]], inc=(fc == 47))
            pv_ = self.ps[2][:, 0:96].rearrange("p (f j) -> p f j", j=2)
            for j in range(2):
                fw.op("dve", lambda e: e.tensor_tensor(modT[:, :, j], pv_[:, :, j], mod_b[:], ALU.add), reads=[self.psb[2]], writes=[mb])
            for j, sfx in ((0, "_l"), (1, "_c")):
                fw.op("dve", lambda e, j=j, sfx=sfx: e.scalar_tensor_tensor(res["a" + sfx][:], modT[:, 16:32, j], 1.0, ng[:], ALU.add, ALU.mult),
                      reads=[mb], writes=[mb])
                fw.op("dve", lambda e, j=j, sfx=sfx: e.tensor_copy(res["b" + sfx][:], modT[:, 0:16, j]), reads=[mb], writes=[mb])
                fw.op("dve", lambda e, j=j, sfx=sfx: e.tensor_copy(res["g" + sfx][:], modT[:, 32:48, j]), reads=[mb], writes=[mb])
            fw.barrier()
        return res

    def norm_bufs(self, ph, with_out=False):
        fw = self.fw
        nb = {}
        nb["xt"] = [fw.sb(ph, [128, KC, 512], F32, "xt") for _ in range(2)]
        nb["xb"] = [Buf("xt0"), Buf("xt1")]
        nb["sq"] = [fw.sb(ph, [128, 512], BF16, "sq") for _ in range(2)]
        nb["sqb"] = [Buf(), Buf()]
        nb["rstd"] = fw.sb(ph, [128, 512], F32, "rstd")
        nb["rb"] = Buf("rstd")
        nb["tmp"] = [fw.sb(ph, [128, 512], F32, "ntmp") for _ in range(2)]
        nb["tmb"] = [Buf(), Buf()]
        if with_out:
            nb["ot"] = [fw.sb(ph, [128, KC, 512], F32, "nout")]
            nb["otb"] = [Buf()]
        nb["cnt"] = 0
        return nb

    def norm_phase(self, nb, src, t0, n, a, b, hT=None, hoff=0, hbufs=None, out_dram=None):
        fw = self.fw
        xt, xb, sq, sqb, rstd, rb, tmp, tmb = (nb[k] for k in ("xt", "xb", "sq", "sqb", "rstd", "rb", "tmp", "tmb"))
        if out_dram is not None:
            ot, otb = nb["ot"], nb["otb"]
        blks = tok_blocks(n)
        for (s, m) in blks:
            bi = nb["cnt"]
            nb["cnt"] += 1
            sl = bi % 2
            fw.dma(xt[sl][:, :, :m], src[:, t0 + s:t0 + s + m].rearrange("(k p) t -> p k t", p=128), writes=[xb[sl]])
            pi = 6 + (bi % 2)
            for k in range(KC):
                q = k % 2
                fw.op("act", lambda e, k=k, q=q: e.activation(out=sq[q][:, :m], in_=xt[sl][:, k, :m], func=AF.Square),
                      reads=[xb[sl]], writes=[sqb[q]])
                fw.op("pe", lambda e, k=k, q=q: e.matmul(self.ps[pi][:, :m], self.ones_bf[:], sq[q][:, :m], start=(k == 0), stop=(k == KC - 1)),
                      reads=[sqb[q]], writes=[self.psb[pi]])
            fw.op("act", lambda e: e.activation(out=rstd[:, :m], in_=self.ps[pi][:, :m], func=AF.Sqrt, scale=1.0 / D, bias=self.eps_t[:, 0:1]),
                  reads=[self.psb[pi]], writes=[rb])
            fw.op("dve", lambda e: e.reciprocal(rstd[:, :m], rstd[:, :m]), reads=[rb], writes=[rb])
            for k in range(KC):
                q = k % 2
                fw.op("dve", lambda e, k=k, q=q: e.tensor_tensor(tmp[q][:, :m], xt[sl][:, k, :m], rstd[:, :m], ALU.mult),
                      reads=[xb[sl], rb], writes=[tmb[q]])
                if out_dram is None:
                    hb = hbufs[(hoff + s) // 512]
                    fw.op("act", lambda e, k=k, q=q: e.activation(out=hT[:, k, hoff + s:hoff + s + m], in_=tmp[q][:, :m], func=AF.Identity,
                                                                   scale=a[:, k:k + 1], bias=b[:, k:k + 1]),
                          reads=[tmb[q]], writes=[hb])
                else:
                    fw.op("act", lambda e, k=k, q=q: e.activation(out=ot[0][:, k, :m], in_=tmp[q][:, :m], func=AF.Copy, scale=a[:, k:k + 1]),
                          reads=[tmb[q]], writes=[otb[0]])
            if out_dram is not None:
                fw.dma(out_dram[:, t0 + s:t0 + s + m].rearrange("(k p) t -> p k t", p=128), ot[0][:, :, :m], reads=[otb[0]])

    def proj_fm(self, ph, actT, abufs, kc, ntok, w, jobs, rope=None):
        fw = self.fw
        wst = [fw.sb(ph, [128, kc, 128], F32, "wst") for _ in range(2)]
        wsb = [Buf(), Buf()]
        wbf = [fw.sb(ph, [128, kc, 128], BF16, "wbf") for _ in range(2)]
        wbb = [Buf(), Buf()]
        has_rope = any(j["mode"] == "rope" for j in jobs)
        has_res = any(j["mode"] == "resid" for j in jobs)
        if has_rope:
            wsw = [fw.sb(ph, [128, kc, 128], BF16, "wsw") for _ in range(2)]
            wwb = [Buf(), Buf()]
            Ct = fw.sb(ph, [128, ntok], F32, "ropeC")
            St = fw.sb(ph, [128, ntok], F32, "ropeS")
            rpb_ = Buf("rope")
            fw.dma(Ct[:], rope[0], writes=[rpb_])
            fw.dma(St[:], rope[1], writes=[rpb_])
            t1 = [fw.sb(ph, [128, 512], F32, "rt1") for _ in range(2)]
            t2 = [fw.sb(ph, [128, 512], F32, "rt2") for _ in range(2)]
            t1b = [Buf(), Buf()]
            t2b = [Buf(), Buf()]
        odt = F32 if has_res else BF16
        ot = [fw.sb(ph, [128, ntok], odt, "pot") for _ in range(2)]
        otb = [Buf(), Buf()]
        if has_res:
            xs = [fw.sb(ph, [128, ntok], F32, "pxs") for _ in range(2)]
            xsb = [Buf(), Buf()]
        blks = tok_blocks(ntok)
        evac_rr = 0
        def issue_loads(ji):
            job = jobs[ji]
            s = ji % 2
            M = sum(n for (_, n) in job["segs"])
            off = 0
            for (c0, n) in job["segs"]:
                fw.dma(wst[s][:, :, off:off + n], w[:, c0:c0 + n].rearrange("(k p) c -> p k c", p=128), writes=[wsb[s]])
                off += n
            fw.op("pool", lambda e: e.tensor_copy(wbf[s][:, :, :M], wst[s][:, :, :M]), reads=[wsb[s]], writes=[wbb[s]])
            if job["mode"] == "rope":
                hd = job["swap"]
                v_in = wst[s][:, :, :M].rearrange("p k (h two j) -> p k h two j", two=2, j=hd)
                v_out = wsw[s][:, :, :M].rearrange("p k (h two j) -> p k h two j", two=2, j=hd)
                fw.op("pool", lambda e: e.tensor_copy(v_out[:, :, :, 0, :], v_in[:, :, :, 1, :]), reads=[wsb[s]], writes=[wwb[s]])
                fw.op("pool", lambda e: e.tensor_copy(v_out[:, :, :, 1, :], v_in[:, :, :, 0, :]), reads=[wsb[s]], writes=[wwb[s]])

        issue_loads(0)
        for ji, job in enumerate(jobs):
            s = ji % 2
            M = sum(n for (_, n) in job["segs"])
            mode = job["mode"]
            if ji + 1 < len(jobs):
                issue_loads(ji + 1)
            sc = job.get("scale", 1.0)
            dst = job["dst"]
            for bi, (t0, m) in enumerate(blks):
                pa = (2 * bi) % 4 if mode == "rope" else (evac_rr % 4)
                pb = pa + 1
                for k in range(kc):
                    fw.op("pe", lambda e, k=k, pa=pa: e.matmul(self.ps[pa][:M, :m], wbf[s][:, k, :M], actT[:, k, t0:t0 + m],
                                                                start=(k == 0), stop=(k == kc - 1)),
                          reads=[wbb[s], abufs[t0 // 512]], writes=[self.psb[pa]], inc=(k == kc - 1))
                if mode == "rope":
                    for k in range(kc):
                        fw.op("pe", lambda e, k=k, pb=pb: e.matmul(self.ps[pb][:M, :m], wsw[s][:, k, :M], actT[:, k, t0:t0 + m],
                                                                    start=(k == 0), stop=(k == kc - 1)),
                              reads=[wwb[s], abufs[t0 // 512]], writes=[self.psb[pb]], inc=(k == kc - 1))
                if dst[0] == "sb":
                    o_ap = dst[1][:M, dst[2], t0:t0 + m]
                    o_b = dst[3][t0 // 512]
                else:
                    o_ap = ot[s][:M, t0:t0 + m]
                    o_b = otb[s]
                if mode == "plain":
                    if evac_rr % 2 == 0:
                        fw.op("act", lambda e, pa=pa, o_ap=o_ap: e.activation(out=o_ap, in_=self.ps[pa][:M, :m], func=AF.Copy, scale=float(sc)),
                              reads=[self.psb[pa]], writes=[o_b])
                    else:
                        fw.op("dve", lambda e, pa=pa, o_ap=o_ap: e.tensor_scalar(o_ap, self.ps[pa][:M, :m], float(sc), None, ALU.mult),
                              reads=[self.psb[pa]], writes=[o_b])
                elif mode == "silu":
                    fw.op("act", lambda e, pa=pa, o_ap=o_ap: e.activation(out=o_ap, in_=self.ps[pa][:M, :m], func=AF.Silu),
                          reads=[self.psb[pa]], writes=[o_b])
                elif mode == "rope":
                    q = bi % 2
                    fw.op("dve", lambda e, pa=pa, q=q: e.scalar_tensor_tensor(t1[q][:M, :m], self.ps[pa][:M, :m], float(sc), Ct[:M, t0:t0 + m], ALU.mult, ALU.mult),
                          reads=[self.psb[pa], rpb_], writes=[t1b[q]])
                    fw.op("dve", lambda e, pb=pb, q=q: e.scalar_tensor_tensor(t2[q][:M, :m], self.ps[pb][:M, :m], float(sc), St[:M, t0:t0 + m], ALU.mult, ALU.mult),
                          reads=[self.psb[pb], rpb_], writes=[t2b[q]])
                    fw.op("pool", lambda e, q=q, o_ap=o_ap: e.tensor_tensor(o_ap, t1[q][:M, :m], t2[q][:M, :m], ALU.add),
                          reads=[t1b[q], t2b[q]], writes=[o_b])
                elif mode == "resid":
                    gate = job["gate"]
                    for (g0, g1, gap) in gate:
                        a0, a1 = max(g0, t0), min(g1, t0 + m)
                        if a0 >= a1:
                            continue
                        fw.op("dve", lambda e, pa=pa, a0=a0, a1=a1, gap=gap: e.scalar_tensor_tensor(
                            ot[s][:M, a0:a1], self.ps[pa][:M, a0 - t0:a1 - t0], gap, xs[s][:M, a0:a1], ALU.mult, ALU.add),
                            reads=[self.psb[pa], xsb[s]], writes=[o_b])
                evac_rr += 1
            if dst[0] == "dram":
                for (dap, c0, c1) in dst[1]:
                    fw.dma(dap, ot[s][:M, c0:c1], reads=[otb[s]])

    def proj_tm(self, ph, actT, abufs, kc, ntok, w, segs_list, dst_tok0):
        fw = self.fw
        NC_ = 256
        wst = [fw.sb(ph, [128, kc, NC_], F32, "vwst") for _ in range(2)]
        wsb = [Buf(), Buf()]
        wbf = [fw.sb(ph, [128, kc, NC_], BF16, "vwbf") for _ in range(2)]
        wbb = [Buf(), Buf()]
        vt = [fw.sb(ph, [128, 4, NC_], BF16, "vt") for _ in range(2)]
        vtb = [Buf(), Buf()]
        ntile = (ntok + 127) // 128
        assert ntok % 128 == 0
        rr = 0
        def issue_loads(ji):
            segs, _ = segs_list[ji]
            s = ji % 2
            M = sum(n for (_, n) in segs)
            off = 0
            for (c0, n) in segs:
                fw.dma(wst[s][:, :, off:off + n], w[:, c0:c0 + n].rearrange("(k p) c -> p k c", p=128), writes=[wsb[s]])
                off += n
            fw.op("pool", lambda e: e.tensor_copy(wbf[s][:, :, :M], wst[s][:, :, :M]), reads=[wsb[s]], writes=[wbb[s]])

        issue_loads(0)
        for ji, (segs, dsts) in enumerate(segs_list):
            s = ji % 2
            M = sum(n for (_, n) in segs)
            if ji + 1 < len(segs_list):
                issue_loads(ji + 1)
            for g0 in range(0, ntile, 4):
                gn = min(4, ntile - g0)
                vs = (g0 // 4) % 2
                for ti in range(g0, g0 + gn):
                    pa = 4 + (rr % 2)
                    rr += 1
                    for k in range(kc):
                        fw.op("pe", lambda e, k=k, pa=pa, ti=ti: e.matmul(self.ps[pa][:, :M], actT[:, k, ti * 128:(ti + 1) * 128], wbf[s][:, k, :M],
                                                                          start=(k == 0), stop=(k == kc - 1)),
                              reads=[wbb[s], abufs[(ti * 128) // 512]], writes=[self.psb[pa]], inc=(k == kc - 1))
                    if rr % 2 == 0:
                        fw.op("act", lambda e, pa=pa, ti=ti: e.activation(out=vt[vs][:, ti - g0, :M], in_=self.ps[pa][:, :M], func=AF.Copy),
                              reads=[self.psb[pa]], writes=[vtb[vs]])
                    else:
                        fw.op("dve", lambda e, pa=pa, ti=ti: e.tensor_copy(vt[vs][:, ti - g0, :M], self.ps[pa][:, :M]),
                              reads=[self.psb[pa]], writes=[vtb[vs]])
                r0 = dst_tok0 + g0 * 128
                for di, dap in enumerate(dsts):
                    w_ = dap.shape[1]
                    fw.dma(dap[r0:r0 + gn * 128, :].rearrange("(a p) c -> p a c", p=128), vt[vs][:, :gn, di * w_:(di + 1) * w_], reads=[vtb[vs]])

    def attn_stream(self, N, kblocks, exp_scale, pv, ptr, ptb, sring, den=None):
        fw = self.fw
        nk = len(kblocks)

        def emit_s(i):
            kb = kblocks[i]
            si = sring[i % len(sring)]
            parts = list(kb["s"])
            if kb.get("mask") is not None:
                parts.append((self.ident_bf[:], kb["mask"][0], kb["mask"][1]))
            for j, (l, r, deps) in enumerate(parts):
                fw.op("pe", lambda e, l=l, r=r, j=j, si=si: e.matmul(self.ps[si][:, :N], l, r, start=(j == 0), stop=(j == len(parts) - 1)),
                      reads=list(deps), writes=[self.psb[si]], inc=(j == len(parts) - 1))
            pi = i % len(ptr)
            fw.op("act", lambda e, si=si, pi=pi: e.activation(out=ptr[pi][:, :N], in_=self.ps[si][:, :N], func=AF.Exp, scale=float(exp_scale)),
                  reads=[self.psb[si]], writes=[ptb[pi]])

        def emit_pv(i):
            kb = kblocks[i]
            pi = i % len(ptr)
            for j, ((pidx, M), l) in enumerate(zip(pv, kb["v"])):
                fw.op("pe", lambda e, pidx=pidx, M=M, l=l, pi=pi: e.matmul(self.ps[pidx][:M, :N], l, ptr[pi][:, :N], start=(i == 0), stop=(i == nk - 1)),
                      reads=[ptb[pi]] + list(kb["vdeps"]), writes=[self.psb[pidx]], inc=(j == len(pv) - 1))

        used = {}
        dstate = {"pe": False}

        def emit_den(i):
            pi = i % len(ptr)
            pidx, M = den["out"]
            if i % 3 == 2:
                fw.op("pe", lambda e: e.matmul(self.ps[pidx][:M, :N], self.ones_bf[:, :M], ptr[pi][:, :N], start=(not dstate["pe"]), stop=False),
                      reads=[ptb[pi]], writes=[self.psb[pidx]], inc=True)
                dstate["pe"] = True
                return
            acc, ab = den["acc"]["dve"]
            if "dve" not in used:
                used["dve"] = True
                fw.op("dve", lambda e: e.tensor_copy(acc[:, :N], ptr[pi][:, :N]), reads=[ptb[pi]], writes=[ab], nodrain=True)
            else:
                fw.op("dve", lambda e: e.tensor_tensor(acc[:, :N], acc[:, :N], ptr[pi][:, :N], ALU.add), reads=[ptb[pi], ab], writes=[ab], nodrain=True)

        emit_s(0)
        for i in range(nk):
            if i + 1 < nk:
                emit_s(i + 1)
            emit_pv(i)
            if den is not None:
                emit_den(i)
        if den is not None:
            pidx, M = den["out"]
            acc, ab = den["acc"]["dve"]
            fw.op("pe", lambda e: e.matmul(self.ps[pidx][:M, :N], self.ones_f[:, :M], acc[:, :N], start=(not dstate["pe"]), stop=True),
                  reads=[ab], writes=[self.psb[pidx]], inc=True)

    def groups(self):
        LT = self.LT
        if LT <= 2048:
            return [[("l", 0, LT), ("c", 0, CT)]]
        h = LT // 2
        return [[("l", 0, h)], [("l", h, LT - h), ("c", 0, CT)]]

    def make_hT(self, ph, grp, cur, mv, xoth=None):
        fw = self.fw
        ntok = sum(n for (_, _, n) in grp)
        hT = fw.sb(ph, [128, KC, ntok], BF16, "hT")
        hb = [Buf(f"h{i}") for i in range((ntok + 511) // 512)]
        off = 0
        segs = []
        with contextlib.ExitStack() as ph2:
            nb = self.norm_bufs(ph2)
            for (kind, t0, n) in grp:
                if kind == "l":
                    self.norm_phase(nb, cur[0], t0, n, mv["a_l"], mv["b_l"], hT=hT, hoff=off, hbufs=hb)
                elif kind == "o":
                    self.norm_phase(nb, xoth, t0, n, mv["a_l"], mv["b_l"], hT=hT, hoff=off, hbufs=hb)
                else:
                    self.norm_phase(nb, cur[1], t0, n, mv["a_c"], mv["b_c"], hT=hT, hoff=off, hbufs=hb)
                segs.append((kind, t0, n, off))
                off += n
            fw.barrier()
        return hT, hb, ntok, segs

    def dst_rows(self, scr, r0, M, segs, own_only=True):
        LT = self.LT
        out = []
        for (kind, t0, n, off) in segs:
            if kind == "l":
                out.append((scr[r0:r0 + M, t0:t0 + n], off, off + n))
            elif kind == "c":
                out.append((scr[r0:r0 + M, LT:LT + n], off, off + n))
        return out

    def rope_aps(self, Ct, St, segs):
        LT = self.LT
        kind, t0, n, off = segs[0]
        base = {"l": 0, "c": LT, "o": LT + CT}[kind] + t0
        tot = sum(s[2] for s in segs)
        return (Ct[:, base:base + tot], St[:, base:base + tot])

    def out_proj(self, cur, dst, w_out_name, OGT, mv, need_ctx):
        fw = self.fw
        LT = self.LT
        w_out = self.inp(w_out_name, [D, D])
        for grp in self.groups():
            grp = [g for g in grp if need_ctx or g[0] == "l"]
            with contextlib.ExitStack() as ph:
                ntok = sum(n for (_, _, n) in grp)
                og = fw.sb(ph, [128, KC, ntok], BF16, "ogT")
                ob = [Buf() for _ in range((ntok + 511) // 512)]
                segs = []
                off = 0
                for (kind, t0, n) in grp:
                    base = t0 if kind == "l" else LT
                    for (s, m) in tok_blocks(n):
                        fw.dma(og[:, :, off + s:off + s + m], OGT.ap()[:, base + s:base + s + m].rearrange("(k p) t -> p k t", p=128),
                               writes=[ob[(off + s) // 512]])
                    segs.append((kind, t0, n, off))
                    off += n
                jobs = []
                for c in range(KC):
                    gate = []
                    for (kind, t0, n, o) in segs:
                        gate.append((o, o + n, (mv["g_l"] if kind == "l" else mv["g_c"])[:, c:c + 1]))
                    srcs = []
                    dsts = []
                    for (kind, t0, n, o) in segs:
                        sap = cur[0][c * 128:(c + 1) * 128, t0:t0 + n] if kind == "l" else cur[1][c * 128:(c + 1) * 128, 0:n]
                        dap = dst[0][c * 128:(c + 1) * 128, t0:t0 + n] if kind == "l" else dst[1][c * 128:(c + 1) * 128, 0:n]
                        srcs.append((sap, o, o + n))
                        dsts.append((dap, o, o + n))
                    jobs.append(dict(segs=[(c * 128, 128)], mode="resid", dst=("dram", dsts), xsrcs=srcs, gate=gate))
                self.proj_fm_resid(ph, og, ob, ntok, w_out, jobs)
                fw.barrier()

    def proj_fm_resid(self, ph, actT, abufs, ntok, w, jobs):
        fw = self.fw
        for j in jobs:
            j["xsrc"] = None
        self._resid_jobs(ph, actT, abufs, ntok, w, jobs)

    def _resid_jobs(self, ph, actT, abufs, ntok, w, jobs):
        fw = self.fw
        kc = KC
        wst = [fw.sb(ph, [128, kc, 128], F32, "wst") for _ in range(2)]
        wsb = [Buf(), Buf()]
        wbf = [fw.sb(ph, [128, kc, 128], BF16, "wbf") for _ in range(2)]
        wbb = [Buf(), Buf()]
        ot = [fw.sb(ph, [128, ntok], F32, "pot") for _ in range(2)]
        otb = [Buf(), Buf()]
        xs = [fw.sb(ph, [128, ntok], F32, "pxs") for _ in range(2)]
        xsb = [Buf(), Buf()]
        blks = tok_blocks(ntok)
        rr = 0
        def issue_loads(ji):
            job = jobs[ji]
            s = ji % 2
            (c0, n) = job["segs"][0]
            fw.dma(wst[s][:, :, :n], w[:, c0:c0 + n].rearrange("(k p) c -> p k c", p=128), writes=[wsb[s]])
            fw.op("pool", lambda e: e.tensor_copy(wbf[s][:], wst[s][:]), reads=[wsb[s]], writes=[wbb[s]])
            for (sap, a0, a1) in job["xsrcs"]:
                fw.dma(xs[s][:, a0:a1], sap, writes=[xsb[s]])

        issue_loads(0)
        for ji, job in enumerate(jobs):
            s = ji % 2
            if ji + 1 < len(jobs):
                issue_loads(ji + 1)
            for bi, (t0, m) in enumerate(blks):
                pa = rr % 4
                rr += 1
                for k in range(kc):
                    fw.op("pe", lambda e, k=k, pa=pa: e.matmul(self.ps[pa][:, :m], wbf[s][:, k, :], actT[:, k, t0:t0 + m],
                                                                start=(k == 0), stop=(k == kc - 1)),
                          reads=[wbb[s], abufs[t0 // 512]], writes=[self.psb[pa]], inc=(k == kc - 1))
                for (g0, g1, gap) in job["gate"]:
                    a0, a1 = max(g0, t0), min(g1, t0 + m)
                    if a0 >= a1:
                        continue
                    fw.op("dve", lambda e, pa=pa, a0=a0, a1=a1, gap=gap: e.scalar_tensor_tensor(
                        ot[s][:, a0:a1], self.ps[pa][:, a0 - t0:a1 - t0], gap, xs[s][:, a0:a1], ALU.mult, ALU.add),
                        reads=[self.psb[pa], xsb[s]], writes=[otb[s]])
            for (dap, a0, a1) in job["dst"][1]:
                fw.dma(dap, ot[s][:, a0:a1], reads=[otb[s]])

    def layer3(self, cur, dst, xoth, need_ctx):
        fw, nc = self.fw, self.nc
        LT, T = self.LT, self.T
        L = 3
        mv = self.mod_phase(L)
        w_in = self.inp("l3_w_in", [D, 8192])
        lamv = self.load_vec("l3_lam", 4)
        subg = self.load_vec("l3_subln", 2)
        Ct = self.inp("rope128C", [128, T])
        St = self.inp("rope128S", [128, T])
        lam_init = 0.8 - 0.6 * math.exp(-0.3 * 3)
        QT = self.scratch("QT3", [2048, T], BF16)
        KTc = [self.scratch(f"KT3_{c}", [128, T], BF16) for c in range(16)]
        KTallc = [self.scratch(f"KT3all_{c}", [256, T], BF16) for c in range(16)]
        Vc = [self.scratch(f"V3_{c}", [T, 128], BF16) for c in range(16)]
        Vallc = [self.scratch(f"V3all_{c}", [2 * T, 128], BF16) for c in range(16)]
        GT = self.scratch("GT3", [2048, T], BF16)
        OGT = self.scratch("OGT3", [2048, T], BF16)
        scale = 128 ** -0.5
        pers = self._persist
        neglam = pers.enter_context(nc.sbuf_tensor("neglam", [128, 1], F32))
        subw = pers.enter_context(nc.sbuf_tensor("subw", [128, 2], F32))
        with contextlib.ExitStack() as ph:
            pr = fw.sb(ph, [128, 2], F32, "lpr")
            ones_f = fw.sb(ph, [128, 128], F32, "onesf")
            ex = fw.sb(ph, [128, 2], F32, "lex")
            lb = Buf()
            fw.op("dve", lambda e: e.memset(ones_f[:], 1.0), writes=[lb])
            v4 = lamv[:].rearrange("p (a b) -> p a b", b=2)
            fw.op("dve", lambda e: e.tensor_tensor(pr[:], v4[:, :, 0], v4[:, :, 1], ALU.mult), reads=[lb], writes=[lb])
            fw.op("pe", lambda e: e.matmul(self.ps[0][:, 0:2], ones_f[:], pr[:], start=True, stop=True), reads=[lb], writes=[self.psb[0]])
            fw.op("act", lambda e: e.activation(out=ex[:], in_=self.ps[0][:, 0:2], func=AF.Exp), reads=[self.psb[0]], writes=[lb])
            fw.op("dve", lambda e: e.scalar_tensor_tensor(neglam[:], ex[:, 1:2], -lam_init, ex[:, 0:1], ALU.add, ALU.subtract), reads=[lb], writes=[lb])
            fw.op("dve", lambda e: e.tensor_scalar(subw[:], subg[:], 1.0 - lam_init, None, ALU.mult), reads=[lb], writes=[lb])
            fw.barrier()
            import os
            if os.environ.get("DBGLAM"):
                o = self.nc.dram_tensor("dbg_lam", [128, 8], F32, kind="ExternalOutput")
                self.dbg_out.append("dbg_lam")
                dd = fw.sb(ph, [128, 8], F32, "dd")
                db = Buf()
                fw.op("dve", lambda e: e.memset(dd[:], 0.0), writes=[db])
                fw.op("dve", lambda e: e.tensor_copy(dd[:, 0:1], neglam[:]), writes=[db])
                fw.op("dve", lambda e: e.tensor_copy(dd[:, 1:3], ex[:]), writes=[db])
                fw.op("dve", lambda e: e.tensor_copy(dd[:, 3:5], pr[:]), writes=[db])
                fw.op("dve", lambda e: e.tensor_copy(dd[:, 5:7], subw[:]), writes=[db])
                fw.dma(o.ap()[:, :], dd[:], reads=[db])
                fw.barrier()
        for grp in self.groups():
            with contextlib.ExitStack() as ph:
                hT, hb, ntok, segs = self.make_hT(ph, grp, cur, mv)
                with contextlib.ExitStack() as ph2:
                    jobs = []
                    for c in range(16):
                        jobs.append(dict(segs=[(c * 128, 128)], mode="rope", swap=64, dst=("dram", self.dst_rows(QT.ap(), c * 128, 128, segs))))
                    for c in range(16):
                        jobs.append(dict(segs=[(2048 + c * 128, 128)], mode="rope", swap=64, dst=("dram", self.dst_rows(KTc[c].ap(), 0, 128, segs))))
                    for c in range(16):
                        jobs.append(dict(segs=[(6144 + c * 128, 128)], mode="silu", dst=("dram", self.dst_rows(GT.ap(), c * 128, 128, segs))))
                    self.proj_fm(ph2, hT, hb, KC, ntok, w_in, jobs, rope=self.rope_aps(Ct, St, segs))
                    fw.barrier()
                with contextlib.ExitStack() as ph2:
                    for (kind, t0, n, off) in segs:
                        base = t0 if kind == "l" else LT
                        sub = [([(4096 + j * 256, 256)], [Vc[2 * j].ap(), Vc[2 * j + 1].ap()]) for j in range(8)]
                        self.proj_tm(ph2, _View(hT, off), hb[off // 512:], KC, n, w_in, sub, base)
                    fw.barrier()
        for c in range(16):
            fw.collective(KTc[c].ap().opt(), KTallc[c].ap().opt())
            fw.collective(Vc[c].ap().opt(), Vallc[c].ap().opt())
        fw.barrier()
        qsegs = [("l", s, n) for (s, n) in tok_blocks(LT)]
        if need_ctx:
            qsegs.append(("c", 0, CT))
        nkb_l = LT // 128
        with contextlib.ExitStack() as ph:
            NKB = 2 * nkb_l + CT // 128
            Kt = [[fw.sb(ph, [128, NKB * 128], BF16, "K3") for _ in range(2)] for _ in range(2)]
            Kb = [[[Buf() for _ in range(3)] for _ in range(2)] for _ in range(2)]
            Vt = [fw.sb(ph, [128, NKB, 256], BF16, "V3") for _ in range(2)]
            Vb = [[Buf() for _ in range(3)] for _ in range(2)]
            Qt = [fw.sb(ph, [128, 512], BF16, "Q3") for _ in range(4)]
            Qb = [Buf() for _ in range(4)]
            Gt = [fw.sb(ph, [128, 2, 512], BF16, "G3") for _ in range(2)]
            Gb = [Buf() for _ in range(2)]
            ptr = [fw.sb(ph, [128, 512], BF16, "P3") for _ in range(NPTR)]
            ptb = [Buf() for _ in range(NPTR)]
            o1 = [fw.sb(ph, [128, 2, 512], F32, "o1") for _ in range(2)]
            o1b = [Buf() for _ in range(2)]
            rc = [fw.sb(ph, [128, 512], F32, "rc") for _ in range(2)]
            rcb = [Buf() for _ in range(2)]
            tt = [fw.sb(ph, [128, 512], F32, "tt") for _ in range(2)]
            ttb = [Buf() for _ in range(2)]
            sqt = [fw.sb(ph, [128, 512], BF16, "sq3") for _ in range(2)]
            sqb = [Buf() for _ in range(2)]
            rs = fw.sb(ph, [128, 512], F32, "rs3")
            rsb = Buf()
            ogt = [fw.sb(ph, [128, 2, 512], BF16, "og3") for _ in range(2)]
            ogb = [Buf() for _ in range(2)]
            dacc = {"dve": (fw.sb(ph, [128, 512], F32, "daccD"), Buf()), "pool": (fw.sb(ph, [128, 512], F32, "daccP"), Buf())}
            def load_head(h):
                sl = h % 2
                for i in range(2):
                    r0 = (2 * h + i) * 128
                    c = 2 * h + i
                    for rk in range(2):
                        fw.dma(Kt[sl][i][:, rk * LT:(rk + 1) * LT], KTallc[c].ap()[rk * 128:(rk + 1) * 128, 0:LT], writes=[Kb[sl][i][rk]])
                    fw.dma(Kt[sl][i][:, 2 * LT:2 * LT + CT], KTc[c].ap()[:, LT:T], writes=[Kb[sl][i][2]])
                for cc in range(2):
                    c = 2 * h + cc
                    for rk in range(2):
                        fw.dma(Vt[sl][:, rk * nkb_l:(rk + 1) * nkb_l, cc * 128:(cc + 1) * 128],
                               Vallc[c].ap()[rk * T:rk * T + LT, :].rearrange("(b p) c -> p b c", p=128), writes=[Vb[sl][rk]])
                    fw.dma(Vt[sl][:, 2 * nkb_l:NKB, cc * 128:(cc + 1) * 128], Vc[c].ap()[LT:T, :].rearrange("(b p) c -> p b c", p=128), writes=[Vb[sl][2]])

            items = [(h, qs_) for h in range(8) for qs_ in qsegs]

            def load_q(qi):
                h, (kind, s0, N) = items[qi]
                tb = s0 if kind == "l" else LT
                gs = qi % 2
                fw.dma(Gt[gs][:, :, :N], GT.ap()[h * 256:(h + 1) * 256, tb:tb + N].rearrange("(c p) t -> p c t", p=128), writes=[Gb[gs]])
                for i in range(2):
                    qs = (2 * qi + i) % 4
                    r0 = (2 * h + i) * 128
                    fw.dma(Qt[qs][:, :N], QT.ap()[r0:r0 + 128, tb:tb + N], writes=[Qb[qs]])

            acc_rr = 0
            load_head(0)
            load_q(0)
            for qi, (h, (kind, s0, N)) in enumerate(items):
                sl = h % 2
                tb = s0 if kind == "l" else LT
                gs = qi % 2
                if qi % len(qsegs) == 0 and h + 1 < 8:
                    load_head(h + 1)
                if qi + 1 < len(items):
                    load_q(qi + 1)
                for i in range(2):
                        qs = (2 * qi + i) % 4
                        if kind == "l":
                            kbl = list(range(NKB))
                        else:
                            kbl = list(range(2 * nkb_l, NKB))
                        a0 = 2 + 3 * (acc_rr % 2)
                        acc_rr += 1
                        pv = [(a0, 128), (a0 + 1, 128), (a0 + 2, 128)] if not DENACC else [(a0, 128), (a0 + 1, 128)]
                        kblocks = []
                        for kb in kbl:
                            part = 0 if kb < nkb_l else (1 if kb < 2 * nkb_l else 2)
                            kblocks.append(dict(
                                s=[(Kt[sl][i][:, kb * 128:(kb + 1) * 128], Qt[qs][:, :N], [Kb[sl][i][part], Qb[qs]])],
                                v=[Vt[sl][:, kb, 0:128], Vt[sl][:, kb, 128:256]] + ([] if DENACC else [self.ones_bf[:]]),
                                vdeps=[Vb[sl][part]]))
                        if DENACC:
                            self.attn_stream(N, kblocks, scale, pv, ptr, ptb, [0, 1], den=dict(acc=dacc, out=(a0 + 2, 128)))
                        else:
                            self.attn_stream(N, kblocks, scale, pv, ptr, ptb, [0, 1])
                        ri = i
                        fw.op("dve", lambda e: e.reciprocal(rc[ri][:, :N], self.ps[a0 + 2][:, :N]), reads=[self.psb[a0 + 2]], writes=[rcb[ri]])
                        os_ = qi % 2
                        if i == 0:
                            for c in range(2):
                                fw.op("dve", lambda e: e.tensor_tensor(o1[os_][:, c, :N], self.ps[a0 + c][:, :N], rc[0][:, :N], ALU.mult),
                                      reads=[self.psb[a0 + c], rcb[0]], writes=[o1b[os_]])
                        else:
                            for c in range(2):
                                fw.op("dve", lambda e: e.tensor_tensor(tt[c][:, :N], self.ps[a0 + c][:, :N], rc[1][:, :N], ALU.mult),
                                      reads=[self.psb[a0 + c], rcb[1]], writes=[ttb[c]])
                                fw.op("dve", lambda e: e.scalar_tensor_tensor(o1[os_][:, c, :N], tt[c][:, :N], neglam[:, 0:1], o1[os_][:, c, :N], ALU.mult, ALU.add),
                                      reads=[ttb[c], o1b[os_]], writes=[o1b[os_]])
                                fw.op("act", lambda e: e.activation(out=sqt[c][:, :N], in_=o1[os_][:, c, :N], func=AF.Square),
                                      reads=[o1b[os_]], writes=[sqb[c]])
                            for c in range(2):
                                fw.op("pe", lambda e: e.matmul(self.ps[0][:, :N], self.ones_bf[:], sqt[c][:, :N], start=(c == 0), stop=(c == 1)),
                                      reads=[sqb[c]], writes=[self.psb[0]], inc=(c == 1))
                            fw.op("act", lambda e: e.activation(out=rs[:, :N], in_=self.ps[0][:, :N], func=AF.Sqrt, scale=1.0 / 256, bias=self.eps_t[:, 0:1]),
                                  reads=[self.psb[0]], writes=[rsb])
                            fw.op("dve", lambda e: e.reciprocal(rs[:, :N], rs[:, :N]), reads=[rsb], writes=[rsb])
                            for c in range(2):
                                fw.op("dve", lambda e: e.scalar_tensor_tensor(tt[c][:, :N], o1[os_][:, c, :N], subw[:, c:c + 1], rs[:, :N], ALU.mult, ALU.mult),
                                      reads=[o1b[os_], rsb], writes=[ttb[c]])
                                fw.op("pool", lambda e: e.tensor_tensor(ogt[os_][:, c, :N], tt[c][:, :N], Gt[gs][:, c, :N], ALU.mult),
                                      reads=[ttb[c], Gb[gs]], writes=[ogb[os_]])
                            fw.dma(OGT.ap()[h * 256:(h + 1) * 256, tb:tb + N].rearrange("(c p) t -> p c t", p=128), ogt[os_][:, :, :N], reads=[ogb[os_]])
            fw.barrier()
        self.out_proj(cur, dst, "l3_w_out", OGT, mv, need_ctx)


    def simple_bufs(self, ph, M):
        fw = self.fw
        sbf = {}
        sbf["Qt"] = [fw.sb(ph, [128, 512], BF16, "Qt") for _ in range(2)]
        sbf["Qb"] = [Buf(), Buf()]
        sbf["Gt"] = [fw.sb(ph, [128, 512], BF16, "Gt") for _ in range(2)]
        sbf["Gb"] = [Buf(), Buf()]
        sbf["ptr"] = [fw.sb(ph, [128, 512], BF16, "Pt") for _ in range(NPTR)]
        sbf["ptb"] = [Buf() for _ in range(NPTR)]
        sbf["rc"] = [fw.sb(ph, [128, 512], F32, "rc") for _ in range(2)]
        sbf["rcb"] = [Buf(), Buf()]
        sbf["tt"] = [fw.sb(ph, [128, 512], F32, "tt") for _ in range(2)]
        sbf["ttb"] = [Buf(), Buf()]
        sbf["og"] = [fw.sb(ph, [128, 512], BF16, "og") for _ in range(2)]
        sbf["ogb"] = [Buf(), Buf()]
        return sbf

    def simple_finalize(self, sbf, qi, M, N, po, pd, extra, Gt_ap, Gbuf, out_ap):
        fw = self.fw
        s = qi % 2
        rc, rcb, tt, ttb, og, ogb = (sbf[k] for k in ("rc", "rcb", "tt", "ttb", "og", "ogb"))
        if extra is not None:
            fw.op("dve", lambda e: e.tensor_scalar(rc[s][:M, :N], self.ps[pd][:M, :N], extra, None, ALU.add), reads=[self.psb[pd]], writes=[rcb[s]])
            fw.op("dve", lambda e: e.reciprocal(rc[s][:M, :N], rc[s][:M, :N]), reads=[rcb[s]], writes=[rcb[s]])
        else:
            fw.op("dve", lambda e: e.reciprocal(rc[s][:M, :N], self.ps[pd][:M, :N]), reads=[self.psb[pd]], writes=[rcb[s]])
        fw.op("dve", lambda e: e.tensor_tensor(tt[s][:M, :N], self.ps[po][:M, :N], rc[s][:M, :N], ALU.mult), reads=[self.psb[po], rcb[s]], writes=[ttb[s]])
        fw.op("pool", lambda e: e.tensor_tensor(og[s][:M, :N], tt[s][:M, :N], Gt_ap, ALU.mult), reads=[ttb[s], Gbuf], writes=[ogb[s]])
        fw.dma(out_ap, og[s][:M, :N], reads=[ogb[s]])

    def layer1(self, cur, dst, xoth, need_ctx):
        fw, nc = self.fw, self.nc
        LT, T = self.LT, self.T
        mv = self.mod_phase(1)
        w_in = self.inp("l1_w_in", [D, 4608])
        Ct = self.inp("rope64C", [128, T + LT])
        St = self.inp("rope64S", [128, T + LT])
        sink = self.load_vec("l1_sink_rep", 32)
        masks_in = self.inp("swa_mask", [128, 8, 512], BF16)
        QT = self.scratch("QT1", [2048, T], BF16)
        KT = self.scratch("KT1", [256, T], BF16)
        V = self.scratch("V1", [T, 256], BF16)
        GT = self.scratch("GT1", [2048, T], BF16)
        OGT = self.scratch("OGT1", [2048, T], BF16)
        KH = self.scratch("KH1", [256, 256], BF16)
        KHall = self.scratch("KH1all", [512, 256], BF16)
        VH = self.scratch("VH1", [256, 256], BF16)
        VHall = self.scratch("VH1all", [512, 256], BF16)
        pers = self._persist
        es_ = pers.enter_context(nc.sbuf_tensor("sinkexp", [128, 32], F32))
        sb_ = Buf()
        fw.op("act", lambda e: e.activation(out=es_[:], in_=sink[:], func=AF.Exp), writes=[sb_])
        fw.barrier()
        for grp in self.groups():
            with contextlib.ExitStack() as ph:
                hT, hb, ntok, segs = self.make_hT(ph, grp, cur, mv)
                with contextlib.ExitStack() as ph2:
                    jobs = []
                    for c in range(16):
                        jobs.append(dict(segs=[(c * 128, 128)], mode="rope", swap=32, scale=0.125, dst=("dram", self.dst_rows(QT.ap(), c * 128, 128, segs))))
                    for c in range(2):
                        jobs.append(dict(segs=[(2048 + c * 128, 128)], mode="rope", swap=32, dst=("dram", self.dst_rows(KT.ap(), c * 128, 128, segs))))
                    for c in range(16):
                        jobs.append(dict(segs=[(2560 + c * 128, 128)], mode="silu", dst=("dram", self.dst_rows(GT.ap(), c * 128, 128, segs))))
                    self.proj_fm(ph2, hT, hb, KC, ntok, w_in, jobs, rope=self.rope_aps(Ct, St, segs))
                    fw.barrier()
                with contextlib.ExitStack() as ph2:
                    for (kind, t0, n, off) in segs:
                        base = t0 if kind == "l" else LT
                        self.proj_tm(ph2, _View(hT, off), hb[off // 512:], KC, n, w_in, [([(2304, 256)], [V.ap()])], base)
                    fw.barrier()
        fw.dma(KH.ap()[:, 0:128], KT.ap()[:, 0:128])
        fw.dma(KH.ap()[:, 128:256], KT.ap()[:, LT - 128:LT])
        fw.dma(VH.ap()[0:128, :], V.ap()[0:128, :])
        fw.dma(VH.ap()[128:256, :], V.ap()[LT - 128:LT, :])
        fw.barrier()
        fw.collective(KH.ap().opt(), KHall.ap().opt())
        fw.collective(VH.ap().opt(), VHall.ap().opt())
        fw.barrier()
        nbl = LT // 128
        NB = nbl + 2 + 2
        nR = LT // 512
        with contextlib.ExitStack() as ph:
            msk = fw.sb(ph, [128, 8, 512], BF16, "swamask")
            mb = Buf()
            fw.dma(msk[:], masks_in[:, :, :], writes=[mb])
            Kt = [fw.sb(ph, [128, NB * 128], BF16, "K1") for _ in range(2)]
            Kb = [Buf(), Buf()]
            for s_ in range(2):
                fw.op("pool", lambda e: e.memset(Kt[s_][:], 0.0), writes=[Kb[s_]])
            Vt = [fw.sb(ph, [128, NB, 64], BF16, "V1") for _ in range(2)]
            Vb = [Buf(), Buf()]
            sbf = self.simple_bufs(ph, 64)
            Qt, Qb, Gt, Gb, ptr, ptb = (sbf[k] for k in ("Qt", "Qb", "Gt", "Gb", "ptr", "ptb"))
            for s_ in range(2):
                fw.op("pool", lambda e: e.memset(Qt[s_][:], 0.0), writes=[Qb[s_]])

            def load_kv(g):
                s = g % 2
                r0 = 64 * g
                fw.dma(Kt[s][0:64, 0:128], KHall.ap()[r0:r0 + 64, 128:256], writes=[Kb[s]])
                fw.dma(Kt[s][0:64, 128:128 + LT], KT.ap()[r0:r0 + 64, 0:LT], writes=[Kb[s]])
                fw.dma(Kt[s][0:64, 128 + LT:256 + LT], KHall.ap()[256 + r0:256 + r0 + 64, 0:128], writes=[Kb[s]])
                fw.dma(Kt[s][0:64, 256 + LT:256 + LT + CT], KT.ap()[r0:r0 + 64, LT:T], writes=[Kb[s]])
                fw.dma(Vt[s][:, 0, :], VHall.ap()[128:256, r0:r0 + 64], writes=[Vb[s]])
                fw.dma(Vt[s][:, 1:1 + nbl, :], V.ap()[0:LT, r0:r0 + 64].rearrange("(b p) c -> p b c", p=128), writes=[Vb[s]])
                fw.dma(Vt[s][:, 1 + nbl, :], VHall.ap()[256:384, r0:r0 + 64], writes=[Vb[s]])
                fw.dma(Vt[s][:, 2 + nbl:NB, :], V.ap()[LT:T, r0:r0 + 64].rearrange("(b p) c -> p b c", p=128), writes=[Vb[s]])

            qsegs = [("l", R) for R in range(nR)] + ([("c", 0)] if need_ctx else [])
            items = [(h, qs_) for h in range(32) for qs_ in qsegs]

            def load_q(qi):
                h, (kind, R) = items[qi]
                tb, N = (R * 512, 512) if kind == "l" else (LT, CT)
                s = qi % 2
                fw.dma(Qt[s][:64, :N], QT.ap()[64 * h:64 * h + 64, tb:tb + N], writes=[Qb[s]])
                fw.dma(Gt[s][:64, :N], GT.ap()[64 * h:64 * h + 64, tb:tb + N], writes=[Gb[s]])

            load_kv(0)
            load_q(0)
            for qi, (h, (kind, R)) in enumerate(items):
                g = h // 8
                ks = g % 2
                s = qi % 2
                tb, N = (R * 512, 512) if kind == "l" else (LT, CT)
                if qi % (8 * len(qsegs)) == 0 and g + 1 < 4:
                    load_kv(g + 1)
                if qi + 1 < len(items):
                    load_q(qi + 1)
                kblocks = []
                if kind == "l":
                    for jo in range(6):
                        blk = 4 * R + jo
                        mi = jo
                        if R == 0 and jo == 0:
                            mi = 6
                        if R == nR - 1 and jo == 5:
                            mi = 7
                        kblocks.append(dict(s=[(Kt[ks][:, blk * 128:(blk + 1) * 128], Qt[s][:, :N], [Kb[ks], Qb[s]])],
                                            mask=(msk[:, mi, :N], [mb]),
                                            v=[Vt[ks][:, blk, :], self.ones_bf[:, 0:64]], vdeps=[Vb[ks]]))
                for cb in range(2):
                    blk = 2 + nbl + cb
                    kblocks.append(dict(s=[(Kt[ks][:, blk * 128:(blk + 1) * 128], Qt[s][:, :N], [Kb[ks], Qb[s]])],
                                        v=[Vt[ks][:, blk, :], self.ones_bf[:, 0:64]], vdeps=[Vb[ks]]))
                a0 = 2 + 2 * (qi % 3)
                self.attn_stream(N, kblocks, 1.0, [(a0, 64), (a0 + 1, 64)], ptr, ptb, [0, 1])
                self.simple_finalize(sbf, qi, 64, N, a0, a0 + 1, es_[:64, h:h + 1], Gt[s][:64, :N], Gb[s], OGT.ap()[64 * h:64 * h + 64, tb:tb + N])
            fw.barrier()
        self.out_proj(cur, dst, "l1_w_out", OGT, mv, need_ctx)

    def layer2(self, cur, dst, xoth, need_ctx):
        fw, nc = self.fw, self.nc
        LT, T = self.LT, self.T
        mv = self.mod_phase(2)
        w_in = self.inp("l2_w_in", [D, 8192])
        bias_in = self.inp("na_bias", [32, 128, 24, 512], BF16)
        QT = self.scratch("QT2", [2048, T], BF16)
        KT = self.scratch("KT2", [2048, T], BF16)
        V = self.scratch("V2", [T, 2048], BF16)
        GT = self.scratch("GT2", [2048, T], BF16)
        OGT = self.scratch("OGT2", [2048, T], BF16)
        KH = [self.scratch(f"KH2_{i}", [1024, 512], BF16) for i in range(2)]
        KHall = [self.scratch(f"KH2all_{i}", [2048, 512], BF16) for i in range(2)]
        VH = [self.scratch(f"VH2_{i}", [256, 2048], BF16) for i in range(2)]
        VHall = [self.scratch(f"VH2all_{i}", [512, 2048], BF16) for i in range(2)]
        for grp in self.groups():
            with contextlib.ExitStack() as ph:
                hT, hb, ntok, segs = self.make_hT(ph, grp, cur, mv)
                with contextlib.ExitStack() as ph2:
                    jobs = []
                    for c in range(16):
                        jobs.append(dict(segs=[(c * 128, 128)], mode="plain", scale=0.125, dst=("dram", self.dst_rows(QT.ap(), c * 128, 128, segs))))
                    for c in range(16):
                        jobs.append(dict(segs=[(2048 + c * 128, 128)], mode="plain", dst=("dram", self.dst_rows(KT.ap(), c * 128, 128, segs))))
                    for c in range(16):
                        jobs.append(dict(segs=[(6144 + c * 128, 128)], mode="silu", dst=("dram", self.dst_rows(GT.ap(), c * 128, 128, segs))))
                    self.proj_fm(ph2, hT, hb, KC, ntok, w_in, jobs)
                    fw.barrier()
                with contextlib.ExitStack() as ph2:
                    for (kind, t0, n, off) in segs:
                        base = t0 if kind == "l" else LT
                        sub = [([(4096 + j * 256, 256)], [V.ap()[:, j * 256:(j + 1) * 256]]) for j in range(8)]
                        self.proj_tm(ph2, _View(hT, off), hb[off // 512:], KC, n, w_in, sub, base)
                    fw.barrier()
        for i in range(2):
            fw.dma(KH[i].ap()[:, 0:256], KT.ap()[i * 1024:(i + 1) * 1024, 0:256])
            fw.dma(KH[i].ap()[:, 256:512], KT.ap()[i * 1024:(i + 1) * 1024, LT - 256:LT])
        fw.dma(VH[0].ap()[:, :], V.ap()[0:256, :])
        fw.dma(VH[1].ap()[:, :], V.ap()[LT - 256:LT, :])
        fw.barrier()
        for i in range(2):
            fw.collective(KH[i].ap().opt(), KHall[i].ap().opt())
            fw.collective(VH[i].ap().opt(), VHall[i].ap().opt())
        fw.barrier()
        nbl = LT // 128
        NB = nbl + 4 + 2
        nR = LT // 512
        with contextlib.ExitStack() as ph:
            Kt = [fw.sb(ph, [128, NB * 128], BF16, "K2") for _ in range(2)]
            Kb = [Buf(), Buf()]
            for s_ in range(2):
                fw.op("pool", lambda e: e.memset(Kt[s_][:], 0.0), writes=[Kb[s_]])
            Vt = [fw.sb(ph, [128, NB, 64], BF16, "V2") for _ in range(2)]
            Vb = [Buf(), Buf()]
            Bt = [fw.sb(ph, [128, 24, 512], BF16, "B2") for _ in range(2)]
            Bb = [Buf(), Buf()]
            sbf = self.simple_bufs(ph, 64)
            Qt, Qb, Gt, Gb, ptr, ptb = (sbf[k] for k in ("Qt", "Qb", "Gt", "Gb", "ptr", "ptb"))
            for s_ in range(2):
                fw.op("pool", lambda e: e.memset(Qt[s_][:], 0.0), writes=[Qb[s_]])

            def load_kv(h):
                s = h % 2
                r0 = 64 * h
                ci, wi = r0 // 1024, r0 % 1024
                fw.dma(Kt[s][0:64, 0:256], KHall[ci].ap()[wi:wi + 64, 256:512], writes=[Kb[s]])
                fw.dma(Kt[s][0:64, 256:256 + LT], KT.ap()[r0:r0 + 64, 0:LT], writes=[Kb[s]])
                fw.dma(Kt[s][0:64, 256 + LT:512 + LT], KHall[ci].ap()[1024 + wi:1024 + wi + 64, 0:256], writes=[Kb[s]])
                fw.dma(Kt[s][0:64, 512 + LT:512 + LT + CT], KT.ap()[r0:r0 + 64, LT:T], writes=[Kb[s]])
                fw.dma(Vt[s][:, 0:2, :], VHall[1].ap()[0:256, r0:r0 + 64].rearrange("(b p) c -> p b c", p=128), writes=[Vb[s]])
                fw.dma(Vt[s][:, 2:2 + nbl, :], V.ap()[0:LT, r0:r0 + 64].rearrange("(b p) c -> p b c", p=128), writes=[Vb[s]])
                fw.dma(Vt[s][:, 2 + nbl:4 + nbl, :], VHall[0].ap()[256:512, r0:r0 + 64].rearrange("(b p) c -> p b c", p=128), writes=[Vb[s]])
                fw.dma(Vt[s][:, 4 + nbl:NB, :], V.ap()[LT:T, r0:r0 + 64].rearrange("(b p) c -> p b c", p=128), writes=[Vb[s]])
                fw.dma(Bt[s][:], bias_in[h, :, :, :], writes=[Bb[s]])

            qsegs = [("l", R) for R in range(nR)] + ([("c", 0)] if need_ctx else [])
            items = [(h, qs_) for h in range(32) for qs_ in qsegs]

            def load_q(qi):
                h, (kind, R) = items[qi]
                tb, N = (R * 512, 512) if kind == "l" else (LT, CT)
                s = qi % 2
                fw.dma(Qt[s][:64, :N], QT.ap()[64 * h:64 * h + 64, tb:tb + N], writes=[Qb[s]])
                fw.dma(Gt[s][:64, :N], GT.ap()[64 * h:64 * h + 64, tb:tb + N], writes=[Gb[s]])

            load_kv(0)
            load_q(0)
            for qi, (h, (kind, R)) in enumerate(items):
                ks = h % 2
                s = qi % 2
                tb, N = (R * 512, 512) if kind == "l" else (LT, CT)
                if qi % len(qsegs) == 0 and h + 1 < 32:
                    load_kv(h + 1)
                if qi + 1 < len(items):
                    load_q(qi + 1)
                kblocks = []
                if kind == "l":
                    var = 0 if R == 0 else (2 if R == nR - 1 else 1)
                    for jo in range(8):
                        blk = 4 * R + jo
                        kblocks.append(dict(s=[(Kt[ks][:, blk * 128:(blk + 1) * 128], Qt[s][:, :N], [Kb[ks], Qb[s]])],
                                            mask=(Bt[ks][:, var * 8 + jo, :N], [Bb[ks]]),
                                            v=[Vt[ks][:, blk, :], self.ones_bf[:, 0:64]], vdeps=[Vb[ks]]))
                for cb in range(2):
                    blk = 4 + nbl + cb
                    kblocks.append(dict(s=[(Kt[ks][:, blk * 128:(blk + 1) * 128], Qt[s][:, :N], [Kb[ks], Qb[s]])],
                                        v=[Vt[ks][:, blk, :], self.ones_bf[:, 0:64]], vdeps=[Vb[ks]]))
                a0 = 2 + 2 * (qi % 3)
                self.attn_stream(N, kblocks, 1.0, [(a0, 64), (a0 + 1, 64)], ptr, ptb, [0, 1])
                self.simple_finalize(sbf, qi, 64, N, a0, a0 + 1, None, Gt[s][:64, :N], Gb[s], OGT.ap()[64 * h:64 * h + 64, tb:tb + N])
            fw.barrier()
        self.out_proj(cur, dst, "l2_w_out", OGT, mv, need_ctx)

    def layer0(self, cur, dst, xoth, need_ctx):
        fw, nc = self.fw, self.nc
        LT, T = self.LT, self.T
        TK = T + LT
        mv = self.mod_phase(0)
        w_in = self.inp("l0_w_in", [D, 3136])
        w_qb = self.inp("l0_w_qb", [512, 3072])
        w_kvb = self.inp("l0_w_kvb", [512, 4096])
        qng = self.load_vec("l0_q_norm", 4)
        kvng = self.load_vec("l0_kv_norm", 4)
        Ct = self.inp("rope64C", [128, T + LT])
        St = self.inp("rope64S", [128, T + LT])
        QN = self.scratch("QN0", [2048, T], BF16)
        QR = self.scratch("QR0", [1024, T], BF16)
        KN = self.scratch("KN0", [2048, TK], BF16)
        KR = self.scratch("KR0", [64, TK], BF16)
        V = self.scratch("V0", [TK, 2048], BF16)
        GT = self.scratch("GT0", [2048, T], BF16)
        OGT = self.scratch("OGT0", [2048, T], BF16)
        scale = 192 ** -0.5
        own_groups = self.groups()
        oth_groups = [[("o", s, n)] for (s, n) in tok_blocks(LT, 2048)]
        for grp in own_groups + oth_groups:
            own = grp[0][0] != "o"
            with contextlib.ExitStack() as ph:
                ntok = sum(n for (_, _, n) in grp)
                nblk = (ntok + 511) // 512
                qcT = fw.sb(ph, [128, 4, ntok], BF16, "qcT") if own else None
                kvcT = fw.sb(ph, [128, 4, ntok], BF16, "kvcT")
                qcb = [Buf() for _ in range(nblk)]
                kvb_ = [Buf() for _ in range(nblk)]
                with contextlib.ExitStack() as phh:
                    hT, hb, ntok, segs = self.make_hT(phh, grp, cur, mv, xoth=xoth)

                    def tkdst(scr, r0, M):
                        out = []
                        for (kind, t0, n, off) in segs:
                            base = {"l": t0, "c": LT, "o": T + t0}[kind]
                            out.append((scr[r0:r0 + M, base:base + n], off, off + n))
                        return out

                    with contextlib.ExitStack() as ph2:
                        jobs = []
                        if own:
                            for c in range(4):
                                jobs.append(dict(segs=[(c * 128, 128)], mode="plain", dst=("sb", qcT, c, qcb)))
                        for c in range(4):
                            jobs.append(dict(segs=[(512 + c * 128, 128)], mode="plain", dst=("sb", kvcT, c, kvb_)))
                        jobs.append(dict(segs=[(1024, 64)], mode="rope", swap=32, dst=("dram", tkdst(KR.ap(), 0, 64))))
                        if own:
                            for c in range(16):
                                jobs.append(dict(segs=[(1088 + c * 128, 128)], mode="silu", dst=("dram", self.dst_rows(GT.ap(), c * 128, 128, segs))))
                        self.proj_fm(ph2, hT, hb, KC, ntok, w_in, jobs, rope=self.rope_aps(Ct, St, segs))
                        fw.barrier()
                with contextlib.ExitStack() as ph2:
                    sq = [fw.sb(ph2, [128, 512], BF16, "lsq") for _ in range(2)]
                    sqb = [Buf(), Buf()]
                    rstd = fw.sb(ph2, [128, 512], F32, "lrstd")
                    rb = Buf()
                    tmp = [fw.sb(ph2, [128, 512], F32, "ltmp") for _ in range(2)]
                    tmb = [Buf(), Buf()]
                    todo = ([(qcT, qcb, qng)] if own else []) + [(kvcT, kvb_, kvng)]
                    cnt = 0
                    for (tT, tb_, gv) in todo:
                        for (t0, m) in tok_blocks(ntok):
                            bb = tb_[t0 // 512]
                            pi = 6 + (cnt % 2)
                            cnt += 1
                            for k in range(4):
                                q = k % 2
                                fw.op("act", lambda e: e.activation(out=sq[q][:, :m], in_=tT[:, k, t0:t0 + m], func=AF.Square), reads=[bb], writes=[sqb[q]])
                                fw.op("pe", lambda e: e.matmul(self.ps[pi][:, :m], self.ones_bf[:], sq[q][:, :m], start=(k == 0), stop=(k == 3)),
                                      reads=[sqb[q]], writes=[self.psb[pi]])
                            fw.op("act", lambda e: e.activation(out=rstd[:, :m], in_=self.ps[pi][:, :m], func=AF.Sqrt, scale=1.0 / 512, bias=self.eps_t[:, 0:1]),
                                  reads=[self.psb[pi]], writes=[rb])
                            fw.op("dve", lambda e: e.reciprocal(rstd[:, :m], rstd[:, :m]), reads=[rb], writes=[rb])
                            for k in range(4):
                                q = k % 2
                                fw.op("dve", lambda e: e.tensor_tensor(tmp[q][:, :m], tT[:, k, t0:t0 + m], rstd[:, :m], ALU.mult), reads=[bb, rb], writes=[tmb[q]])
                                fw.op("act", lambda e: e.activation(out=tT[:, k, t0:t0 + m], in_=tmp[q][:, :m], func=AF.Copy, scale=gv[:, k:k + 1]),
                                      reads=[tmb[q]], writes=[bb])
                    fw.barrier()
                with contextlib.ExitStack() as ph2:
                    if own:
                        jobs = []
                        for h in range(16):
                            jobs.append(dict(segs=[(192 * h, 128)], mode="plain", dst=("dram", self.dst_rows(QN.ap(), 128 * h, 128, segs))))
                        for j in range(8):
                            jobs.append(dict(segs=[(192 * (2 * j) + 128, 64), (192 * (2 * j + 1) + 128, 64)], mode="rope", swap=32,
                                             dst=("dram", self.dst_rows(QR.ap(), 128 * j, 128, segs))))
                        self.proj_fm(ph2, qcT, qcb, 4, ntok, w_qb, jobs, rope=self.rope_aps(Ct, St, segs))
                        fw.barrier()
                with contextlib.ExitStack() as ph2:
                    jobs = []
                    for h in range(16):
                        jobs.append(dict(segs=[(256 * h, 128)], mode="plain", dst=("dram", tkdst(KN.ap(), 128 * h, 128))))
                    self.proj_fm(ph2, kvcT, kvb_, 4, ntok, w_kvb, jobs)
                    fw.barrier()
                with contextlib.ExitStack() as ph2:
                    for (kind, t0, n, off) in segs:
                        base = {"l": t0, "c": LT, "o": T + t0}[kind]
                        sub = [([(256 * (2 * j) + 128, 128), (256 * (2 * j + 1) + 128, 128)], [V.ap()[:, j * 256:(j + 1) * 256]]) for j in range(8)]
                        self.proj_tm(ph2, _View(kvcT, off), kvb_[off // 512:], 4, n, w_kvb, sub, base)
                    fw.barrier()
        NKB = TK // 128
        cb0 = LT // 128
        with contextlib.ExitStack() as ph:
            Krt = fw.sb(ph, [128, TK], BF16, "KR")
            Krb = Buf()
            fw.op("pool", lambda e: e.memset(Krt[:], 0.0), writes=[Krb])
            fw.dma(Krt[0:64, :], KR.ap()[:, :], writes=[Krb])
            Kt = [fw.sb(ph, [128, TK], BF16, "K0") for _ in range(2)]
            Kb = [Buf(), Buf()]
            Vt = [fw.sb(ph, [128, NKB, 128], BF16, "V0") for _ in range(2)]
            Vb = [Buf(), Buf()]
            dacc = {"dve": (fw.sb(ph, [128, 512], F32, "daccD"), Buf()), "pool": (fw.sb(ph, [128, 512], F32, "daccP"), Buf())}
            Qr = [fw.sb(ph, [128, 512], BF16, "Qr") for _ in range(2)]
            Qrb = [Buf(), Buf()]
            for s_ in range(2):
                fw.op("pool", lambda e: e.memset(Qr[s_][:], 0.0), writes=[Qrb[s_]])
            sbf = self.simple_bufs(ph, 128)
            Qt, Qb, Gt, Gb, ptr, ptb = (sbf[k] for k in ("Qt", "Qb", "Gt", "Gb", "ptr", "ptb"))

            def load_kv(h):
                s = h % 2
                half = TK // 2
                for a in range(2):
                    fw.dma(Kt[s][:, a * half:(a + 1) * half], KN.ap()[128 * h:128 * h + 128, a * half:(a + 1) * half], writes=[Kb[s]])
                    fw.dma(Vt[s][:, a * (NKB // 2):(a + 1) * (NKB // 2), :],
                           V.ap()[a * half:(a + 1) * half, 128 * h:128 * h + 128].rearrange("(b p) c -> p b c", p=128), writes=[Vb[s]])

            qsegs = [("l", s_, n_) for (s_, n_) in tok_blocks(LT)] + ([("c", LT, CT)] if need_ctx else [])
            items = [(h, qs_) for h in range(16) for qs_ in qsegs]

            def load_q(qi):
                h, (kind, tb, N) = items[qi]
                s = qi % 2
                fw.dma(Qt[s][:, :N], QN.ap()[128 * h:128 * h + 128, tb:tb + N], writes=[Qb[s]])
                fw.dma(Qr[s][0:64, :N], QR.ap()[64 * h:64 * h + 64, tb:tb + N], writes=[Qrb[s]])
                fw.dma(Gt[s][:, :N], GT.ap()[128 * h:128 * h + 128, tb:tb + N], writes=[Gb[s]])

            load_kv(0)
            load_q(0)
            for qi, (h, (kind, tb, N)) in enumerate(items):
                ks = h % 2
                s = qi % 2
                if qi % len(qsegs) == 0 and h + 1 < 16:
                    load_kv(h + 1)
                if qi + 1 < len(items):
                    load_q(qi + 1)
                kbl = list(range(NKB)) if kind == "l" else [cb0, cb0 + 1]
                kblocks = []
                for blk in kbl:
                    kblocks.append(dict(s=[(Kt[ks][:, blk * 128:(blk + 1) * 128], Qt[s][:, :N], [Kb[ks], Qb[s]]),
                                           (Krt[:, blk * 128:(blk + 1) * 128], Qr[s][:, :N], [Krb, Qrb[s]])],
                                        v=[Vt[ks][:, blk, :], self.ones_bf[:]] if not DENACC else [Vt[ks][:, blk, :]], vdeps=[Vb[ks]]))
                a0 = 2 + 2 * (qi % 3)
                if DENACC:
                    self.attn_stream(N, kblocks, scale, [(a0, 128)], ptr, ptb, [0, 1], den=dict(acc=dacc, out=(a0 + 1, 128)))
                else:
                    self.attn_stream(N, kblocks, scale, [(a0, 128), (a0 + 1, 128)], ptr, ptb, [0, 1])
                self.simple_finalize(sbf, qi, 128, N, a0, a0 + 1, None, Gt[s][:, :N], Gb[s], OGT.ap()[128 * h:128 * h + 128, tb:tb + N])
            fw.barrier()
        self.out_proj(cur, dst, "l0_w_out", OGT, mv, need_ctx)


class _View:
    def __init__(self, t, off):
        self.t = t
        self.off = off

    def __getitem__(self, idx):
        p, k, sl = idx
        return self.t[p, k, self.off + sl.start:self.off + sl.stop]


def _vl(v):
    v = np.asarray(v, np.float32)
    n = v.shape[0] // 128
    return np.ascontiguousarray(v.reshape(n, 128).T)


def _rope_tables(pos, d_rot, n_ctx, pos_oth=None):
    d_axis = d_rot // 2
    inv = (10000.0 ** (-np.arange(0, d_axis, 2, dtype=np.float32) / d_axis)).astype(np.float32)

    def tab(p):
        row = (p // GRID_W).astype(np.float32)
        col = (p % GRID_W).astype(np.float32)
        ang = np.concatenate([row[:, None] * inv, col[:, None] * inv], axis=-1).astype(np.float32)
        c = np.cos(ang).T
        s = np.sin(ang).T
        return np.concatenate([c, c], 0), np.concatenate([-s, s], 0)

    parts_c, parts_s = [], []
    c, s = tab(pos)
    parts_c.append(c)
    parts_s.append(s)
    parts_c.append(np.ones((d_rot, n_ctx), np.float32))
    parts_s.append(np.zeros((d_rot, n_ctx), np.float32))
    if pos_oth is not None:
        c, s = tab(pos_oth)
        parts_c.append(c)
        parts_s.append(s)
    C = np.concatenate(parts_c, 1)
    S = np.concatenate(parts_s, 1)
    rep = 128 // d_rot
    return (np.ascontiguousarray(np.tile(C, (rep, 1)), np.float32), np.ascontiguousarray(np.tile(S, (rep, 1)), np.float32))


def _prep(inputs, prog, b, r):
    LT = prog.LT
    m = {}
    xT = np.asarray(inputs["x"][b], np.float32).T
    pos_own = np.arange(r * LT, (r + 1) * LT)
    pos_oth = np.arange((1 - r) * LT, (2 - r) * LT)
    for name, (shape, dt) in prog.inputs.items():
        if name == "ident":
            v = np.eye(128, dtype=np.float32)
        elif name == "xT_own":
            v = xT[:, r * LT:(r + 1) * LT]
        elif name == "xT_oth":
            v = xT[:, (1 - r) * LT:(2 - r) * LT]
        elif name == "ctxT":
            v = np.asarray(inputs["ctx"][b], np.float32).T
        elif name == "cvec":
            v = np.stack([_vl(inputs["c"][b]), _vl(inputs["c_ctx"])], axis=-1)
        elif name == "l3_lam":
            v = np.stack([np.asarray(inputs[k], np.float32) for k in ("l3_lam_q1", "l3_lam_k1", "l3_lam_q2", "l3_lam_k2")], axis=1)
        elif name == "rope128C":
            v = _rope_tables(pos_own, 128, CT)[0]
        elif name == "rope128S":
            v = _rope_tables(pos_own, 128, CT)[1]
        elif name == "rope64C":
            v = _rope_tables(pos_own, 64, CT, pos_oth)[0]
        elif name == "rope64S":
            v = _rope_tables(pos_own, 64, CT, pos_oth)[1]
        elif name in inputs and tuple(np.shape(inputs[name])) == shape:
            v = inputs[name]
        elif name in inputs and np.ndim(inputs[name]) == 1:
            v = _vl(inputs[name])
        else:
            v = _special(inputs, prog, name, b, r)
        if dt == BF16:
            v = np.ascontiguousarray(np.asarray(v).astype(ml_dtypes.bfloat16))
        else:
            v = np.ascontiguousarray(np.asarray(v, np.float32))
        assert v.shape == shape, (name, v.shape, shape)
        m[name] = v
    return m


def _special(inputs, prog, name, b, r):
    LT = prog.LT
    if name == "l1_sink_rep":
        return np.tile(np.asarray(inputs["l1_sink"], np.float32)[None, :], (128, 1))
    if name == "swa_mask":
        kk = np.arange(128)[:, None]
        qq = np.arange(512)[None, :]
        m = np.full((128, 8, 512), NEG, np.float32)
        for jo in range(6):
            ok = np.abs(qq - (128 * jo - 128 + kk)) <= 128
            m[:, jo, :] = np.where(ok, 0.0, NEG)
        if r == 1:
            m[:, 6, :] = m[:, 0, :]
        if r == 0:
            m[:, 7, :] = m[:, 5, :]
        return m
    if name == "na_bias":
        rpb = np.asarray(inputs["l2_rpb"], np.float32)
        rows_half = LT // GRID_W
        rows = 2 * rows_half
        nR = LT // 512
        kk = np.arange(128)[:, None]
        qq = np.arange(512)[None, :]
        out = np.empty((32, 128, 24, 512), ml_dtypes.bfloat16)
        for v, R in enumerate((0, min(1, nR - 1), nR - 1)):
            for jo in range(8):
                qr = r * rows_half + 8 * R + qq // GRID_W
                qc = qq % GRID_W
                kr = r * rows_half + 8 * R - 4 + 2 * jo + kk // GRID_W
                kc = kk % GRID_W
                rs = np.clip(qr - 4, 0, rows - 8)
                cs = np.clip(qc - 8, 0, GRID_W - 16)
                ok = (kr >= 0) & (kr < rows) & (kr >= rs) & (kr < rs + 8) & (kc >= cs) & (kc < cs + 16)
                dr = np.clip(kr - qr + 7, 0, 14)
                dc = np.clip(kc - qc, -15, 15) + 15
                dr, dc, ok = np.broadcast_arrays(dr, dc, ok)
                t = rpb[:, dr, dc]
                t = np.where(ok[None], t, NEG)
                out[:, :, v * 8 + jo, :] = t.astype(ml_dtypes.bfloat16)
        return out
    raise KeyError(name)


_CACHE = {}


def run(inputs, layers=(0, 1, 2, 3), final=True, need_ctx_last=False, dbg=()):
    x = np.asarray(inputs["x"])
    B, SEQ = x.shape[0], x.shape[1]
    LT = SEQ // 2
    key = (LT, tuple(layers), final, need_ctx_last, B, tuple(dbg))
    if key not in _CACHE:
        prog = Prog(LT, list(layers), final, need_ctx_last, B)
        prog.dbg_names = list(dbg)
        prog.build()
        _CACHE[key] = prog
    prog = _CACHE[key]
    in_maps = []
    for core in range(2 * B):
        in_maps.append(_prep(inputs, prog, core // 2, core % 2))
    res = run_bass_kernel_spmd(prog.nc, in_maps, core_ids=list(range(2 * B)))
    if dbg:
        global DBG
        DBG = [{n: np.asarray(res.results[core][n]) for n in prog.dbg_out} for core in range(2 * B)]
    out = np.empty((B, SEQ, D), np.float32)
    for core in range(2 * B):
        b, r = core // 2, core % 2
        out[b, r * LT:(r + 1) * LT, :] = res.results[core]["outT"].T
    return out


def kernel(**inputs):
    return run(inputs)
```
